# Optimizing a Trainium2 kernel written in Bass

```python
import jax
import jax.numpy as jnp
from jax import lax
import numpy as np

D_MODEL = 1024
BATCH = 8
SEQ = 4096
DEPTH = 4
DEC_BATCH = 32
DEC_SEQ = 32
PAST_LEN = 4096

CHUNK = 64
N_META = 16
D_MIX = D_MODEL
HD_ATT = 64
D_ATT = D_MIX // 2
H_ATT = D_ATT // HD_ATT
IDX_HEADS = 8
IDX_DIM = 64
TOPK_MAX = 256
HD_ML = 128
D_ML = D_MIX - D_ATT
H_ML = D_ML // HD_ML
CONV_W = 4
ROPE_THETA = 10000.0
Q_BLOCK = 16
EPS = 1e-6

SPLITS = (D_ATT, D_ATT, D_ATT, D_ATT, IDX_HEADS * IDX_DIM, IDX_DIM, IDX_HEADS,
          D_ML, D_ML, D_ML, D_ML, H_ML, H_ML)
D_IN = sum(SPLITS)

kernel_name = "hybrid_dsa_mlstm_stream_step"


def rmsnorm(x, g):
    x32 = x.astype(jnp.float32)
    y = x32 * lax.rsqrt(jnp.mean(x32 * x32, axis=-1, keepdims=True) + EPS)
    return (y * g.astype(jnp.float32)).astype(x.dtype)


def rope(x, pos):
    half = x.shape[-1] // 2
    freqs = ROPE_THETA ** (-jnp.arange(half, dtype=jnp.float32) / half)
    ang = pos[:, None] * freqs[None, :]
    cos = jnp.cos(ang)[:, None, :]
    sin = jnp.sin(ang)[:, None, :]
    x32 = x.astype(jnp.float32)
    x1, x2 = x32[..., :half], x32[..., half:]
    return jnp.concatenate([x1 * cos - x2 * sin, x2 * cos + x1 * sin], axis=-1).astype(x.dtype)


def chunk_id(p):
    return jnp.where(p < N_META, -1, (p - N_META) // CHUNK)


def split_cols(a):
    out = []
    off = 0
    for w in SPLITS:
        out.append(a[..., off:off + w])
        off += w
    return out


def mixer_inputs(x, pos, lp):
    b, t, _ = x.shape
    h = rmsnorm(x, lp["norm_g"])
    proj = jnp.einsum("btd,de->bte", h, lp["w_in"])
    q, k, v, z_a, qi, ki, wi, u, v_m, o_m, z_m, ig, fg = split_cols(proj)
    q = rope(rmsnorm(q.reshape(b, t, H_ATT, HD_ATT), lp["q_norm_g"]), pos)
    k = rope(rmsnorm(k.reshape(b, t, H_ATT, HD_ATT), lp["k_norm_g"]), pos)
    v = v.reshape(b, t, H_ATT, HD_ATT)
    qi = rope(qi.reshape(b, t, IDX_HEADS, IDX_DIM), pos)
    ki = rope(ki[:, :, None, :], pos)[:, :, 0, :]
    ig = (ig + lp["b_igate"]).astype(jnp.float32)
    lf = jax.nn.log_sigmoid((fg + lp["b_fgate"]).astype(jnp.float32))
    return (q, k, v, z_a, qi, ki, wi), (u, v_m, o_m, z_m, ig, lf)


def index_select_attend(q, qi, wi, k, v, ki, allowed, k_sel):
    f32 = jnp.float32
    rel = jax.nn.relu(jnp.einsum("bthi,bsi->btsh", qi.astype(f32), ki.astype(f32)) * IDX_DIM ** -0.5)
    score = jnp.einsum("btsh,bth->bts", rel, wi.astype(f32)) * IDX_HEADS ** -0.5
    if allowed is not None:
        score = jnp.where(allowed[None], score, -jnp.inf)
    top_val, top_idx = lax.top_k(score, k_sel)
    bidx = jnp.arange(k.shape[0])[:, None, None]
    kg = k[bidx, top_idx].astype(f32)
    vg = v[bidx, top_idx].astype(f32)
    logits = jnp.einsum("bthd,btjhd->bthj", q.astype(f32), kg) * HD_ATT ** -0.5
    logits = jnp.where(jnp.isfinite(top_val)[:, :, None, :], logits, -jnp.inf)
    p = jax.nn.softmax(logits, axis=-1)
    return jnp.einsum("bthj,btjhd->bthd", p, vg).astype(q.dtype)


def prompt_attention(q, qi, wi, k, v, ki, k_sel):
    b, t = q.shape[:2]
    nb = t // Q_BLOCK
    key_cid = chunk_id(jnp.arange(t))

    def to_blocks(a):
        return jnp.moveaxis(a.reshape((b, nb, Q_BLOCK) + a.shape[2:]), 1, 0)

    def one_block(args):
        qb, qib, wib, start = args
        q_cid = chunk_id(start + jnp.arange(Q_BLOCK))
        allowed = key_cid[None, :] <= q_cid[:, None]
        return index_select_attend(qb, qib, wib, k, v, ki, allowed, k_sel)

    starts = jnp.arange(nb) * Q_BLOCK
    out = lax.map(one_block, (to_blocks(q), to_blocks(qi), to_blocks(wi), starts))
    return jnp.moveaxis(out, 0, 1).reshape(q.shape)


def causal_conv(u, buf, w, bias):
    t = u.shape[1]
    xp = jnp.concatenate([buf.astype(u.dtype), u], axis=1)
    y = bias
    for j in range(CONV_W):
        y = y + xp[:, j:j + t] * w[j]
    return y, xp[:, t:]


def mlstm_qkv(c, v_m, lp):
    b, t, _ = c.shape
    ch = c.reshape(b, t, H_ML, HD_ML)
    q = jnp.einsum("bthd,hde->bthe", ch, lp["wq_m"]) * HD_ML ** -0.5
    k = jnp.einsum("bthd,hde->bthe", ch, lp["wk_m"])
    v = v_m.reshape(b, t, H_ML, HD_ML)
    return q.astype(jnp.float32), k.astype(jnp.float32), v.astype(jnp.float32)


def mlstm_block(carry, blk):
    C, n, m = carry
    q, k, v, ig, lf = blk
    L = q.shape[1]
    bcum = jnp.cumsum(lf, axis=1)
    causal = jnp.tril(jnp.ones((L, L), dtype=bool))
    dmat = bcum[:, :, None, :] - bcum[:, None, :, :] + ig[:, None, :, :]
    dmat = jnp.where(causal[None, :, :, None], dmat, -jnp.inf)
    g = bcum + m[:, None, :]
    m_t = jnp.maximum(g, jnp.max(dmat, axis=2))
    pmat = jnp.einsum("bthd,bshd->btsh", q, k) * jnp.exp(dmat - m_t[:, :, None, :])
    inter = jnp.exp(g - m_t)
    num = jnp.einsum("btsh,bshd->bthd", pmat, v) + inter[..., None] * jnp.einsum("bthk,bhkv->bthv", q, C)
    den = jnp.sum(pmat, axis=2) + inter * jnp.einsum("bthk,bhk->bth", q, n)
    h = num / jnp.maximum(jnp.abs(den), jnp.exp(-m_t))[..., None]
    m_new = m_t[:, -1]
    wgt = jnp.exp(bcum[:, -1:, :] - bcum + ig - m_new[:, None, :])
    decay = jnp.exp(bcum[:, -1] + m - m_new)
    C_new = decay[..., None, None] * C + jnp.einsum("bsh,bshk,bshv->bhkv", wgt, k, v)
    n_new = decay[..., None] * n + jnp.einsum("bsh,bshk->bhk", wgt, k)
    return (C_new, n_new, m_new), h


def mlstm_prompt(q, k, v, ig, lf):
    b, t = q.shape[:2]
    carry = (jnp.zeros((b, H_ML, HD_ML, HD_ML), jnp.float32),
             jnp.zeros((b, H_ML, HD_ML), jnp.float32),
             jnp.zeros((b, H_ML), jnp.float32))
    carry, h_meta = mlstm_block(carry, (q[:, :N_META], k[:, :N_META], v[:, :N_META],
                                        ig[:, :N_META], lf[:, :N_META]))
    hs = [h_meta]
    if t > N_META:
        nb = (t - N_META) // CHUNK
        blocks = tuple(jnp.moveaxis(a[:, N_META:].reshape((b, nb, CHUNK) + a.shape[2:]), 1, 0)
                       for a in (q, k, v, ig, lf))
        carry, h_rest = lax.scan(mlstm_block, carry, blocks)
        hs.append(jnp.moveaxis(h_rest, 0, 1).reshape((b, t - N_META) + h_rest.shape[3:]))
    return jnp.concatenate(hs, axis=1), carry


def layer_output(x, attn, z_a, h_m, c, o_m, z_m, lp):
    b, t, _ = x.shape
    hn = rmsnorm(h_m.astype(x.dtype), lp["head_norm_g"].reshape(H_ML, HD_ML)).reshape(b, t, D_ML)
    branch_m = (jax.nn.sigmoid(o_m) * hn + lp["skip"] * c) * jax.nn.silu(z_m)
    branch_a = attn.reshape(b, t, D_ATT) * jax.nn.silu(z_a)
    mixed = jnp.concatenate([branch_a, branch_m], axis=-1)
    return x + jnp.einsum("bte,ed->btd", mixed, lp["w_out"])


def layer_prompt(x, lp, k_sel):
    b, t, _ = x.shape
    pos = jnp.arange(t).astype(jnp.float32)
    (q, k, v, z_a, qi, ki, wi), (u, v_m, o_m, z_m, ig, lf) = mixer_inputs(x, pos, lp)
    attn = prompt_attention(q, qi, wi, k, v, ki, min(k_sel, t))
    c_pre, conv_buf = causal_conv(u, jnp.zeros((b, CONV_W - 1, D_ML), u.dtype), lp["conv_w"], lp["conv_b"])
    c = jax.nn.silu(c_pre)
    qm, km, vm = mlstm_qkv(c, v_m, lp)
    h_m, (C, n, m) = mlstm_prompt(qm, km, vm, ig, lf)
    y = layer_output(x, attn, z_a, h_m, c, o_m, z_m, lp)
    return y, (k, v, ki, C, n, m, conv_buf)


def layer_sample(x, lp, k_sel, meta_k, meta_v, meta_ki, ck, cv, cki, C, n, m, conv_buf):
    b, t, _ = x.shape
    pos = (N_META + ck.shape[1] + jnp.arange(t)).astype(jnp.float32)
    (q, k, v, z_a, qi, ki, wi), (u, v_m, o_m, z_m, ig, lf) = mixer_inputs(x, pos, lp)

    def with_meta(meta_rows, cache_rows, new_rows):
        meta_b = jnp.broadcast_to(meta_rows, (b,) + meta_rows.shape[1:]).astype(new_rows.dtype)
        return jnp.concatenate([meta_b, cache_rows.astype(new_rows.dtype), new_rows], axis=1)

    attn = index_select_attend(q, qi, wi, with_meta(meta_k, ck, k), with_meta(meta_v, cv, v),
                               with_meta(meta_ki, cki, ki), None, k_sel)
    c_pre, new_buf = causal_conv(u, conv_buf, lp["conv_w"], lp["conv_b"])
    c = jax.nn.silu(c_pre)
    qm, km, vm = mlstm_qkv(c, v_m, lp)
    carry = (C.astype(jnp.float32), n.astype(jnp.float32), m.astype(jnp.float32))
    (C2, n2, m2), h_m = mlstm_block(carry, (qm, km, vm, ig, lf))
    y = layer_output(x, attn, z_a, h_m, c, o_m, z_m, lp)
    return y, (k, v, ki, C2, n2, m2, new_buf)


def setup_inputs(seed: int = 0) -> dict:
    key = jax.random.key(seed)
    ks = jax.random.split(key, 24)
    nrm = jax.random.normal
    f32 = jnp.float32
    return {
        "x_prompt": nrm(ks[0], (BATCH, SEQ, D_MODEL), f32),
        "x_sample": nrm(ks[1], (DEC_BATCH, DEC_SEQ, D_MODEL), f32),
        "cache_k": nrm(ks[2], (DEPTH, DEC_BATCH, PAST_LEN, H_ATT, HD_ATT), f32),
        "cache_v": nrm(ks[3], (DEPTH, DEC_BATCH, PAST_LEN, H_ATT, HD_ATT), f32),
        "cache_kidx": nrm(ks[4], (DEPTH, DEC_BATCH, PAST_LEN, IDX_DIM), f32),
        "state_C": 0.5 * nrm(ks[5], (DEPTH, DEC_BATCH, H_ML, HD_ML, HD_ML), f32),
        "state_n": 0.5 * nrm(ks[6], (DEPTH, DEC_BATCH, H_ML, HD_ML), f32),
        "state_m": nrm(ks[7], (DEPTH, DEC_BATCH, H_ML), f32),
        "state_conv": nrm(ks[8], (DEPTH, DEC_BATCH, CONV_W - 1, D_ML), f32),
        "meta": nrm(ks[9], (N_META, D_MODEL), f32),
        "norm_g": 1.0 + 0.02 * nrm(ks[10], (DEPTH, D_MODEL), f32),
        "w_in": nrm(ks[11], (DEPTH, D_MODEL, D_IN), f32) * D_MODEL ** -0.5,
        "q_norm_g": 1.0 + 0.02 * nrm(ks[12], (DEPTH, HD_ATT), f32),
        "k_norm_g": 1.0 + 0.02 * nrm(ks[13], (DEPTH, HD_ATT), f32),
        "conv_w": nrm(ks[14], (DEPTH, CONV_W, D_ML), f32) * CONV_W ** -0.5,
        "conv_b": 0.02 * nrm(ks[15], (DEPTH, D_ML), f32),
        "wq_m": nrm(ks[16], (DEPTH, H_ML, HD_ML, HD_ML), f32) * HD_ML ** -0.5,
        "wk_m": nrm(ks[17], (DEPTH, H_ML, HD_ML, HD_ML), f32) * HD_ML ** -0.5,
        "b_igate": 0.1 * nrm(ks[18], (DEPTH, H_ML), f32),
        "b_fgate": jnp.linspace(3.0, 6.0, H_ML, dtype=f32)[None, :] + 0.1 * nrm(ks[19], (DEPTH, H_ML), f32),
        "head_norm_g": 1.0 + 0.02 * nrm(ks[20], (DEPTH, D_ML), f32),
        "skip": 1.0 + 0.02 * nrm(ks[21], (DEPTH, D_ML), f32),
        "w_out": nrm(ks[22], (DEPTH, D_MIX, D_MODEL), f32) * D_MIX ** -0.5,
    }


def reference(x_prompt, x_sample, cache_k, cache_v, cache_kidx, state_C, state_n, state_m, state_conv,
              meta, norm_g, w_in, q_norm_g, k_norm_g, conv_w, conv_b, wq_m, wk_m, b_igate, b_fgate,
              head_norm_g, skip, w_out):
    b = x_prompt.shape[0]
    xp = jnp.concatenate([jnp.broadcast_to(meta[None], (b, N_META, D_MODEL)).astype(x_prompt.dtype), x_prompt], axis=1)
    xm = meta[None]
    xs = x_sample
    k_sel_prompt = min(TOPK_MAX, x_prompt.shape[1] // 4)
    k_sel_sample = min(TOPK_MAX, (cache_k.shape[2] + x_sample.shape[1]) // 4)
    p_st = [[] for _ in range(7)]
    s_st = [[] for _ in range(7)]
    for l in range(DEPTH):
        lp = {"norm_g": norm_g[l], "w_in": w_in[l], "q_norm_g": q_norm_g[l], "k_norm_g": k_norm_g[l],
              "conv_w": conv_w[l], "conv_b": conv_b[l], "wq_m": wq_m[l], "wk_m": wk_m[l],
              "b_igate": b_igate[l], "b_fgate": b_fgate[l], "head_norm_g": head_norm_g[l],
              "skip": skip[l], "w_out": w_out[l]}
        xp, st_p = layer_prompt(xp, lp, k_sel_prompt)
        xm, st_m = layer_prompt(xm, lp, N_META)
        xs, st_s = layer_sample(xs, lp, k_sel_sample, st_m[0], st_m[1], st_m[2],
                                cache_k[l], cache_v[l], cache_kidx[l],
                                state_C[l], state_n[l], state_m[l], state_conv[l])
        for i in range(7):
            p_st[i].append(st_p[i])
            s_st[i].append(st_s[i])
    pk, pv, pki, pC, pn, pm, pconv = [jnp.stack(a) for a in p_st]
    sk, sv, ski, sC, sn, sm, sconv = [jnp.stack(a) for a in s_st]
    y_prompt = xp[:, N_META:]
    y_sample = xs
    return (y_prompt, y_sample, pk, pv, pki, pC, pn, pm, pconv, sk, sv, ski, sC, sn, sm, sconv)
```

```python
import numpy as np
from contextlib import ExitStack
import concourse.bass as bass
import concourse.mybir as mybir
from concourse.bass_utils import run_bass_kernel_spmd

F32 = mybir.dt.float32
BF16 = mybir.dt.bfloat16
AF = mybir.ActivationFunctionType
ALU = mybir.AluOpType
AX = mybir.AxisListType

EPS = 1e-6
NEG = -30000.0
SEM_CH = 24000


class Cfg:
    def __init__(self, ntiles=32, past=4096, depth=4, ksel_p=256, ksel_s=256, nsb=4, nbis=18, brk=8.0):
        self.ntiles, self.past, self.depth = ntiles, past, depth
        self.ksel_p, self.ksel_s, self.nsb, self.nbis, self.brk = ksel_p, ksel_s, nsb, nbis, brk
        self.npos = 16 + 128 * ntiles
        self.ncb = past // 128
        self.kc = 16 + past + 32
        self.cw = max(self.npos, self.kc)
        self.nslot = 2 + max(ntiles, self.ncb)


class Op:
    __slots__ = ("eng", "fn", "raw", "oth", "chan", "chan_n", "inc", "waits", "incval")


class Sched:
    def __init__(self):
        self.ops = []
        self.last_w = {}
        self.readers = {}
        self.chan_last = {}
        self.chan_cnt = {}

    def add(self, eng, fn, r=(), w=(), chan=None):
        op = Op()
        op.eng, op.fn, op.chan = eng, fn, chan
        idx = len(self.ops)
        raw, oth = set(), set()
        for k in r:
            lw = self.last_w.get(k)
            if lw is not None:
                raw.add(lw)
        for k in w:
            lw = self.last_w.get(k)
            if lw is not None:
                oth.add(lw)
            for rd in self.readers.get(k, ()):
                oth.add(rd)
        if chan is not None:
            pl = self.chan_last.get(chan)
            if pl is not None:
                raw.add(pl)
            self.chan_last[chan] = idx
            self.chan_cnt[chan] = self.chan_cnt.get(chan, 0) + 1
            op.chan_n = self.chan_cnt[chan]
        op.raw, op.oth = raw, oth
        for k in r:
            self.readers.setdefault(k, []).append(idx)
        for k in w:
            self.last_w[k] = idx
            self.readers[k] = []
        self.ops.append(op)
        return idx

    def finalize(self):
        ops = self.ops
        needed = set()
        for op in ops:
            deps = {}
            for j in op.raw:
                deps[j] = True
            for j in op.oth:
                deps.setdefault(j, False)
            keep = []
            for j, is_raw in deps.items():
                o = ops[j]
                if o.chan is None and o.eng == op.eng and not is_raw:
                    continue
                keep.append(j)
                if o.chan is None:
                    needed.add(j)
            op.waits = keep
        cnt = {}
        for i, op in enumerate(ops):
            op.inc = False
            if op.chan is None and i in needed:
                cnt[op.eng] = cnt.get(op.eng, 0) + 1
                op.inc = True
                op.incval = cnt[op.eng]
        self.eng_cnt = cnt
        waited = {}
        for op in ops:
            tg = {}
            for j in op.waits:
                o = ops[j]
                if o.chan is not None:
                    t, v = ("c", o.chan), 16 * o.chan_n
                else:
                    t, v = ("e", o.eng), o.incval
                if v > tg.get(t, 0):
                    tg[t] = v
            wl = []
            wd = waited.setdefault(op.eng, {})
            for t, v in tg.items():
                if v > wd.get(t, 0):
                    wd[t] = v
                    wl.append((t, v))
            op.waits = wl


def build(cfg):
    NT, PAST, DEPTH = cfg.ntiles, cfg.past, cfg.depth
    NPOS, NCB, KCOLS, CW, NSLOT, NSB = cfg.npos, cfg.ncb, cfg.kc, cfg.cw, cfg.nslot, cfg.nsb
    nc = bass.Bass("TRN2", target_bir_lowering=False)
    S = Sched()

    def din(name, shape):
        return nc.dram_tensor(name, list(shape), F32, kind="ExternalInput").ap()

    def dout(name, shape):
        return nc.dram_tensor(name, list(shape), F32, kind="ExternalOutput").ap()

    def dscr(name, shape):
        return nc.dram_tensor(name, list(shape), F32, kind="Internal").ap()

    xp = din("xp", [128 * NT, 1024])
    xs = din("xs", [NSB, 32, 1024])
    ck = din("ck", [DEPTH, NSB, PAST, 512])
    cv = din("cv", [DEPTH, NSB, PAST, 512])
    cki = din("cki", [DEPTH, NSB, PAST, 64])
    stC = din("stC", [DEPTH, NSB, 4, 128, 128])
    stn = din("stn", [DEPTH, NSB, 4, 128])
    stm = din("stm", [DEPTH, NSB, 4])
    stconv = din("stconv", [DEPTH, NSB, 3, 512])
    meta = din("meta", [16, 1024])
    norm_g = din("norm_g", [DEPTH, 1024])
    w_in = din("w_in", [DEPTH, 1024, 4688])
    q_norm_g = din("q_norm_g", [DEPTH, 64])
    k_norm_g = din("k_norm_g", [DEPTH, 64])
    conv_w = din("conv_w", [DEPTH, 4, 512])
    conv_b = din("conv_b", [DEPTH, 512])
    wq_m = din("wq_m", [DEPTH, 4, 128, 128])
    wk_m = din("wk_m", [DEPTH, 4, 128, 128])
    b_igate = din("b_igate", [DEPTH, 4])
    b_fgate = din("b_fgate", [DEPTH, 4])
    head_norm_g = din("head_norm_g", [DEPTH, 512])
    skip = din("skip", [DEPTH, 512])
    w_out = din("w_out", [DEPTH, 1024, 1024])
    c_ident = din("c_ident", [128, 128])
    c_tri = din("c_tri", [128, 128])
    c_rope = din("c_rope", [KCOLS, 64])
    c_i4 = din("c_i4", [4, 4])
    c_ones4 = din("c_ones4", [4, 128])

    yp = dout("yp", [128 * NT, 1024])
    ys = dout("ys", [NSB, 32, 1024])
    pk = dout("pk", [DEPTH, NPOS, 512])
    pv = dout("pv", [DEPTH, NPOS, 512])
    pki = dout("pki", [DEPTH, NPOS, 64])
    pC = dout("pC", [DEPTH, 4, 128, 128])
    pn = dout("pn", [DEPTH, 4, 128])
    pm = dout("pm", [DEPTH, 4])
    pconv = dout("pconv", [DEPTH, 3, 512])
    sk = dout("sk", [DEPTH, NSB, 32, 512])
    sv = dout("sv", [DEPTH, NSB, 32, 512])
    ski = dout("ski", [DEPTH, NSB, 32, 64])
    sC = dout("sC", [DEPTH, NSB, 4, 128, 128])
    sn = dout("sn", [DEPTH, NSB, 4, 128])
    sm = dout("sm", [DEPTH, NSB, 4])
    sconv = dout("sconv", [DEPTH, NSB, 3, 512])

    XPs = [dscr("XP0", [NPOS, 1024]), dscr("XP1", [NPOS, 1024])]
    XSs = [dscr("XS0", [NSB, 32, 1024]), dscr("XS1", [NSB, 32, 1024])]
    YAp = dscr("YAp", [NPOS, 1024])
    YAs = dscr("YAs", [NSB, 32, 1024])

    es = ExitStack()

    def sb(name, shape, dt=F32):
        return es.enter_context(nc.sbuf_tensor(name, list(shape), dt))

    WBUF = sb("WBUF", [128, 8, 2632], BF16)
    WOUT = sb("WOUT", [128, 4, 1024], BF16)
    KT = sb("KT", [128, 4, CW], BF16)
    VA = sb("VA", [128, NSLOT, 8, 65], BF16)
    KI2 = sb("KI2", [128, CW], BF16)
    SC = sb("SC", [128, CW], F32)
    MB = [sb("MB0", [128, CW], BF16), sb("MB1", [128, max(CW, 4128)], BF16)]
    XT = [sb("XT0", [128, 1024]), sb("XT1", [128, 1024])]
    YT = sb("YT", [128, 1024])
    CS = [sb("CS0", [128, 64]), sb("CS1", [128, 64])]
    HN = sb("HN", [128, 1024], BF16)
    HNT = sb("HNT", [128, 8, 128], BF16)
    FS = [sb("FS%d" % i, [128, 528]) for i in range(9)]
    BS = [sb("BS%d" % i, [128, 512], BF16) for i in range(6)]
    QTB = [sb("QTB0", [128, 4, 2, 128], BF16), sb("QTB1", [128, 4, 2, 128], BF16)]
    QIT = sb("QIT", [128, 4, 128], BF16)
    SM = sb("SM", [128, 128])
    SR = sb("SR", [4, 3, 128])
    SG = sb("SG", [4, 16])
    IDF = sb("IDF", [128, 128])
    IDB = sb("IDB", [128, 128], BF16)
    I2 = sb("I2", [128, 2, 128], BF16)
    TRI = sb("TRI", [128, 128])
    I4 = sb("I4", [4, 4])
    DD = sb("DD", [4, 4])
    ONES4 = sb("ONES4", [4, 128])
    GT = sb("GT", [128, 8])
    GQ = sb("GQ", [128, 64])
    GK = sb("GK", [128, 64])
    GBI = sb("GBI", [128, 4])
    GBF = sb("GBF", [128, 4])
    CVW = sb("CVW", [128, 4, 4])
    CVB = sb("CVB", [128, 4])
    MB1f = MB[1][:, 0:4128].bitcast(F32)
    GH = MB1f[:, 0:512]
    SKP = MB1f[:, 512:1024]
    UEXT = MB1f[:, 1024:1548].rearrange("p (b t) -> p b t", t=131)
    CNS = MB1f[:, 1548:2064].rearrange("p (h d) -> p h d", d=129)
    WQK = QTB[1][:, :, :, :].rearrange("p a b c -> p (a b c)").rearrange("p (q h e) -> p q h e", q=2, h=4)
    FS8b = FS[8][:, 0:516].bitcast(BF16)
    CNB = FS8b[:, 0:516].rearrange("p (h d) -> p h d", d=129)
    VM = FS8b[:, 516:1032].rearrange("p (h d) -> p h d", d=129)
    ALIAS = [("MB", 1), "GH", "SKP", "UEXT", "CNS", ("QTB", 1), "WQK", "FS8", "CNB", "VM"]
    PB = [es.enter_context(nc.psum_tensor("PB%d" % i, [128, 512], F32)) for i in range(8)]

    def pbf(i):
        return PB[i][:, :].bitcast(BF16)

    def P(i):
        return ("P", i)

    def dma(q, out, in_, r, w, chan, nonc=False):
        if nonc:
            S.add(q, lambda e: e.dma_start(out=out, in_=in_, allow_slow_non_contiguous=True), r, w, chan)
        else:
            S.add(q, lambda e: e.dma_start(out=out, in_=in_), r, w, chan)

    def mm(out, lhsT, rhs, start, stop, r, w, skip=False):
        S.add("pe", lambda e: e.matmul(out, lhsT=lhsT, rhs=rhs, start=start, stop=stop,
                                       skip_group_check=skip), r, w)

    def tr(out, in_, ident, r, w):
        S.add("pe", lambda e: e.transpose(out=out, in_=in_, identity=ident), r, w)

    def act(out, in_, func, r, w, bias=0.0, scale=1.0, accum=None):
        if accum is None:
            S.add("act", lambda e: e.activation(out=out, in_=in_, func=func, bias=bias, scale=scale), r, w)
        else:
            S.add("act", lambda e: e.activation(out=out, in_=in_, func=func, bias=bias, scale=scale,
                                                accum_out=accum), r, w)

    def ts(out, in0, s1, s2, op0, op1, r, w, eng="dve", accum=None):
        if op1 is None:
            S.add(eng, lambda e: e.tensor_scalar(out=out, in0=in0, scalar1=s1, scalar2=None, op0=op0), r, w)
        elif accum is None:
            S.add(eng, lambda e: e.tensor_scalar(out=out, in0=in0, scalar1=s1, scalar2=s2, op0=op0, op1=op1), r, w)
        else:
            S.add(eng, lambda e: e.tensor_scalar(out=out, in0=in0, scalar1=s1, scalar2=s2, op0=op0, op1=op1,
                                                 accum_out=accum), r, w)

    def tt(out, in0, in1, op, r, w, eng="dve"):
        S.add(eng, lambda e: e.tensor_tensor(out=out, in0=in0, in1=in1, op=op), r, w)

    def stt(out, in0, scalar, in1, op0, op1, r, w):
        S.add("dve", lambda e: e.scalar_tensor_tensor(out=out, in0=in0, scalar=scalar, in1=in1, op0=op0, op1=op1), r, w)

    def cp(out, in_, r, w, eng="dve"):
        if eng == "act":
            S.add(eng, lambda e: e.activation(out=out, in_=in_, func=AF.Copy), r, w)
        else:
            S.add(eng, lambda e: e.tensor_copy(out=out, in_=in_), r, w)

    def mset(ap, val, w, eng="dve"):
        S.add(eng, lambda e: e.memset(ap, val), (), w)

    def red(out, in_, op, r, w):
        S.add("dve", lambda e: e.tensor_reduce(out=out, in_=in_, axis=AX.X, op=op), r, w)

    def recip(out, in_, r, w):
        S.add("dve", lambda e: e.reciprocal(out=out, in_=in_), r, w)

    dma("sp", IDF[:, :], c_ident, (), ["IDF"], "c0")
    dma("sp", TRI[:, :], c_tri, (), ["TRI"], "c1")
    dma("sp", I4[:, :], c_i4, (), ["I4"], "c2")
    dma("sp", ONES4[:, :], c_ones4, (), ["ONES4"], "c3")
    cp(IDB[:, :], IDF[:, :], ["IDF"], ["IDB"])
    cp(I2[:, 0, :], IDF[:, :], ["IDF"], ["I2"])
    cp(I2[:, 1, :], IDF[:, :], ["IDF"], ["I2"])
    mset(QTB[0][:, :, :, :], 0.0, [("QTB", 0)])
    mset(VA[:, :, :, 64:65], 1.0, [("V", sl) for sl in range(NSLOT)])

    def barrier(keys):
        S.add("dve", lambda e: e.memset(SM[:, 127:128], 0.0), (), list(keys))

    def load_layer_params(l):
        dma("sp", GT[:, :], norm_g[l].rearrange("(k p) -> p k", p=128), (), ["GT"], "g0", nonc=True)
        dma("sp", GQ[:, :], q_norm_g[l].partition_broadcast(128), (), ["GQ"], "g1")
        dma("sp", GK[:, :], k_norm_g[l].partition_broadcast(128), (), ["GK"], "g2")
        dma("sp", GBI[:, :], b_igate[l].partition_broadcast(128), (), ["GBI"], "g3")
        dma("sp", GBF[:, :], b_fgate[l].partition_broadcast(128), (), ["GBF"], "g4")
        for j in range(4):
            dma("sp", CVW[:, :, j], conv_w[l, j].rearrange("(b f) -> f b", f=128), (), ["CVW"], "g7%d" % j, nonc=True)
        dma("sp", CVB[:, :], conv_b[l].rearrange("(b f) -> f b", f=128), (), ["CVB"], "g8", nonc=True)

    def load_weights_A(l):
        dma("pool", WBUF[:, :, 0:2632], w_in[l][:, 0:2632].rearrange("(k p) n -> p k n", p=128), (), ["WBUF"], "w0")
        dma("pool", WOUT[:, :, :], w_out[l][0:512, :].rearrange("(k p) n -> p k n", p=128), (), ["WOUT"], "w1")

    def load_weights_B(l):
        dma("pool", WBUF[:, :, 0:2056], w_in[l][:, 2632:4688].rearrange("(k p) n -> p k n", p=128), (), ["WBUF"], "w0")
        dma("pool", WOUT[:, :, :], w_out[l][512:1024, :].rearrange("(k p) n -> p k n", p=128), (), ["WOUT"], "w1")
        dma("pool", WQK[:, 0, :, :], wq_m[l].rearrange("h d e -> d h e"), (), ["WQK"], "w2")
        dma("pool", WQK[:, 1, :, :], wk_m[l].rearrange("h d e -> d h e"), (), ["WQK"], "w3")
        dma("sp", GH, head_norm_g[l].partition_broadcast(128), (), ["GH"], "g5")
        dma("sp", SKP, skip[l].partition_broadcast(128), (), ["SKP"], "g6")

    xslot = [0]

    def load_norm(nt, xsrc, xkey, pos0=None):
        s = xslot[0]
        xslot[0] ^= 1
        xt = XT[s]
        dma("sp", xt[:nt, :], xsrc, [xkey], [("XT", s)], "xt%d" % s)
        if pos0 is not None:
            dma("sp", CS[s][:nt, :], c_rope[pos0:pos0 + nt, :], (), [("CS", s)], "cs%d" % s)
        act(HN[:nt, :], xt[:nt, :], AF.Square, [("XT", s)], ["HN", "ss"], accum=SM[:nt, 0:1])
        act(SM[:nt, 1:2], SM[:nt, 0:1], AF.Ln, ["ss"], ["ss1"], bias=EPS, scale=1.0 / 1024)
        act(SM[:nt, 2:3], SM[:nt, 1:2], AF.Exp, ["ss1"], ["rstd"], scale=-0.5)
        act(HN[:nt, :], xt[:nt, :], AF.Copy, [("XT", s), "rstd"], ["HN"], scale=SM[:nt, 2:3])
        pt = pbf(2).rearrange("p (k t) -> p k t", t=128)
        for kc in range(8):
            tr(pt[:, kc, :nt], HN[:nt, kc * 128:(kc + 1) * 128], IDB[:nt, :nt], ["HN", "IDB"], [P(2)])
        tt(HNT[:, :, :nt], pt[:, :, :nt], GT[:, :].unsqueeze(2).to_broadcast([128, 8, nt]), ALU.mult,
           ["GT"], [P(2), "HNT"])
        return s

    def inproj(nt, c0, n, bank):
        for kc in range(8):
            mm(PB[bank][:nt, 0:n], HNT[:, kc, :nt], WBUF[:, kc, c0:c0 + n], kc == 0, kc == 7,
               ["HNT", "WBUF"], [P(bank)])

    def rope(nt, s, x1, x2, o1, o2, nh, rkeys, wkey, tmp, tkey="FS2"):
        cosb = CS[s][:nt, 0:32].unsqueeze(1).to_broadcast([nt, nh, 32])
        sinb = CS[s][:nt, 32:64].unsqueeze(1).to_broadcast([nt, nh, 32])
        t1 = tmp[:nt, 0:nh * 32].rearrange("p (h d) -> p h d", d=32)
        t2 = tmp[:nt, 256:256 + nh * 32].rearrange("p (h d) -> p h d", d=32)
        rk = list(rkeys) + [("CS", s)]
        tt(t1, x1, cosb, ALU.mult, rk, [tkey])
        tt(t2, x2, sinb, ALU.mult, rk, [tkey])
        tt(o1, t1, t2, ALU.subtract, [tkey], [wkey])
        tt(t1, x2, cosb, ALU.mult, rk, [tkey])
        tt(t2, x1, sinb, ALU.mult, rk, [tkey])
        tt(o2, t1, t2, ALU.add, [tkey], [wkey])

    def qknorm(nt, bank, gtile, gkey, dst, dkey):
        ps3 = PB[bank][:nt, :].rearrange("p (h d) -> p h d", d=64)
        sq = FS[1]
        act(sq[:nt, 0:512], PB[bank][:nt, :], AF.Square, (), [P(bank), "FS1"])
        red(SM[:nt, 8:16], sq[:nt, 0:512].rearrange("p (h d) -> p h d", d=64), ALU.add, ["FS1"], ["qss"])
        act(SM[:nt, 16:24], SM[:nt, 8:16], AF.Ln, ["qss"], ["qss1"], bias=EPS, scale=1.0 / 64)
        act(SM[:nt, 24:32], SM[:nt, 16:24], AF.Exp, ["qss1"], ["qrstd"], scale=-0.5)
        d3 = dst[:nt, 0:512].rearrange("p (h d) -> p h d", d=64)
        tt(d3, ps3, SM[:nt, 24:32].unsqueeze(2).to_broadcast([nt, 8, 64]), ALU.mult, ["qrstd"], [P(bank), dkey])
        tt(d3, d3, gtile[:nt, :].unsqueeze(1).to_broadcast([nt, 8, 64]), ALU.mult, [dkey, gkey], [dkey])

    WSC = 0.125 * (8.0 ** -0.5)

    tcount = [0]

    def slot_cols(slot):
        if slot == 0:
            return (0, 16)
        if slot == NSLOT - 1:
            return (16 + PAST, 16 + PAST + 32)
        return (16 + 128 * (slot - 1), 16 + 128 * slot)

    def passA_front(l, nt, xsrc, xkey, pos0, blocks, slot_own, kdst, vdst, kidst, ya_dst, ya_key, ksel, chunkmask):
        par = tcount[0] % 2
        tcount[0] += 1
        col_own = pos0
        s = load_norm(nt, xsrc, xkey, pos0)
        pt = pbf(2).rearrange("p (k t) -> p k t", t=128)
        qtb = QTB[par]
        qkey = ("QTB", par)
        inproj(nt, 0, 512, 0)
        inproj(nt, 512, 512, 1)
        qn = FS[0]
        qknorm(nt, 0, GQ, "GQ", qn, "FS0")
        q3 = qn[:nt, 0:512].rearrange("p (h d) -> p h d", d=64)
        qr = BS[0]
        qr3 = qr[:nt, :].rearrange("p (h d) -> p h d", d=64)
        rope(nt, s, q3[:, :, 0:32], q3[:, :, 32:64], qr3[:, :, 0:32], qr3[:, :, 32:64], 8, ["FS0"], "BS0", FS[2])
        inproj(nt, 1024, 512, 0)
        for p in range(4):
            tr(pt[:, p, :nt], qr[:nt, p * 128:(p + 1) * 128], IDB[:nt, :nt], ["BS0", "IDB"], [P(2)])
        cp(qtb[0:64, :, 0, :nt], pt[0:64, 0:4, :nt], (), [P(2), qkey])
        cp(qtb[64:128, :, 1, :nt], pt[64:128, 0:4, :nt], (), [P(2), qkey])
        kn = FS[3]
        qknorm(nt, 1, GK, "GK", kn, "FS3")
        k3 = kn[:nt, 0:512].rearrange("p (h d) -> p h d", d=64)
        kr = FS[4]
        kr3 = kr[:nt, 0:512].rearrange("p (h d) -> p h d", d=64)
        rope(nt, s, k3[:, :, 0:32], k3[:, :, 32:64], kr3[:, :, 0:32], kr3[:, :, 32:64], 8, ["FS3"], "FS4", FS[2])
        inproj(nt, 1536, 512, 1)
        dma("pool", kdst, kr[:nt, 0:512], ["FS4"], (), "ko")
        kb16 = BS[1]
        cp(kb16[:nt, :], kr[:nt, 0:512], ["FS4"], ["BS1"])
        for p in range(4):
            tr(pt[:, p, :nt], kb16[:nt, p * 128:(p + 1) * 128], IDB[:nt, :nt], ["BS1", "IDB"], [P(2)])
        cp(KT[:, :, col_own:col_own + nt], pt[:, 0:4, :nt], (), [P(2), ("KT", slot_own)])
        vf = FS[5]
        cp(vf[:nt, 0:512], PB[0][:nt, :], (), [P(0), "FS5"], eng="act")
        inproj(nt, 2048, 512, 0)
        dma("pool", vdst, vf[:nt, 0:512], ["FS5"], (), "vo")
        cp(VA[:nt, slot_own, :, 0:64], vf[:nt, 0:512].rearrange("p (h d) -> p h d", d=64), ["FS5"], [("V", slot_own)])
        zi = 6 if par == 0 else 8
        zas = FS[zi]
        zkey = "FS%d" % zi
        act(zas[:nt, 0:512], PB[1][:nt, :], AF.Silu, (), [P(1), zkey])
        inproj(nt, 2560, 72, 1)
        qir = BS[0]
        qir3 = qir[:nt, :].rearrange("p (h d) -> p h d", d=64)
        cp(FS[0][:nt, 0:512], PB[0][:nt, :], (), [P(0), "FS0"], eng="act")
        qf3 = FS[0][:nt, 0:512].rearrange("p (h d) -> p h d", d=64)
        rope(nt, s, qf3[:, :, 0:32], qf3[:, :, 32:64], qir3[:, :, 0:32], qir3[:, :, 32:64], 8, ["FS0"], "BS0", FS[2])
        for p in range(4):
            tr(pt[:, p, :nt], qir[:nt, p * 128:(p + 1) * 128], IDB[:nt, :nt], ["BS0", "IDB"], [P(2)])
        cp(QIT[:, :, :nt], pt[:, 0:4, :nt], (), [P(2), "QIT"])
        kif = FS[7]
        cp(kif[:nt, 0:72], PB[1][:nt, 0:72], (), [P(1), "FS7"], eng="act")
        ts(SM[:nt, 32:40], kif[:nt, 64:72], WSC, None, ALU.mult, None, ["FS7"], ["wis"])
        ki1 = kif[:nt, 0:64].rearrange("p (h d) -> p h d", d=64)
        kio = kif[:nt, 128:192].rearrange("p (h d) -> p h d", d=64)
        rope(nt, s, ki1[:, :, 0:32], ki1[:, :, 32:64], kio[:, :, 0:32], kio[:, :, 32:64], 1, ["FS7"], "FS7", FS[2])
        dma("pool", kidst, kif[:nt, 128:192], ["FS7"], (), "kio")
        ki2 = BS[1]
        cp(ki2[:nt, 0:64], kif[:nt, 128:192], ["FS7"], ["BS1"])
        cp(ki2[:nt, 64:128], kif[:nt, 128:192], ["FS7"], ["BS1"])
        tr(pt[:, 4, :nt], ki2[:nt, 0:128], IDB[:nt, :nt], ["BS1", "IDB"], [P(2)])
        cp(KI2[:, col_own:col_own + nt], pt[:, 4, :nt], (), [P(2), ("KI", slot_own)])
        attend_front(nt, blocks, ksel, chunkmask, par)
        return dict(nt=nt, s=s, par=par, blocks=blocks, zas=zas, zkey=zkey, ya_dst=ya_dst, ya_key=ya_key)

    def passA_back(c):
        nt, s, par, zas, zkey = c["nt"], c["s"], c["par"], c["zas"], c["zkey"]
        pt = pbf(2).rearrange("p (k t) -> p k t", t=128)
        attend_back(nt, c["blocks"], par)
        at = FS[0]
        akey = "FS0"
        for b in range(2):
            ov = PB[6 + b][:nt, 0:260].rearrange("p (h d) -> p h d", d=65)
            recip(SM[:nt, 40 + 4 * b:44 + 4 * b], ov[:, :, 64], (), [P(6 + b), "rden"])
            tt(at[:nt, 256 * b:256 * (b + 1)].rearrange("p (h d) -> p h d", d=64), ov[:, :, 0:64],
               SM[:nt, 40 + 4 * b:44 + 4 * b].unsqueeze(2).to_broadcast([nt, 4, 64]), ALU.mult,
               ["rden"], [P(6 + b), akey])
        mxa = BS[2]
        tt(mxa[:nt, :], at[:nt, 0:512], zas[:nt, 0:512], ALU.mult, [akey, zkey], ["BS2"])
        for p in range(4):
            tr(pt[:, p, :nt], mxa[:nt, p * 128:(p + 1) * 128], IDB[:nt, :nt], ["BS2", "IDB"], [P(2)])
        mxt = BS[3]
        mxt3 = mxt[:, :].rearrange("p (k t) -> p k t", t=128)
        cp(mxt3[:, :, :nt], pt[:, 0:4, :nt], (), [P(2), "BS3"])
        for hf in range(2):
            for kc in range(4):
                mm(PB[hf][:nt, :], mxt3[:, kc, :nt], WOUT[:, kc, hf * 512:(hf + 1) * 512], kc == 0, kc == 3,
                   ["BS3", "WOUT"], [P(hf)])
            tt(YT[:nt, hf * 512:(hf + 1) * 512], PB[hf][:nt, :], XT[s][:nt, hf * 512:(hf + 1) * 512], ALU.add,
               [("XT", s)], [P(hf), "YT"])
        dma("pool", c["ya_dst"], YT[:nt, :], ["YT"], [c["ya_key"]], "yo")

    def attend_front(nt, blocks, ksel, chunkmask, par):
        L = sum(b[1] for b in blocks)
        assert blocks[0][0] == 0
        mb = MB[par]
        mkey = ("MB", par)
        ibanks = [3, 4, 5]
        nb = 0
        c0 = 0
        while c0 < L:
            n = min(512, L - c0)
            kikeys = [("KI", bl[2]) for bl in blocks if bl[0] < c0 + n and bl[0] + bl[1] > c0]
            for h in range(8):
                p, e = divmod(h, 2)
                bank = ibanks[nb % 3]
                ri = (1, 3)[nb % 2]
                rbuf = FS[ri]
                rkey = "FS%d" % ri
                nb += 1
                mm(PB[bank][:nt, 0:n], QIT[64 * e:64 * e + 64, p, :nt], KI2[64 * e:64 * e + 64, c0:c0 + n],
                   True, True, ["QIT"] + kikeys, [P(bank)])
                act(rbuf[:nt, 0:n], PB[bank][:nt, 0:n], AF.Relu, (), [P(bank), rkey])
                if h == 0:
                    ts(SC[:nt, c0:c0 + n], rbuf[:nt, 0:n], SM[:nt, 32:33], None, ALU.mult, None,
                       [rkey, "wis"], ["SC"])
                else:
                    stt(SC[:nt, c0:c0 + n], rbuf[:nt, 0:n], SM[:nt, 32 + h:33 + h], SC[:nt, c0:c0 + n],
                        ALU.mult, ALU.add, [rkey, "wis", "SC"], ["SC"])
            c0 += n
        if chunkmask:
            mset(SC[0:64, L - 64:L], -1.0e4, ["SC"])
        LO = cfg.brk
        if L > ksel:
            thr = SM[:nt, 48:49]
            cnt = SM[:nt, 49:50]
            tmp = SM[:nt, 50:51]
            nth = [SM[:nt, 51:52], SM[:nt, 52:53]]
            use_act = False
            st = LO
            if not use_act:
                mset(thr, 0.0, ["thr"])
                for it in range(cfg.nbis):
                    ts(mb[:nt, 0:L], SC[:nt, 0:L], thr, None, ALU.is_ge, ALU.add, ["SC", "thr"], [mkey, "cnt"], accum=cnt)
                    ts(tmp, cnt, ksel - 0.5, st, ALU.is_ge, ALU.mult, ["cnt"], ["btmp"])
                    if it < cfg.nbis - 1:
                        stt(thr, tmp, -st / 2, thr, ALU.add, ALU.add, ["btmp", "thr"], ["thr"])
                    else:
                        stt(thr, tmp, -st, thr, ALU.add, ALU.add, ["btmp", "thr"], ["thr"])
                    st /= 2
            else:
                mset(nth[0], 0.0, ["nth0"])
                cc = 2.0 * ksel - 1.0 - L
                for it in range(cfg.nbis):
                    a_, b_ = it % 2, (it + 1) % 2
                    act(mb[:nt, 0:L], SC[:nt, 0:L], AF.Sign, ["SC", "nth%d" % a_], [mkey, "cnt"], bias=nth[a_], accum=cnt)
                    act(tmp, cnt, AF.Sign, ["cnt"], ["btmp"], bias=0.5 - cc)
                    act(nth[b_], tmp, AF.Identity, ["btmp", "nth%d" % a_], ["nth%d" % b_], bias=nth[a_], scale=-st / 2)
                    st /= 2
                fin = cfg.nbis % 2
                st_last = st * 2
                act(thr, nth[fin], AF.Identity, ["nth%d" % fin], ["thr"], bias=-st_last / 2, scale=-1.0)
            ts(mb[:nt, 0:L], SC[:nt, 0:L], thr, NEG, ALU.is_lt, ALU.mult, ["SC", "thr"], [mkey])
        else:
            ts(mb[:nt, 0:L], SC[:nt, 0:L], -LO, NEG, ALU.is_lt, ALU.mult, ["SC"], [mkey])

    def attend_back(nt, blocks, par):
        mb = MB[par]
        mkey = ("MB", par)
        qtb = QTB[par]
        qkey = ("QTB", par)
        qbanks = [(3, 4), (0, 1)]
        nblk = len(blocks)
        first = [True, True]
        npt = 0
        for bi, (c0, kb, slot) in enumerate(blocks):
            bp = qbanks[bi % 2]
            for half in range(2):
                bank = bp[half]
                for pp in range(2):
                    p = 2 * half + pp
                    o3 = PB[bank][:kb, pp * 2 * nt:(pp + 1) * 2 * nt].rearrange("p (e t) -> p e t", e=2)
                    mm(o3, KT[:, p, c0:c0 + kb], qtb[:, p, :, :nt], True, False, [("KT", slot), qkey], [P(bank)])
                    mm(o3, mb[:nt, c0:c0 + kb], I2[:nt, :, :nt], False, True, [mkey, "I2"], [P(bank)])
                pi = 4 + (npt % 2)
                ptb = BS[pi]
                pkey = "BS%d" % pi
                npt += 1
                act(ptb[:kb, 0:4 * nt], PB[bank][:kb, 0:4 * nt], AF.Exp, (), [P(bank), pkey], scale=0.125)
                for pp in range(2):
                    for e in range(2):
                        h = 2 * (2 * half + pp) + e
                        ob = 6 + h // 4
                        hh = h % 4
                        st_ = first[h // 4]
                        first[h // 4] = False
                        mm(PB[ob][:nt, hh * 65:(hh + 1) * 65], ptb[:kb, (pp * 2 + e) * nt:(pp * 2 + e + 1) * nt],
                           VA[:kb, slot, h, :], st_, (bi == nblk - 1 and hh == 3), [pkey, ("V", slot)], [P(ob)], skip=True)

    def passB(l, nt, xsrc, xkey, ya_src, ya_key, out_dsts, out_keys):
        s = load_norm(nt, xsrc, xkey, None)
        dma("sp", YT[:nt, :], ya_src, [ya_key], ["YT"], "yi")
        O = 64
        u3 = PB[3][:, :].rearrange("p (b t) -> p b t", t=128)
        for b in range(4):
            for kc in range(8):
                mm(u3[:, b, :nt], WBUF[:, kc, b * 128:(b + 1) * 128], HNT[:, kc, :nt], kc == 0, kc == 7,
                   ["HNT", "WBUF"], [P(3)])
        cp(UEXT[:, :, 3:3 + nt], u3[:, :, :nt], (), [P(3), "UEXT"], eng="act")
        inproj(nt, 512, 512, 0)
        cp(VM[:nt, :, 0:128], PB[0][:nt, :].rearrange("p (h d) -> p h d", d=128), (), [P(0), "VM"])
        inproj(nt, 1024, 512, 1)
        sgo = FS[0]
        act(sgo[:nt, 0:512], PB[1][:nt, :], AF.Sigmoid, (), [P(1), "FS0"])
        inproj(nt, 1536, 512, 0)
        szm = FS[1]
        act(szm[:nt, 0:512], PB[0][:nt, :], AF.Silu, (), [P(0), "FS1"])
        inproj(nt, 2048, 8, 1)
        tt(SM[:nt, O + 0:O + 4], PB[1][:nt, 0:4], GBI[:nt, :], ALU.add, ["GBI"], [P(1), "igb"])
        tt(SM[:nt, O + 4:O + 8], PB[1][:nt, 4:8], GBF[:nt, :], ALU.add, ["GBF"], [P(1), "fgb"])
        act(SM[:nt, O + 8:O + 12], SM[:nt, O + 4:O + 8], AF.Exp, ["fgb"], ["e1"], scale=-1.0)
        act(SM[:nt, O + 12:O + 16], SM[:nt, O + 8:O + 12], AF.Ln, ["e1"], ["nlf"], bias=1.0)
        mm(PB[5][0:4, 0:nt], SM[:nt, O + 0:O + 4], IDF[:nt, :nt], True, False, ["igb", "IDF"], [P(5)])
        mm(PB[5][0:4, 0:nt], SM[:nt, O + 12:O + 16], TRI[:nt, :nt], False, True, ["nlf", "TRI"], [P(5)])
        mm(PB[5][0:4, 128:128 + nt], SM[:nt, O + 12:O + 16], TRI[:nt, :nt], True, True, ["nlf", "TRI"], [P(5)])
        MU, NBC, MUN, DLT, DEC, NMU, CB, AMX = [SG[0:4, i:i + 1] for i in range(8)]
        arow = SR[0:4, 0, :nt]
        orow = SR[0:4, 1, :nt]
        crow = SR[0:4, 2, :nt]
        ts(arow, PB[5][0:4, 0:nt], NBC, None, ALU.add, None, ["NBC"], [P(5), "arow"])
        red(AMX, arow, ALU.max, ["arow"], ["AMX"])
        tt(MUN, AMX, MU, ALU.max, ["AMX", "MU"], ["MUN"])
        tt(DLT, MU, MUN, ALU.subtract, ["MU", "MUN"], ["DLT"])
        act(DEC, DLT, AF.Exp, ["DLT"], ["DEC"])
        ts(NMU, MUN, -1.0, None, ALU.mult, None, ["MUN"], ["NMU"])
        tt(CB, NBC, MUN, ALU.subtract, ["NBC", "MUN"], ["CB"])
        act(orow, arow, AF.Exp, ["arow", "NMU"], ["orow"], bias=NMU)
        act(crow, PB[5][0:4, 128:128 + nt], AF.Exp, ["CB"], [P(5), "crow"], bias=CB)
        tt(NBC, NBC, PB[5][0:4, 128 + nt - 1:128 + nt], ALU.add, ["NBC"], [P(5), "NBC"])
        cp(MU, MUN, ["MUN"], ["MU"])
        ts(DD[:, :], I4[:, :], DEC, None, ALU.mult, None, ["I4", "DEC"], ["DD"])
        mm(PB[5][:nt, 256:260], orow, I4[:, :], True, True, ["orow", "I4"], [P(5)])
        mm(PB[5][:nt, 260:264], crow, I4[:, :], True, True, ["crow", "I4"], [P(5)])
        mm(PB[5][:, 264:268], ONES4[:, :], DD[:, :], True, True, ["ONES4", "DD"], [P(5)])
        cp(SM[:nt, O + 16:O + 24], PB[5][:nt, 256:264], (), [P(5), "wc"])
        cp(SM[:, O + 24:O + 28], PB[5][:, 264:268], (), [P(5), "dbc"])
        WCO = O + 16
        CLO = O + 20
        DBO = O + 24
        cacc = FS[2]
        ca3 = cacc[:, 0:512].rearrange("p (b t) -> p b t", t=128)
        for b in range(4):
            ts(ca3[:, b, :nt], UEXT[:, b, 0:nt], CVW[:, b, 0:1], CVB[:, b:b + 1], ALU.mult, ALU.add,
               ["UEXT", "CVW", "CVB"], ["FS2"])
            for j in range(1, 4):
                stt(ca3[:, b, :nt], UEXT[:, b, j:j + nt], CVW[:, b, j:j + 1], ca3[:, b, :nt], ALU.mult, ALU.add,
                    ["UEXT", "CVW", "FS2"], ["FS2"])
        ct = BS[0]
        ct3 = ct[:, :].rearrange("p (b t) -> p b t", t=128)
        act(ct3[:, :, :nt], ca3[:, :, :nt], AF.Silu, ["FS2"], ["BS0"])
        cp(UEXT[:, :, 0:3], UEXT[:, :, nt:nt + 3], ["UEXT"], ["UEXT"])
        q3p = PB[4][:, :].rearrange("p (h t) -> p h t", t=128)
        for h in range(4):
            mm(q3p[:, h, :nt], WQK[:, 0, h, :], ct3[:, h, :nt], True, True, ["WQK", "BS0"], [P(4)])
        qmt = BS[1]
        qmt3 = qmt[:, :].rearrange("p (h t) -> p h t", t=128)
        ts(qmt3[:, :, :nt], q3p[:, :, :nt], 128.0 ** -0.5, None, ALU.mult, None, (), [P(4), "BS1"])
        for h in range(4):
            mm(q3p[:, h, :nt], WQK[:, 1, h, :], ct3[:, h, :nt], True, True, ["WQK", "BS0"], [P(4)])
        kmt = BS[2]
        kmt3 = kmt[:, :].rearrange("p (h t) -> p h t", t=128)
        cp(kmt3[:, :, :nt], q3p[:, :, :nt], (), [P(4), "BS2"])
        k3p = PB[3][:nt, :].rearrange("p (h e) -> p h e", e=128)
        for h in range(4):
            mm(k3p[:, h, :], ct3[:, h, :nt], WQK[:, 1, h, :], True, True, ["WQK", "BS0"], [P(3)])
        kw = BS[3]
        kw3 = kw[:nt, :].rearrange("p (h e) -> p h e", e=128)
        for h in range(4):
            ts(kw3[:, h, :], k3p[:, h, :], SM[:nt, WCO + h:WCO + h + 1], None, ALU.mult, None, ["wc"], [P(3), "BS3"])
        s3p = PB[4][:nt, :].rearrange("p (h t) -> p h t", t=128)
        for h in range(4):
            mm(s3p[:, h, :nt], kmt3[:, h, :nt], qmt3[:, h, :nt], True, True, ["BS2", "BS1"], [P(4)])
        pmt = BS[4]
        pmt3 = pmt[:nt, :].rearrange("p (h t) -> p h t", t=128)
        for h in range(4):
            stt(pmt3[:, h, :nt], s3p[:, h, :nt], SM[:nt, WCO + h:WCO + h + 1], TRI[:nt, :nt], ALU.mult, ALU.mult,
                ["wc", "TRI"], [P(4), "BS4"])
        for h in range(4):
            ts(CNS[:, h, :], CNS[:, h, :], SM[:, DBO + h:DBO + h + 1], None, ALU.mult, None, ["dbc", "CNS"], ["CNS"])
        cp(CNB[:, :, :], CNS[:, :, :], ["CNS"], ["CNB"], eng="act")
        for h in range(4):
            b, hh = divmod(h, 2)
            o = PB[6 + b][:nt, hh * 129:(hh + 1) * 129]
            mm(o, pmt3[:, h, :nt], VM[:nt, h, :], True, False, ["BS4", "VM"], [P(6 + b)])
            mm(o, qmt3[:, h, :nt], CNB[:, h, :], False, True, ["BS1", "CNB"], [P(6 + b)])
        for h in range(4):
            b, hh = divmod(h, 2)
            mm(PB[b][:, hh * 129:(hh + 1) * 129], kw3[:, h, :], VM[:nt, h, :], True, True, ["BS3", "VM"], [P(b)])
        for b in range(2):
            tt(CNS[:, 2 * b:2 * b + 2, :], CNS[:, 2 * b:2 * b + 2, :],
               PB[b][:, 0:258].rearrange("p (h d) -> p h d", d=129), ALU.add, ["CNS"], [P(b), "CNS"])
        hb = FS[3]
        for b in range(2):
            nd = PB[6 + b][:nt, 0:258].rearrange("p (h d) -> p h d", d=129)
            tt(SM[:nt, O + 28 + 2 * b:O + 30 + 2 * b], nd[:, :, 128], SM[:nt, CLO + 2 * b:CLO + 2 + 2 * b], ALU.max,
               ["wc"], [P(6 + b), "dmax"])
            stt(SM[:nt, O + 28 + 2 * b:O + 30 + 2 * b], nd[:, :, 128], -1.0, SM[:nt, O + 28 + 2 * b:O + 30 + 2 * b],
                ALU.mult, ALU.max, ["dmax"], [P(6 + b), "dmax"])
            recip(SM[:nt, O + 32 + 2 * b:O + 34 + 2 * b], SM[:nt, O + 28 + 2 * b:O + 30 + 2 * b], ["dmax"], ["rdm"])
            tt(hb[:nt, 256 * b:256 * (b + 1)].rearrange("p (h d) -> p h d", d=128), nd[:, :, 0:128],
               SM[:nt, O + 32 + 2 * b:O + 34 + 2 * b].unsqueeze(2).to_broadcast([nt, 2, 128]), ALU.mult,
               ["rdm"], [P(6 + b), "FS3"])
        sq = FS[4]
        act(sq[:nt, 0:512], hb[:nt, 0:512], AF.Square, ["FS3"], ["FS4"])
        red(SM[:nt, O + 36:O + 40], sq[:nt, 0:512].rearrange("p (h d) -> p h d", d=128), ALU.add, ["FS4"], ["hss"])
        act(SM[:nt, O + 40:O + 44], SM[:nt, O + 36:O + 40], AF.Ln, ["hss"], ["hss1"], bias=EPS, scale=1.0 / 128)
        act(SM[:nt, O + 44:O + 48], SM[:nt, O + 40:O + 44], AF.Exp, ["hss1"], ["hrstd"], scale=-0.5)
        hb3 = hb[:nt, 0:512].rearrange("p (h d) -> p h d", d=128)
        tt(hb3, hb3, SM[:nt, O + 44:O + 48].unsqueeze(2).to_broadcast([nt, 4, 128]), ALU.mult, ["hrstd", "FS3"], ["FS3"])
        tt(hb[:nt, 0:512], hb[:nt, 0:512], GH[:nt, :], ALU.mult, ["FS3", "GH"], ["FS3"])
        tt(hb[:nt, 0:512], hb[:nt, 0:512], sgo[:nt, 0:512], ALU.mult, ["FS3", "FS0"], ["FS3"])
        pt = pbf(2).rearrange("p (k t) -> p k t", t=128)
        for b in range(4):
            tr(pt[:nt, b, :], ct3[:, b, :nt], IDB[:, :], ["BS0", "IDB"], [P(2)])
        t2 = FS[4]
        tt(t2[:nt, 0:512].rearrange("p (k t) -> p k t", t=128), pt[:nt, 0:4, :],
           SKP[:nt, :].rearrange("p (k t) -> p k t", t=128), ALU.mult, ["SKP"], [P(2), "FS4"])
        tt(hb[:nt, 0:512], hb[:nt, 0:512], t2[:nt, 0:512], ALU.add, ["FS3", "FS4"], ["FS3"])
        mxb = BS[5]
        tt(mxb[:nt, :], hb[:nt, 0:512], szm[:nt, 0:512], ALU.mult, ["FS3", "FS1"], ["BS5"])
        for p in range(4):
            tr(pt[:, p, :nt], mxb[:nt, p * 128:(p + 1) * 128], IDB[:nt, :nt], ["BS5", "IDB"], [P(2)])
        mxt = BS[1]
        mxt3 = mxt[:, :].rearrange("p (k t) -> p k t", t=128)
        cp(mxt3[:, :, :nt], pt[:, 0:4, :nt], (), [P(2), "BS1"])
        for hf in range(2):
            for kc in range(4):
                mm(PB[hf][:nt, :], mxt3[:, kc, :nt], WOUT[:, kc, hf * 512:(hf + 1) * 512], kc == 0, kc == 3,
                   ["BS1", "WOUT"], [P(hf)])
            tt(YT[:nt, hf * 512:(hf + 1) * 512], PB[hf][:nt, :], YT[:nt, hf * 512:(hf + 1) * 512], ALU.add,
               ["YT"], [P(hf), "YT"])
        for i, (dst, key) in enumerate(zip(out_dsts, out_keys)):
            dma("pool", dst, YT[:nt, :], ["YT"], [key] if key else (), "xo%d" % i)

    def state_out(Cd, nd_, md, cvd, chs):
        dma("pool", Cd.rearrange("h k v -> k h v"), CNS[:, :, 0:128], ["CNS"], (), chs + "C")
        dma("pool", nd_.rearrange("h k -> k h"), CNS[:, :, 128], ["CNS"], (), chs + "n", nonc=True)
        tt(SG[0:4, 8:9], SG[0:4, 0:1], SG[0:4, 1:2], ALU.subtract, ["MU", "NBC"], ["MOUT"])
        dma("pool", md.rearrange("(h o) -> h o", o=1), SG[0:4, 8:9], ["MOUT"], (), chs + "m", nonc=True)
        for j in range(3):
            dma("pool", cvd[j].rearrange("(b f) -> f b", f=128), UEXT[:, :, j], ["UEXT"], (), chs + "v%d" % j, nonc=True)

    for l in range(DEPTH):
        load_layer_params(l)
        load_weights_A(l)
        xin_p = XPs[l % 2]
        xout_p = XPs[(l + 1) % 2]
        xin_s = XSs[l % 2]
        xout_s = XSs[(l + 1) % 2]
        barrier(ALIAS)
        mset(QTB[1][:, :, :, :], 0.0, [("QTB", 1)])
        prev = None
        for ti in range(NT + 1):
            if ti == 0:
                nt, pos0 = 16, 0
                xsrc = meta if l == 0 else xin_p[0:16, :]
                blocks = [(0, 16, 0)]
                cm = False
            else:
                fi = ti - 1
                nt, pos0 = 128, 16 + 128 * fi
                xsrc = xp[128 * fi:128 * fi + 128, :] if l == 0 else xin_p[pos0:pos0 + 128, :]
                blocks = [(0, 16, 0)] + [(16 + 128 * j, 128, 1 + j) for j in range(fi + 1)]
                cm = True
            ctx = passA_front(l, nt, xsrc, ("XP", l % 2, ti), pos0, blocks, ti,
                              pk[l, pos0:pos0 + nt, :], pv[l, pos0:pos0 + nt, :], pki[l, pos0:pos0 + nt, :],
                              YAp[pos0:pos0 + nt, :], ("YAp", ti), cfg.ksel_p, cm)
            if prev is not None:
                passA_back(prev)
            prev = ctx
        passA_back(prev)
        for b in range(NSB):
            for j in range(NCB):
                kst = FS[j % 2]
                vst = FS[2 + (j % 2)]
                kkst = "FS%d" % (j % 2)
                kvst = "FS%d" % (2 + j % 2)
                dma("sp", kst[:, 0:512], ck[l, b, 128 * j:128 * (j + 1), :], (), [kkst], "ks%d" % (j % 2))
                dma("sp", vst[:, 0:512], cv[l, b, 128 * j:128 * (j + 1), :], (), [kvst], "vs%d" % (j % 2))
                pf = PB[5][:, :].rearrange("p (k t) -> p k t", t=128)
                for p in range(4):
                    tr(pf[:, p, :], kst[:, p * 128:(p + 1) * 128], IDF[:, :], [kkst, "IDF"], [P(5)])
                cp(KT[:, :, 16 + 128 * j:16 + 128 * (j + 1)], pf[:, :, :], (), [P(5), ("KT", 1 + j)], eng="act")
                cp(VA[:, 1 + j, :, 0:64], vst[:, 0:512].rearrange("p (h d) -> p h d", d=64), [kvst], [("V", 1 + j)])
            for g in range(0, NCB, 4):
                ng = min(4, NCB - g)
                kis = FS[4 + (g // 4) % 2]
                kkey = "FS%d" % (4 + (g // 4) % 2)
                kdkey = "BS%d" % ((g // 4) % 2)
                dma("sp", kis[:, 0:64 * ng].rearrange("p (j d) -> p j d", d=64),
                    cki[l, b, 128 * g:128 * (g + ng), :].rearrange("(j p) d -> p j d", p=128), (), [kkey],
                    "kc%d" % ((g // 4) % 2))
                kd = BS[(g // 4) % 2]
                kd4 = kd[:, 0:128 * ng].rearrange("p (j e d) -> p j e d", e=2, d=64)
                k3 = kis[:, 0:64 * ng].rearrange("p (j d) -> p j d", d=64)
                cp(kd4[:, :, 0, :], k3, [kkey], [kdkey])
                cp(kd4[:, :, 1, :], k3, [kkey], [kdkey])
                pt = pbf(2).rearrange("p (k t) -> p k t", t=128)
                for jj in range(ng):
                    tr(pt[:, jj, :], kd[:, 128 * jj:128 * (jj + 1)], IDB[:, :], [kdkey, "IDB"], [P(2)])
                cp(KI2[:, 16 + 128 * g:16 + 128 * (g + ng)], pbf(2)[:, 0:128 * ng], (), [P(2)] + [("KI", 1 + g + q) for q in range(ng)], eng="act")
            xsrc = xs[b] if l == 0 else xin_s[b]
            pos0 = 16 + PAST
            blocks = [(0, 16, 0)] + [(16 + 128 * j, 128, 1 + j) for j in range(NCB)] + [(pos0, 32, NSLOT - 1)]
            ctx = passA_front(l, 32, xsrc, ("XS", l % 2, b), pos0, blocks, NSLOT - 1,
                              sk[l, b], sv[l, b], ski[l, b], YAs[b], ("YAs", b), cfg.ksel_s, False)
            passA_back(ctx)
        barrier(ALIAS)
        load_weights_B(l)
        mset(VM[:, :, 128:129], 1.0, ["VM"])
        mset(CNS[:, :, :], 0.0, ["CNS"])
        mset(UEXT[:, :, 0:3], 0.0, ["UEXT"])
        mset(SG[0:4, 0:2], 0.0, ["MU", "NBC"])
        last = (l == DEPTH - 1)
        for ti in range(NT + 1):
            if ti == 0:
                nt, pos0 = 16, 0
                xsrc = meta if l == 0 else xin_p[0:16, :]
                dsts, keys = ([], []) if last else ([xout_p[0:16, :]], [("XP", (l + 1) % 2, 0)])
            else:
                fi = ti - 1
                nt, pos0 = 128, 16 + 128 * fi
                xsrc = xp[128 * fi:128 * fi + 128, :] if l == 0 else xin_p[pos0:pos0 + 128, :]
                if last:
                    dsts, keys = [yp[128 * fi:128 * fi + 128, :]], [None]
                else:
                    dsts, keys = [xout_p[pos0:pos0 + 128, :]], [("XP", (l + 1) % 2, ti)]
            passB(l, nt, xsrc, ("XP", l % 2, ti), YAp[pos0:pos0 + nt, :], ("YAp", ti), dsts, keys)
        state_out(pC[l], pn[l], pm[l], pconv[l], "p")
        for b in range(NSB):
            dma("sp", CNS[:, :, 0:128], stC[l, b].rearrange("h k v -> k h v"), (), ["CNS"], "si0")
            dma("sp", CNS[:, :, 128], stn[l, b].rearrange("h k -> k h"), (), ["CNS"], "si1", nonc=True)
            dma("sp", SG[0:4, 0:1], stm[l, b].rearrange("(h o) -> h o", o=1), (), ["MU"], "si2", nonc=True)
            for j in range(3):
                dma("sp", UEXT[:, :, j], stconv[l, b, j].rearrange("(b f) -> f b", f=128), (), ["UEXT"], "si3%d" % j, nonc=True)
            mset(SG[0:4, 1:2], 0.0, ["NBC"])
            xsrc = xs[b] if l == 0 else xin_s[b]
            if last:
                dsts, keys = [ys[b]], [None]
            else:
                dsts, keys = [xout_s[b]], [("XS", (l + 1) % 2, b)]
            passB(l, 32, xsrc, ("XS", l % 2, b), YAs[b], ("YAs", b), dsts, keys)
            state_out(sC[l, b], sn[l, b], sm[l, b], sconv[l, b], "s")

    allch = list(S.chan_last.keys())
    S.add("sp", None, [], [], None)
    fin = S.ops[-1]
    fin.raw = set(S.chan_last[c] for c in allch)

    S.finalize()
    engs = ["pe", "act", "dve", "pool", "sp"]
    esem = {}
    for e in engs:
        n = (S.eng_cnt.get(e, 0) + SEM_CH - 1) // SEM_CH
        esem[e] = [es.enter_context(nc.semaphore("s_%s_%d" % (e, i))) for i in range(max(n, 1))]
    csem = {}
    for c, n in S.chan_cnt.items():
        k = (16 * n + SEM_CH - 1) // SEM_CH
        csem[c] = [es.enter_context(nc.semaphore("c_%s_%d" % (c, i))) for i in range(max(k, 1))]
    DCH = SEM_CH // 16 * 16

    def sem_of(t, v):
        if t[0] == "e":
            return esem[t[1]][(v - 1) // SEM_CH], (v - 1) % SEM_CH + 1
        return csem[t[1]][(v - 1) // DCH], (v - 1) % DCH + 1

    by_eng = {e: [] for e in engs}
    for op in S.ops:
        by_eng[op.eng].append(op)
    block = es.enter_context(nc.Block())

    def emit(ename, h):
        for op in by_eng[ename]:
            for (t, v) in op.waits:
                sem, val = sem_of(t, v)
                h.wait_ge(sem, val)
            if op.fn is None:
                continue
            ins = op.fn(h)
            if op.chan is not None:
                sem, val = sem_of(("c", op.chan), 16 * op.chan_n)
                ins.then_inc(sem, 16)
            elif op.inc:
                sem, val = sem_of(("e", ename), op.incval)
                ins.then_inc(sem, 1)

    @block.tensor
    def _(h):
        emit("pe", h)

    @block.scalar
    def _(h):
        emit("act", h)

    @block.vector
    def _(h):
        emit("dve", h)

    @block.gpsimd
    def _(h):
        emit("pool", h)

    @block.sync
    def _(h):
        emit("sp", h)

    es.close()
    return nc, len(S.ops)


def const_tables(cfg):
    half = 32
    freqs = (np.float32(10000.0) ** (-np.arange(half, dtype=np.float32) / np.float32(half))).astype(np.float32)
    pos = np.arange(cfg.kc, dtype=np.float32)
    ang = (pos[:, None] * freqs[None, :]).astype(np.float32)
    rope = np.concatenate([np.cos(ang.astype(np.float64)), np.sin(ang.astype(np.float64))], axis=1).astype(np.float32)
    tri = np.triu(np.ones((128, 128), np.float32))
    return {
        "c_ident": np.eye(128, dtype=np.float32),
        "c_tri": tri,
        "c_rope": rope,
        "c_i4": np.eye(4, dtype=np.float32),
        "c_ones4": np.ones((4, 128), np.float32),
    }


WNAMES = ["meta", "norm_g", "w_in", "q_norm_g", "k_norm_g", "conv_w", "conv_b", "wq_m", "wk_m",
          "b_igate", "b_fgate", "head_norm_g", "skip", "w_out"]


def run(cfg, inputs, ncores):
    nc, nops = build(cfg)
    consts = const_tables(cfg)
    f = lambda a: np.ascontiguousarray(np.asarray(a, dtype=np.float32))
    nsb = cfg.nsb
    in_maps = []
    for c in range(ncores):
        m = {
            "xp": f(inputs["x_prompt"][c]),
            "xs": f(inputs["x_sample"][nsb * c:nsb * (c + 1)]),
            "ck": f(inputs["cache_k"][:, nsb * c:nsb * (c + 1)]).reshape(cfg.depth, nsb, cfg.past, 512),
            "cv": f(inputs["cache_v"][:, nsb * c:nsb * (c + 1)]).reshape(cfg.depth, nsb, cfg.past, 512),
            "cki": f(inputs["cache_kidx"][:, nsb * c:nsb * (c + 1)]),
            "stC": f(inputs["state_C"][:, nsb * c:nsb * (c + 1)]),
            "stn": f(inputs["state_n"][:, nsb * c:nsb * (c + 1)]),
            "stm": f(inputs["state_m"][:, nsb * c:nsb * (c + 1)]),
            "stconv": f(inputs["state_conv"][:, nsb * c:nsb * (c + 1)]),
        }
        for wn in WNAMES:
            m[wn] = f(inputs[wn])
        m.update(consts)
        in_maps.append(m)
    res = run_bass_kernel_spmd(nc, in_maps, core_ids=list(range(ncores)))
    R = res.results
    D = cfg.depth

    def cat_b(name):
        return np.stack([R[c][name] for c in range(ncores)], axis=1)

    def cat_s(name):
        return np.concatenate([R[c][name] for c in range(ncores)], axis=1)

    y_prompt = np.stack([R[c]["yp"] for c in range(ncores)], axis=0)
    y_sample = np.concatenate([R[c]["ys"] for c in range(ncores)], axis=0)
    pk = cat_b("pk").reshape(D, ncores, cfg.npos, 8, 64)
    pv = cat_b("pv").reshape(D, ncores, cfg.npos, 8, 64)
    pki = cat_b("pki")
    sk = cat_s("sk").reshape(D, ncores * nsb, 32, 8, 64)
    sv = cat_s("sv").reshape(D, ncores * nsb, 32, 8, 64)
    return (y_prompt, y_sample, pk, pv, pki, cat_b("pC"), cat_b("pn"), cat_b("pm"), cat_b("pconv"),
            sk, sv, cat_s("ski"), cat_s("sC"), cat_s("sn"), cat_s("sm"), cat_s("sconv"))


def kernel(**inputs):
    cfg = Cfg()
    outs = run(cfg, inputs, 8)
    return tuple(np.ascontiguousarray(o.astype(np.float32)) for o in outs)
```

```python
import numpy as np
from contextlib import ExitStack
import concourse.bass as bass
import concourse.mybir as mybir
from concourse.bass_utils import run_bass_kernel_spmd

F32 = mybir.dt.float32
BF16 = mybir.dt.bfloat16
AF = mybir.ActivationFunctionType
ALU = mybir.AluOpType
AX = mybir.AxisListType

EPS = 1e-6
NEG = -30000.0
SEM_CH = 24000


class Cfg:
    def __init__(self, ntiles=32, past=4096, depth=4, ksel_p=256, ksel_s=256, nsb=4, nbis=18, brk=8.0):
        self.ntiles, self.past, self.depth = ntiles, past, depth
        self.ksel_p, self.ksel_s, self.nsb, self.nbis, self.brk = ksel_p, ksel_s, nsb, nbis, brk
        self.npos = 16 + 128 * ntiles
        self.ncb = past // 128
        self.kc = 16 + past + 32
        self.cw = max(self.npos, self.kc)
        self.nslot = 2 + max(ntiles, self.ncb)


class Op:
    __slots__ = ("eng", "fn", "raw", "oth", "chan", "chan_n", "inc", "waits", "incval")


class Sched:
    def __init__(self):
        self.ops = []
        self.last_w = {}
        self.readers = {}
        self.chan_last = {}
        self.chan_cnt = {}

    def add(self, eng, fn, r=(), w=(), chan=None):
        op = Op()
        op.eng, op.fn, op.chan = eng, fn, chan
        idx = len(self.ops)
        raw, oth = set(), set()
        for k in r:
            lw = self.last_w.get(k)
            if lw is not None:
                raw.add(lw)
        for k in w:
            lw = self.last_w.get(k)
            if lw is not None:
                oth.add(lw)
            for rd in self.readers.get(k, {}).values():
                oth.add(rd)
        if chan is not None:
            pl = self.chan_last.get(chan)
            if pl is not None:
                raw.add(pl)
            self.chan_last[chan] = idx
            self.chan_cnt[chan] = self.chan_cnt.get(chan, 0) + 1
            op.chan_n = self.chan_cnt[chan]
        op.raw, op.oth = raw, oth
        rk = ("c", chan) if chan is not None else ("e", eng)
        for k in r:
            self.readers.setdefault(k, {})[rk] = idx
        for k in w:
            self.last_w[k] = idx
            self.readers[k] = {}
        self.ops.append(op)
        return idx

    def finalize(self):
        ops = self.ops
        seq = {}
        for i, op in enumerate(ops):
            if op.chan is None:
                seq[op.eng] = seq.get(op.eng, 0) + 1
                op.incval = seq[op.eng]
        waited = {}
        needed = set()
        for op in ops:
            deps = {}
            for j in op.raw:
                deps[j] = True
            for j in op.oth:
                deps.setdefault(j, False)
            tg = {}
            for j, is_raw in deps.items():
                o = ops[j]
                if o.chan is None and o.eng == op.eng and not is_raw:
                    continue
                if o.chan is not None:
                    t, v = ("c", o.chan), 16 * o.chan_n
                else:
                    t, v = ("e", o.eng), o.incval
                if v > tg.get(t, (0, None))[0]:
                    tg[t] = (v, j)
            wl = []
            wd = waited.setdefault(op.eng, {})
            for t, (v, j) in tg.items():
                if v > wd.get(t, 0):
                    wd[t] = v
                    wl.append((t, j))
                    if t[0] == "e":
                        needed.add(j)
            op.waits = wl
        cnt = {}
        for i, op in enumerate(ops):
            op.inc = False
            if op.chan is None and i in needed:
                cnt[op.eng] = cnt.get(op.eng, 0) + 1
                op.inc = True
                op.incval = cnt[op.eng]
        self.eng_cnt = cnt
        for op in ops:
            wl = []
            for (t, j) in op.waits:
                o = ops[j]
                wl.append((t, 16 * o.chan_n if t[0] == "c" else o.incval))
            op.waits = wl


def build(cfg):
    NT, PAST, DEPTH = cfg.ntiles, cfg.past, cfg.depth
    NPOS, NCB, KCOLS, CW, NSLOT, NSB = cfg.npos, cfg.ncb, cfg.kc, cfg.cw, cfg.nslot, cfg.nsb
    nc = bass.Bass("TRN2", target_bir_lowering=False)
    S = Sched()

    def din(name, shape):
        return nc.dram_tensor(name, list(shape), F32, kind="ExternalInput").ap()

    def dout(name, shape):
        return nc.dram_tensor(name, list(shape), F32, kind="ExternalOutput").ap()

    def dscr(name, shape):
        return nc.dram_tensor(name, list(shape), F32, kind="Internal").ap()

    xp = din("xp", [128 * NT, 1024])
    xs = din("xs", [NSB, 32, 1024])
    ck = din("ck", [DEPTH, NSB, PAST, 512])
    cv = din("cv", [DEPTH, NSB, PAST, 512])
    cki = din("cki", [DEPTH, NSB, PAST, 64])
    stC = din("stC", [DEPTH, NSB, 4, 128, 128])
    stn = din("stn", [DEPTH, NSB, 4, 128])
    stm = din("stm", [DEPTH, NSB, 4])
    stconv = din("stconv", [DEPTH, NSB, 3, 512])
    meta = din("meta", [16, 1024])
    norm_g = din("norm_g", [DEPTH, 1024])
    w_in = din("w_in", [DEPTH, 1024, 4688])
    q_norm_g = din("q_norm_g", [DEPTH, 64])
    k_norm_g = din("k_norm_g", [DEPTH, 64])
    conv_w = din("conv_w", [DEPTH, 4, 512])
    conv_b = din("conv_b", [DEPTH, 512])
    wq_m = din("wq_m", [DEPTH, 4, 128, 128])
    wk_m = din("wk_m", [DEPTH, 4, 128, 128])
    b_igate = din("b_igate", [DEPTH, 4])
    b_fgate = din("b_fgate", [DEPTH, 4])
    head_norm_g = din("head_norm_g", [DEPTH, 512])
    skip = din("skip", [DEPTH, 512])
    w_out = din("w_out", [DEPTH, 1024, 1024])
    c_ident = din("c_ident", [128, 128])
    c_tri = din("c_tri", [128, 128])
    c_rope = din("c_rope", [KCOLS, 64])
    c_i4 = din("c_i4", [4, 4])
    c_ones4 = din("c_ones4", [4, 128])

    yp = dout("yp", [128 * NT, 1024])
    ys = dout("ys", [NSB, 32, 1024])
    pk = dout("pk", [DEPTH, NPOS, 512])
    pv = dout("pv", [DEPTH, NPOS, 512])
    pki = dout("pki", [DEPTH, NPOS, 64])
    pC = dout("pC", [DEPTH, 4, 128, 128])
    pn = dout("pn", [DEPTH, 4, 128])
    pm = dout("pm", [DEPTH, 4])
    pconv = dout("pconv", [DEPTH, 3, 512])
    sk = dout("sk", [DEPTH, NSB, 32, 512])
    sv = dout("sv", [DEPTH, NSB, 32, 512])
    ski = dout("ski", [DEPTH, NSB, 32, 64])
    sC = dout("sC", [DEPTH, NSB, 4, 128, 128])
    sn = dout("sn", [DEPTH, NSB, 4, 128])
    sm = dout("sm", [DEPTH, NSB, 4])
    sconv = dout("sconv", [DEPTH, NSB, 3, 512])

    XPs = [dscr("XP0", [NPOS, 1024]), dscr("XP1", [NPOS, 1024])]
    XSs = [dscr("XS0", [NSB, 32, 1024]), dscr("XS1", [NSB, 32, 1024])]
    YAp = dscr("YAp", [NPOS, 1024])
    YAs = dscr("YAs", [NSB, 32, 1024])

    es = ExitStack()

    def sb(name, shape, dt=F32):
        return es.enter_context(nc.sbuf_tensor(name, list(shape), dt))

    WBUF = sb("WBUF", [128, 8, 2632], BF16)
    WOUT = sb("WOUT", [128, 4, 1024], BF16)
    KT = sb("KT", [128, 4, CW], BF16)
    VA = sb("VA", [128, NSLOT, 8, 65], BF16)
    KI2 = sb("KI2", [128, CW], BF16)
    SC = sb("SC", [128, CW], F32)
    MB = [sb("MB0", [128, CW], BF16), sb("MB1", [128, max(CW, 4128)], BF16)]
    XT = [sb("XT0", [128, 1024]), sb("XT1", [128, 1024])]
    YT = sb("YT", [128, 1024])
    CS = [sb("CS0", [128, 64]), sb("CS1", [128, 64])]
    HN = sb("HN", [128, 1024], BF16)
    HNT = sb("HNT", [128, 8, 128], BF16)
    FS = [sb("FS%d" % i, [128, 528]) for i in range(9)]
    BS = [sb("BS%d" % i, [128, 512], BF16) for i in range(6)]
    QTB = [sb("QTB0", [128, 4, 2, 128], BF16), sb("QTB1", [128, 4, 2, 128], BF16)]
    QIT = sb("QIT", [128, 4, 128], BF16)
    SM = sb("SM", [128, 128])
    SR = sb("SR", [4, 3, 128])
    SG = sb("SG", [4, 16])
    IDF = sb("IDF", [128, 128])
    IDB = sb("IDB", [128, 128], BF16)
    I2 = sb("I2", [128, 2, 128], BF16)
    TRI = sb("TRI", [128, 128])
    I4 = sb("I4", [4, 4])
    DD = sb("DD", [4, 4])
    ONES4 = sb("ONES4", [4, 128])
    GT = sb("GT", [128, 8])
    GQ = sb("GQ", [128, 64])
    GK = sb("GK", [128, 64])
    GBI = sb("GBI", [128, 4])
    GBF = sb("GBF", [128, 4])
    CVW = sb("CVW", [128, 4, 4])
    CVB = sb("CVB", [128, 4])
    MB1f = MB[1][:, 0:4128].bitcast(F32)
    GH = MB1f[:, 0:512]
    SKP = MB1f[:, 512:1024]
    UEXT = MB1f[:, 1024:1548].rearrange("p (b t) -> p b t", t=131)
    CNS = MB1f[:, 1548:2064].rearrange("p (h d) -> p h d", d=129)
    WQK = QTB[1][:, :, :, :].rearrange("p a b c -> p (a b c)").rearrange("p (q h e) -> p q h e", q=2, h=4)
    FS8b = FS[8][:, 0:516].bitcast(BF16)
    CNB = FS8b[:, 0:516].rearrange("p (h d) -> p h d", d=129)
    VM = FS8b[:, 516:1032].rearrange("p (h d) -> p h d", d=129)
    ALIAS = [("MB", 1), "GH", "SKP", "UEXT", "CNS", ("QTB", 1), "WQK", "FS8", "CNB", "VM"]
    PB = [es.enter_context(nc.psum_tensor("PB%d" % i, [128, 512], F32)) for i in range(8)]

    def pbf(i):
        return PB[i][:, :].bitcast(BF16)

    def P(i):
        return ("P", i)

    def dma(q, out, in_, r, w, chan, nonc=False):
        if nonc:
            S.add(q, lambda e: e.dma_start(out=out, in_=in_, allow_slow_non_contiguous=True), r, w, chan)
        else:
            S.add(q, lambda e: e.dma_start(out=out, in_=in_), r, w, chan)

    def mm(out, lhsT, rhs, start, stop, r, w, skip=False):
        S.add("pe", lambda e: e.matmul(out, lhsT=lhsT, rhs=rhs, start=start, stop=stop,
                                       skip_group_check=skip), r, w)

    def tr(out, in_, ident, r, w):
        S.add("pe", lambda e: e.transpose(out=out, in_=in_, identity=ident), r, w)

    def act(out, in_, func, r, w, bias=0.0, scale=1.0, accum=None):
        if accum is None:
            S.add("act", lambda e: e.activation(out=out, in_=in_, func=func, bias=bias, scale=scale), r, w)
        else:
            S.add("act", lambda e: e.activation(out=out, in_=in_, func=func, bias=bias, scale=scale,
                                                accum_out=accum), r, w)

    def ts(out, in0, s1, s2, op0, op1, r, w, eng="dve", accum=None):
        if op1 is None:
            S.add(eng, lambda e: e.tensor_scalar(out=out, in0=in0, scalar1=s1, scalar2=None, op0=op0), r, w)
        elif accum is None:
            S.add(eng, lambda e: e.tensor_scalar(out=out, in0=in0, scalar1=s1, scalar2=s2, op0=op0, op1=op1), r, w)
        else:
            S.add(eng, lambda e: e.tensor_scalar(out=out, in0=in0, scalar1=s1, scalar2=s2, op0=op0, op1=op1,
                                                 accum_out=accum), r, w)

    def tt(out, in0, in1, op, r, w, eng="dve"):
        S.add(eng, lambda e: e.tensor_tensor(out=out, in0=in0, in1=in1, op=op), r, w)

    def stt(out, in0, scalar, in1, op0, op1, r, w):
        S.add("dve", lambda e: e.scalar_tensor_tensor(out=out, in0=in0, scalar=scalar, in1=in1, op0=op0, op1=op1), r, w)

    def cp(out, in_, r, w, eng="dve"):
        if eng == "act":
            S.add(eng, lambda e: e.activation(out=out, in_=in_, func=AF.Copy), r, w)
        else:
            S.add(eng, lambda e: e.tensor_copy(out=out, in_=in_), r, w)

    def mset(ap, val, w, eng="dve"):
        S.add(eng, lambda e: e.memset(ap, val), (), w)

    def red(out, in_, op, r, w):
        S.add("dve", lambda e: e.tensor_reduce(out=out, in_=in_, axis=AX.X, op=op), r, w)

    def recip(out, in_, r, w):
        S.add("dve", lambda e: e.reciprocal(out=out, in_=in_), r, w)

    dma("sp", IDF[:, :], c_ident, (), ["IDF"], "c0")
    dma("sp", TRI[:, :], c_tri, (), ["TRI"], "c1")
    dma("sp", I4[:, :], c_i4, (), ["I4"], "c2")
    dma("sp", ONES4[:, :], c_ones4, (), ["ONES4"], "c3")
    cp(IDB[:, :], IDF[:, :], ["IDF"], ["IDB"])
    cp(I2[:, 0, :], IDF[:, :], ["IDF"], ["I2"])
    cp(I2[:, 1, :], IDF[:, :], ["IDF"], ["I2"])
    mset(QTB[0][:, :, :, :], 0.0, [("QTB", 0)])
    mset(VA[:, :, :, 64:65], 1.0, [("V", sl) for sl in range(NSLOT)])

    def barrier(keys):
        S.add("dve", lambda e: e.memset(SM[:, 127:128], 0.0), (), list(keys))

    def load_layer_params(l):
        dma("sp", GT[:, :], norm_g[l].rearrange("(k p) -> p k", p=128), (), ["GT"], "g0", nonc=True)
        dma("sp", GQ[:, :], q_norm_g[l].partition_broadcast(128), (), ["GQ"], "g1")
        dma("sp", GK[:, :], k_norm_g[l].partition_broadcast(128), (), ["GK"], "g2")
        dma("sp", GBI[:, :], b_igate[l].partition_broadcast(128), (), ["GBI"], "g3")
        dma("sp", GBF[:, :], b_fgate[l].partition_broadcast(128), (), ["GBF"], "g4")
        for j in range(4):
            dma("sp", CVW[:, :, j], conv_w[l, j].rearrange("(b f) -> f b", f=128), (), ["CVW"], "g7%d" % j, nonc=True)
        dma("sp", CVB[:, :], conv_b[l].rearrange("(b f) -> f b", f=128), (), ["CVB"], "g8", nonc=True)

    def load_weights_A(l):
        dma("pool", WBUF[:, :, 0:2632], w_in[l][:, 0:2632].rearrange("(k p) n -> p k n", p=128), (), ["WBUF"], "w0")
        dma("pool", WOUT[:, :, :], w_out[l][0:512, :].rearrange("(k p) n -> p k n", p=128), (), ["WOUT"], "w1")

    def load_weights_B(l):
        dma("pool", WBUF[:, :, 0:2056], w_in[l][:, 2632:4688].rearrange("(k p) n -> p k n", p=128), (), ["WBUF"], "w0")
        dma("pool", WOUT[:, :, :], w_out[l][512:1024, :].rearrange("(k p) n -> p k n", p=128), (), ["WOUT"], "w1")
        dma("pool", WQK[:, 0, :, :], wq_m[l].rearrange("h d e -> d h e"), (), ["WQK"], "w2")
        dma("pool", WQK[:, 1, :, :], wk_m[l].rearrange("h d e -> d h e"), (), ["WQK"], "w3")
        dma("sp", GH, head_norm_g[l].partition_broadcast(128), (), ["GH"], "g5")
        dma("sp", SKP, skip[l].partition_broadcast(128), (), ["SKP"], "g6")

    xslot = [0]

    def load_norm(nt, xsrc, xkey, pos0=None):
        s = xslot[0]
        xslot[0] ^= 1
        xt = XT[s]
        dma("sp", xt[:nt, :], xsrc, [xkey], [("XT", s)], "xt%d" % s)
        if pos0 is not None:
            dma("sp", CS[s][:nt, :], c_rope[pos0:pos0 + nt, :], (), [("CS", s)], "cs%d" % s)
        act(HN[:nt, :], xt[:nt, :], AF.Square, [("XT", s)], ["HN", "ss"], accum=SM[:nt, 0:1])
        act(SM[:nt, 1:2], SM[:nt, 0:1], AF.Ln, ["ss"], ["ss1"], bias=EPS, scale=1.0 / 1024)
        act(SM[:nt, 2:3], SM[:nt, 1:2], AF.Exp, ["ss1"], ["rstd"], scale=-0.5)
        act(HN[:nt, :], xt[:nt, :], AF.Copy, [("XT", s), "rstd"], ["HN"], scale=SM[:nt, 2:3])
        pt = pbf(2).rearrange("p (k t) -> p k t", t=128)
        for kc in range(8):
            tr(pt[:, kc, :nt], HN[:nt, kc * 128:(kc + 1) * 128], IDB[:nt, :nt], ["HN", "IDB"], [P(2)])
        tt(HNT[:, :, :nt], pt[:, :, :nt], GT[:, :].unsqueeze(2).to_broadcast([128, 8, nt]), ALU.mult,
           ["GT"], [P(2), "HNT"])
        return s

    def inproj(nt, c0, n, bank):
        for kc in range(8):
            mm(PB[bank][:nt, 0:n], HNT[:, kc, :nt], WBUF[:, kc, c0:c0 + n], kc == 0, kc == 7,
               ["HNT", "WBUF"], [P(bank)])

    def rope(nt, s, x1, x2, o1, o2, nh, rkeys, wkey, tmp, tkey="FS2"):
        cosb = CS[s][:nt, 0:32].unsqueeze(1).to_broadcast([nt, nh, 32])
        sinb = CS[s][:nt, 32:64].unsqueeze(1).to_broadcast([nt, nh, 32])
        t1 = tmp[:nt, 0:nh * 32].rearrange("p (h d) -> p h d", d=32)
        t2 = tmp[:nt, 256:256 + nh * 32].rearrange("p (h d) -> p h d", d=32)
        rk = list(rkeys) + [("CS", s)]
        tt(t1, x1, cosb, ALU.mult, rk, [tkey])
        tt(t2, x2, sinb, ALU.mult, rk, [tkey])
        tt(o1, t1, t2, ALU.subtract, [tkey], [wkey])
        tt(t1, x2, cosb, ALU.mult, rk, [tkey])
        tt(t2, x1, sinb, ALU.mult, rk, [tkey])
        tt(o2, t1, t2, ALU.add, [tkey], [wkey])

    def qknorm(nt, bank, gtile, gkey, dst, dkey):
        ps3 = PB[bank][:nt, :].rearrange("p (h d) -> p h d", d=64)
        sq = FS[1]
        act(sq[:nt, 0:512], PB[bank][:nt, :], AF.Square, (), [P(bank), "FS1"])
        red(SM[:nt, 8:16], sq[:nt, 0:512].rearrange("p (h d) -> p h d", d=64), ALU.add, ["FS1"], ["qss"])
        act(SM[:nt, 16:24], SM[:nt, 8:16], AF.Ln, ["qss"], ["qss1"], bias=EPS, scale=1.0 / 64)
        act(SM[:nt, 24:32], SM[:nt, 16:24], AF.Exp, ["qss1"], ["qrstd"], scale=-0.5)
        d3 = dst[:nt, 0:512].rearrange("p (h d) -> p h d", d=64)
        tt(d3, ps3, SM[:nt, 24:32].unsqueeze(2).to_broadcast([nt, 8, 64]), ALU.mult, ["qrstd"], [P(bank), dkey])
        tt(d3, d3, gtile[:nt, :].unsqueeze(1).to_broadcast([nt, 8, 64]), ALU.mult, [dkey, gkey], [dkey])

    WSC = 0.125 * (8.0 ** -0.5)

    tcount = [0]

    def slot_cols(slot):
        if slot == 0:
            return (0, 16)
        if slot == NSLOT - 1:
            return (16 + PAST, 16 + PAST + 32)
        return (16 + 128 * (slot - 1), 16 + 128 * slot)

    def passA_front(l, nt, xsrc, xkey, pos0, blocks, slot_own, kdst, vdst, kidst, ya_dst, ya_key, ksel, chunkmask):
        par = tcount[0] % 2
        tcount[0] += 1
        col_own = pos0
        s = load_norm(nt, xsrc, xkey, pos0)
        pt = pbf(2).rearrange("p (k t) -> p k t", t=128)
        qtb = QTB[par]
        qkey = ("QTB", par)
        inproj(nt, 0, 512, 0)
        inproj(nt, 512, 512, 1)
        qn = FS[0]
        qknorm(nt, 0, GQ, "GQ", qn, "FS0")
        q3 = qn[:nt, 0:512].rearrange("p (h d) -> p h d", d=64)
        qr = BS[0]
        qr3 = qr[:nt, :].rearrange("p (h d) -> p h d", d=64)
        rope(nt, s, q3[:, :, 0:32], q3[:, :, 32:64], qr3[:, :, 0:32], qr3[:, :, 32:64], 8, ["FS0"], "BS0", FS[2])
        inproj(nt, 1024, 512, 0)
        for p in range(4):
            tr(pt[:, p, :nt], qr[:nt, p * 128:(p + 1) * 128], IDB[:nt, :nt], ["BS0", "IDB"], [P(2)])
        cp(qtb[0:64, :, 0, :nt], pt[0:64, 0:4, :nt], (), [P(2), qkey])
        cp(qtb[64:128, :, 1, :nt], pt[64:128, 0:4, :nt], (), [P(2), qkey])
        kn = FS[3]
        qknorm(nt, 1, GK, "GK", kn, "FS3")
        k3 = kn[:nt, 0:512].rearrange("p (h d) -> p h d", d=64)
        kr = FS[4]
        kr3 = kr[:nt, 0:512].rearrange("p (h d) -> p h d", d=64)
        rope(nt, s, k3[:, :, 0:32], k3[:, :, 32:64], kr3[:, :, 0:32], kr3[:, :, 32:64], 8, ["FS3"], "FS4", FS[2])
        inproj(nt, 1536, 512, 1)
        dma("pool", kdst, kr[:nt, 0:512], ["FS4"], (), "ko")
        kb16 = BS[1]
        cp(kb16[:nt, :], kr[:nt, 0:512], ["FS4"], ["BS1"])
        for p in range(4):
            tr(pt[:, p, :nt], kb16[:nt, p * 128:(p + 1) * 128], IDB[:nt, :nt], ["BS1", "IDB"], [P(2)])
        cp(KT[:, :, col_own:col_own + nt], pt[:, 0:4, :nt], (), [P(2), ("KT", slot_own)])
        vf = FS[5]
        cp(vf[:nt, 0:512], PB[0][:nt, :], (), [P(0), "FS5"], eng="act")
        inproj(nt, 2048, 512, 0)
        dma("pool", vdst, vf[:nt, 0:512], ["FS5"], (), "vo")
        cp(VA[:nt, slot_own, :, 0:64], vf[:nt, 0:512].rearrange("p (h d) -> p h d", d=64), ["FS5"], [("V", slot_own)])
        zi = 6 if par == 0 else 8
        zas = FS[zi]
        zkey = "FS%d" % zi
        act(zas[:nt, 0:512], PB[1][:nt, :], AF.Silu, (), [P(1), zkey])
        inproj(nt, 2560, 72, 1)
        qir = BS[0]
        qir3 = qir[:nt, :].rearrange("p (h d) -> p h d", d=64)
        cp(FS[0][:nt, 0:512], PB[0][:nt, :], (), [P(0), "FS0"], eng="act")
        qf3 = FS[0][:nt, 0:512].rearrange("p (h d) -> p h d", d=64)
        rope(nt, s, qf3[:, :, 0:32], qf3[:, :, 32:64], qir3[:, :, 0:32], qir3[:, :, 32:64], 8, ["FS0"], "BS0", FS[2])
        for p in range(4):
            tr(pt[:, p, :nt], qir[:nt, p * 128:(p + 1) * 128], IDB[:nt, :nt], ["BS0", "IDB"], [P(2)])
        cp(QIT[:, :, :nt], pt[:, 0:4, :nt], (), [P(2), "QIT"])
        kif = FS[7]
        cp(kif[:nt, 0:72], PB[1][:nt, 0:72], (), [P(1), "FS7"], eng="act")
        ts(SM[:nt, 32:40], kif[:nt, 64:72], WSC, None, ALU.mult, None, ["FS7"], ["wis"])
        ki1 = kif[:nt, 0:64].rearrange("p (h d) -> p h d", d=64)
        kio = kif[:nt, 128:192].rearrange("p (h d) -> p h d", d=64)
        rope(nt, s, ki1[:, :, 0:32], ki1[:, :, 32:64], kio[:, :, 0:32], kio[:, :, 32:64], 1, ["FS7"], "FS7", FS[2])
        dma("pool", kidst, kif[:nt, 128:192], ["FS7"], (), "kio")
        ki2 = BS[1]
        cp(ki2[:nt, 0:64], kif[:nt, 128:192], ["FS7"], ["BS1"])
        cp(ki2[:nt, 64:128], kif[:nt, 128:192], ["FS7"], ["BS1"])
        tr(pt[:, 4, :nt], ki2[:nt, 0:128], IDB[:nt, :nt], ["BS1", "IDB"], [P(2)])
        cp(KI2[:, col_own:col_own + nt], pt[:, 4, :nt], (), [P(2), ("KI", slot_own)])
        attend_front(nt, blocks, ksel, chunkmask, par)
        return dict(nt=nt, s=s, par=par, blocks=blocks, zas=zas, zkey=zkey, ya_dst=ya_dst, ya_key=ya_key)

    def passA_back(c):
        nt, s, par, zas, zkey = c["nt"], c["s"], c["par"], c["zas"], c["zkey"]
        pt = pbf(2).rearrange("p (k t) -> p k t", t=128)
        attend_back(nt, c["blocks"], par)
        at = FS[0]
        akey = "FS0"
        for b in range(2):
            ov = PB[6 + b][:nt, 0:260].rearrange("p (h d) -> p h d", d=65)
            recip(SM[:nt, 40 + 4 * b:44 + 4 * b], ov[:, :, 64], (), [P(6 + b), "rden"])
            tt(at[:nt, 256 * b:256 * (b + 1)].rearrange("p (h d) -> p h d", d=64), ov[:, :, 0:64],
               SM[:nt, 40 + 4 * b:44 + 4 * b].unsqueeze(2).to_broadcast([nt, 4, 64]), ALU.mult,
               ["rden"], [P(6 + b), akey])
        mxa = BS[2]
        tt(mxa[:nt, :], at[:nt, 0:512], zas[:nt, 0:512], ALU.mult, [akey, zkey], ["BS2"])
        for p in range(4):
            tr(pt[:, p, :nt], mxa[:nt, p * 128:(p + 1) * 128], IDB[:nt, :nt], ["BS2", "IDB"], [P(2)])
        mxt = BS[3]
        mxt3 = mxt[:, :].rearrange("p (k t) -> p k t", t=128)
        cp(mxt3[:, :, :nt], pt[:, 0:4, :nt], (), [P(2), "BS3"])
        for hf in range(2):
            for kc in range(4):
                mm(PB[hf][:nt, :], mxt3[:, kc, :nt], WOUT[:, kc, hf * 512:(hf + 1) * 512], kc == 0, kc == 3,
                   ["BS3", "WOUT"], [P(hf)])
            tt(YT[:nt, hf * 512:(hf + 1) * 512], PB[hf][:nt, :], XT[s][:nt, hf * 512:(hf + 1) * 512], ALU.add,
               [("XT", s)], [P(hf), "YT"])
        dma("pool", c["ya_dst"], YT[:nt, :], ["YT"], [c["ya_key"]], "yo")

    def attend_front(nt, blocks, ksel, chunkmask, par):
        L = sum(b[1] for b in blocks)
        assert blocks[0][0] == 0
        mb = MB[par]
        mkey = ("MB", par)
        ibanks = [3, 4, 5]
        nb = 0
        c0 = 0
        while c0 < L:
            n = min(512, L - c0)
            kikeys = [("KI", bl[2]) for bl in blocks if bl[0] < c0 + n and bl[0] + bl[1] > c0]
            for h in range(8):
                p, e = divmod(h, 2)
                bank = ibanks[nb % 3]
                ri = (1, 3)[nb % 2]
                rbuf = FS[ri]
                rkey = "FS%d" % ri
                nb += 1
                mm(PB[bank][:nt, 0:n], QIT[64 * e:64 * e + 64, p, :nt], KI2[64 * e:64 * e + 64, c0:c0 + n],
                   True, True, ["QIT"] + kikeys, [P(bank)])
                act(rbuf[:nt, 0:n], PB[bank][:nt, 0:n], AF.Relu, (), [P(bank), rkey])
                if h == 0:
                    ts(SC[:nt, c0:c0 + n], rbuf[:nt, 0:n], SM[:nt, 32:33], None, ALU.mult, None,
                       [rkey, "wis"], ["SC"])
                else:
                    stt(SC[:nt, c0:c0 + n], rbuf[:nt, 0:n], SM[:nt, 32 + h:33 + h], SC[:nt, c0:c0 + n],
                        ALU.mult, ALU.add, [rkey, "wis", "SC"], ["SC"])
            c0 += n
        if chunkmask:
            mset(SC[0:64, L - 64:L], -1.0e4, ["SC"])
        LO = cfg.brk
        if L > ksel:
            thr = SM[:nt, 48:49]
            cnt = SM[:nt, 49:50]
            tmp = SM[:nt, 50:51]
            nth = [SM[:nt, 51:52], SM[:nt, 52:53]]
            use_act = False
            st = LO
            if not use_act:
                mset(thr, 0.0, ["thr"])
                for it in range(cfg.nbis):
                    ts(mb[:nt, 0:L], SC[:nt, 0:L], thr, None, ALU.is_ge, ALU.add, ["SC", "thr"], [mkey, "cnt"], accum=cnt)
                    ts(tmp, cnt, ksel - 0.5, st, ALU.is_ge, ALU.mult, ["cnt"], ["btmp"])
                    if it < cfg.nbis - 1:
                        stt(thr, tmp, -st / 2, thr, ALU.add, ALU.add, ["btmp", "thr"], ["thr"])
                    else:
                        stt(thr, tmp, -st, thr, ALU.add, ALU.add, ["btmp", "thr"], ["thr"])
                    st /= 2
            else:
                mset(nth[0], 0.0, ["nth0"])
                cc = 2.0 * ksel - 1.0 - L
                for it in range(cfg.nbis):
                    a_, b_ = it % 2, (it + 1) % 2
                    act(mb[:nt, 0:L], SC[:nt, 0:L], AF.Sign, ["SC", "nth%d" % a_], [mkey, "cnt"], bias=nth[a_], accum=cnt)
                    act(tmp, cnt, AF.Sign, ["cnt"], ["btmp"], bias=0.5 - cc)
                    act(nth[b_], tmp, AF.Identity, ["btmp", "nth%d" % a_], ["nth%d" % b_], bias=nth[a_], scale=-st / 2)
                    st /= 2
                fin = cfg.nbis % 2
                st_last = st * 2
                act(thr, nth[fin], AF.Identity, ["nth%d" % fin], ["thr"], bias=-st_last / 2, scale=-1.0)
            ts(mb[:nt, 0:L], SC[:nt, 0:L], thr, NEG, ALU.is_lt, ALU.mult, ["SC", "thr"], [mkey])
        else:
            ts(mb[:nt, 0:L], SC[:nt, 0:L], -LO, NEG, ALU.is_lt, ALU.mult, ["SC"], [mkey])

    def attend_back(nt, blocks, par):
        mb = MB[par]
        mkey = ("MB", par)
        qtb = QTB[par]
        qkey = ("QTB", par)
        qbank = [3, 4, 0, 1]
        ptbufs = [4, 5, 0, 1]
        units = [(bi, half) for bi in range(len(blocks)) for half in range(2)]
        first = [True, True]
        nu = len(units)

        def qk_exp(u):
            bi, half = units[u]
            c0, kb, slot = blocks[bi]
            bank = qbank[u % 4]
            for pp in range(2):
                p = 2 * half + pp
                o3 = PB[bank][:kb, pp * 2 * nt:(pp + 1) * 2 * nt].rearrange("p (e t) -> p e t", e=2)
                mm(o3, KT[:, p, c0:c0 + kb], qtb[:, p, :, :nt], True, False, [("KT", slot), qkey], [P(bank)])
                mm(o3, mb[:nt, c0:c0 + kb], I2[:nt, :, :nt], False, True, [mkey, "I2"], [P(bank)])
            pi = ptbufs[u % 4]
            act(BS[pi][:kb, 0:4 * nt], PB[bank][:kb, 0:4 * nt], AF.Exp, (), [P(bank), "BS%d" % pi], scale=0.125)

        def pv(u):
            bi, half = units[u]
            c0, kb, slot = blocks[bi]
            pi = ptbufs[u % 4]
            ptb = BS[pi]
            for pp in range(2):
                for e in range(2):
                    h = 2 * (2 * half + pp) + e
                    ob = 6 + h // 4
                    hh = h % 4
                    st_ = first[h // 4]
                    first[h // 4] = False
                    mm(PB[ob][:nt, hh * 65:(hh + 1) * 65], ptb[:kb, (pp * 2 + e) * nt:(pp * 2 + e + 1) * nt],
                       VA[:kb, slot, h, :], st_, (u == nu - 1 and hh == 3), ["BS%d" % pi, ("V", slot)], [P(ob)], skip=True)

        LAG = 2
        for u in range(nu):
            qk_exp(u)
            if u >= LAG:
                pv(u - LAG)
        for u in range(max(0, nu - LAG), nu):
            pv(u)

    def passB(l, nt, xsrc, xkey, ya_src, ya_key, out_dsts, out_keys):
        s = load_norm(nt, xsrc, xkey, None)
        dma("sp", YT[:nt, :], ya_src, [ya_key], ["YT"], "yi")
        O = 64
        u3 = PB[3][:, :].rearrange("p (b t) -> p b t", t=128)
        for b in range(4):
            for kc in range(8):
                mm(u3[:, b, :nt], WBUF[:, kc, b * 128:(b + 1) * 128], HNT[:, kc, :nt], kc == 0, kc == 7,
                   ["HNT", "WBUF"], [P(3)])
        cp(UEXT[:, :, 3:3 + nt], u3[:, :, :nt], (), [P(3), "UEXT"], eng="act")
        inproj(nt, 512, 512, 0)
        cp(VM[:nt, :, 0:128], PB[0][:nt, :].rearrange("p (h d) -> p h d", d=128), (), [P(0), "VM"])
        inproj(nt, 1024, 512, 1)
        sgo = FS[0]
        act(sgo[:nt, 0:512], PB[1][:nt, :], AF.Sigmoid, (), [P(1), "FS0"])
        inproj(nt, 1536, 512, 0)
        szm = FS[1]
        act(szm[:nt, 0:512], PB[0][:nt, :], AF.Silu, (), [P(0), "FS1"])
        inproj(nt, 2048, 8, 1)
        tt(SM[:nt, O + 0:O + 4], PB[1][:nt, 0:4], GBI[:nt, :], ALU.add, ["GBI"], [P(1), "igb"])
        tt(SM[:nt, O + 4:O + 8], PB[1][:nt, 4:8], GBF[:nt, :], ALU.add, ["GBF"], [P(1), "fgb"])
        act(SM[:nt, O + 8:O + 12], SM[:nt, O + 4:O + 8], AF.Exp, ["fgb"], ["e1"], scale=-1.0)
        act(SM[:nt, O + 12:O + 16], SM[:nt, O + 8:O + 12], AF.Ln, ["e1"], ["nlf"], bias=1.0)
        mm(PB[5][0:4, 0:nt], SM[:nt, O + 0:O + 4], IDF[:nt, :nt], True, False, ["igb", "IDF"], [P(5)])
        mm(PB[5][0:4, 0:nt], SM[:nt, O + 12:O + 16], TRI[:nt, :nt], False, True, ["nlf", "TRI"], [P(5)])
        mm(PB[5][0:4, 128:128 + nt], SM[:nt, O + 12:O + 16], TRI[:nt, :nt], True, True, ["nlf", "TRI"], [P(5)])
        MU, NBC, MUN, DLT, DEC, NMU, CB, AMX = [SG[0:4, i:i + 1] for i in range(8)]
        arow = SR[0:4, 0, :nt]
        orow = SR[0:4, 1, :nt]
        crow = SR[0:4, 2, :nt]
        ts(arow, PB[5][0:4, 0:nt], NBC, None, ALU.add, None, ["NBC"], [P(5), "arow"])
        red(AMX, arow, ALU.max, ["arow"], ["AMX"])
        tt(MUN, AMX, MU, ALU.max, ["AMX", "MU"], ["MUN"])
        tt(DLT, MU, MUN, ALU.subtract, ["MU", "MUN"], ["DLT"])
        act(DEC, DLT, AF.Exp, ["DLT"], ["DEC"])
        ts(NMU, MUN, -1.0, None, ALU.mult, None, ["MUN"], ["NMU"])
        tt(CB, NBC, MUN, ALU.subtract, ["NBC", "MUN"], ["CB"])
        act(orow, arow, AF.Exp, ["arow", "NMU"], ["orow"], bias=NMU)
        act(crow, PB[5][0:4, 128:128 + nt], AF.Exp, ["CB"], [P(5), "crow"], bias=CB)
        tt(NBC, NBC, PB[5][0:4, 128 + nt - 1:128 + nt], ALU.add, ["NBC"], [P(5), "NBC"])
        cp(MU, MUN, ["MUN"], ["MU"])
        ts(DD[:, :], I4[:, :], DEC, None, ALU.mult, None, ["I4", "DEC"], ["DD"])
        mm(PB[5][:nt, 256:260], orow, I4[:, :], True, True, ["orow", "I4"], [P(5)])
        mm(PB[5][:nt, 260:264], crow, I4[:, :], True, True, ["crow", "I4"], [P(5)])
        mm(PB[5][:, 264:268], ONES4[:, :], DD[:, :], True, True, ["ONES4", "DD"], [P(5)])
        cp(SM[:nt, O + 16:O + 24], PB[5][:nt, 256:264], (), [P(5), "wc"])
        cp(SM[:, O + 24:O + 28], PB[5][:, 264:268], (), [P(5), "dbc"])
        WCO = O + 16
        CLO = O + 20
        DBO = O + 24
        cacc = FS[2]
        ca3 = cacc[:, 0:512].rearrange("p (b t) -> p b t", t=128)
        for b in range(4):
            ts(ca3[:, b, :nt], UEXT[:, b, 0:nt], CVW[:, b, 0:1], CVB[:, b:b + 1], ALU.mult, ALU.add,
               ["UEXT", "CVW", "CVB"], ["FS2"])
            for j in range(1, 4):
                stt(ca3[:, b, :nt], UEXT[:, b, j:j + nt], CVW[:, b, j:j + 1], ca3[:, b, :nt], ALU.mult, ALU.add,
                    ["UEXT", "CVW", "FS2"], ["FS2"])
        ct = BS[0]
        ct3 = ct[:, :].rearrange("p (b t) -> p b t", t=128)
        act(ct3[:, :, :nt], ca3[:, :, :nt], AF.Silu, ["FS2"], ["BS0"])
        cp(UEXT[:, :, 0:3], UEXT[:, :, nt:nt + 3], ["UEXT"], ["UEXT"])
        q3p = PB[4][:, :].rearrange("p (h t) -> p h t", t=128)
        for h in range(4):
            mm(q3p[:, h, :nt], WQK[:, 0, h, :], ct3[:, h, :nt], True, True, ["WQK", "BS0"], [P(4)])
        qmt = BS[1]
        qmt3 = qmt[:, :].rearrange("p (h t) -> p h t", t=128)
        ts(qmt3[:, :, :nt], q3p[:, :, :nt], 128.0 ** -0.5, None, ALU.mult, None, (), [P(4), "BS1"])
        for h in range(4):
            mm(q3p[:, h, :nt], WQK[:, 1, h, :], ct3[:, h, :nt], True, True, ["WQK", "BS0"], [P(4)])
        kmt = BS[2]
        kmt3 = kmt[:, :].rearrange("p (h t) -> p h t", t=128)
        cp(kmt3[:, :, :nt], q3p[:, :, :nt], (), [P(4), "BS2"])
        k3p = PB[3][:nt, :].rearrange("p (h e) -> p h e", e=128)
        for h in range(4):
            mm(k3p[:, h, :], ct3[:, h, :nt], WQK[:, 1, h, :], True, True, ["WQK", "BS0"], [P(3)])
        kw = BS[3]
        kw3 = kw[:nt, :].rearrange("p (h e) -> p h e", e=128)
        for h in range(4):
            ts(kw3[:, h, :], k3p[:, h, :], SM[:nt, WCO + h:WCO + h + 1], None, ALU.mult, None, ["wc"], [P(3), "BS3"])
        s3p = PB[4][:nt, :].rearrange("p (h t) -> p h t", t=128)
        for h in range(4):
            mm(s3p[:, h, :nt], kmt3[:, h, :nt], qmt3[:, h, :nt], True, True, ["BS2", "BS1"], [P(4)])
        pmt = BS[4]
        pmt3 = pmt[:nt, :].rearrange("p (h t) -> p h t", t=128)
        for h in range(4):
            stt(pmt3[:, h, :nt], s3p[:, h, :nt], SM[:nt, WCO + h:WCO + h + 1], TRI[:nt, :nt], ALU.mult, ALU.mult,
                ["wc", "TRI"], [P(4), "BS4"])
        for h in range(4):
            ts(CNS[:, h, :], CNS[:, h, :], SM[:, DBO + h:DBO + h + 1], None, ALU.mult, None, ["dbc", "CNS"], ["CNS"])
        cp(CNB[:, :, :], CNS[:, :, :], ["CNS"], ["CNB"], eng="act")
        for h in range(4):
            b, hh = divmod(h, 2)
            o = PB[6 + b][:nt, hh * 129:(hh + 1) * 129]
            mm(o, pmt3[:, h, :nt], VM[:nt, h, :], True, False, ["BS4", "VM"], [P(6 + b)])
            mm(o, qmt3[:, h, :nt], CNB[:, h, :], False, True, ["BS1", "CNB"], [P(6 + b)])
        for h in range(4):
            b, hh = divmod(h, 2)
            mm(PB[b][:, hh * 129:(hh + 1) * 129], kw3[:, h, :], VM[:nt, h, :], True, True, ["BS3", "VM"], [P(b)])
        for b in range(2):
            tt(CNS[:, 2 * b:2 * b + 2, :], CNS[:, 2 * b:2 * b + 2, :],
               PB[b][:, 0:258].rearrange("p (h d) -> p h d", d=129), ALU.add, ["CNS"], [P(b), "CNS"])
        hb = FS[3]
        for b in range(2):
            nd = PB[6 + b][:nt, 0:258].rearrange("p (h d) -> p h d", d=129)
            tt(SM[:nt, O + 28 + 2 * b:O + 30 + 2 * b], nd[:, :, 128], SM[:nt, CLO + 2 * b:CLO + 2 + 2 * b], ALU.max,
               ["wc"], [P(6 + b), "dmax"])
            stt(SM[:nt, O + 28 + 2 * b:O + 30 + 2 * b], nd[:, :, 128], -1.0, SM[:nt, O + 28 + 2 * b:O + 30 + 2 * b],
                ALU.mult, ALU.max, ["dmax"], [P(6 + b), "dmax"])
            recip(SM[:nt, O + 32 + 2 * b:O + 34 + 2 * b], SM[:nt, O + 28 + 2 * b:O + 30 + 2 * b], ["dmax"], ["rdm"])
            tt(hb[:nt, 256 * b:256 * (b + 1)].rearrange("p (h d) -> p h d", d=128), nd[:, :, 0:128],
               SM[:nt, O + 32 + 2 * b:O + 34 + 2 * b].unsqueeze(2).to_broadcast([nt, 2, 128]), ALU.mult,
               ["rdm"], [P(6 + b), "FS3"])
        sq = FS[4]
        act(sq[:nt, 0:512], hb[:nt, 0:512], AF.Square, ["FS3"], ["FS4"])
        red(SM[:nt, O + 36:O + 40], sq[:nt, 0:512].rearrange("p (h d) -> p h d", d=128), ALU.add, ["FS4"], ["hss"])
        act(SM[:nt, O + 40:O + 44], SM[:nt, O + 36:O + 40], AF.Ln, ["hss"], ["hss1"], bias=EPS, scale=1.0 / 128)
        act(SM[:nt, O + 44:O + 48], SM[:nt, O + 40:O + 44], AF.Exp, ["hss1"], ["hrstd"], scale=-0.5)
        hb3 = hb[:nt, 0:512].rearrange("p (h d) -> p h d", d=128)
        tt(hb3, hb3, SM[:nt, O + 44:O + 48].unsqueeze(2).to_broadcast([nt, 4, 128]), ALU.mult, ["hrstd", "FS3"], ["FS3"])
        tt(hb[:nt, 0:512], hb[:nt, 0:512], GH[:nt, :], ALU.mult, ["FS3", "GH"], ["FS3"])
        tt(hb[:nt, 0:512], hb[:nt, 0:512], sgo[:nt, 0:512], ALU.mult, ["FS3", "FS0"], ["FS3"])
        pt = pbf(2).rearrange("p (k t) -> p k t", t=128)
        for b in range(4):
            tr(pt[:nt, b, :], ct3[:, b, :nt], IDB[:, :], ["BS0", "IDB"], [P(2)])
        t2 = FS[4]
        tt(t2[:nt, 0:512].rearrange("p (k t) -> p k t", t=128), pt[:nt, 0:4, :],
           SKP[:nt, :].rearrange("p (k t) -> p k t", t=128), ALU.mult, ["SKP"], [P(2), "FS4"])
        tt(hb[:nt, 0:512], hb[:nt, 0:512], t2[:nt, 0:512], ALU.add, ["FS3", "FS4"], ["FS3"])
        mxb = BS[5]
        tt(mxb[:nt, :], hb[:nt, 0:512], szm[:nt, 0:512], ALU.mult, ["FS3", "FS1"], ["BS5"])
        for p in range(4):
            tr(pt[:, p, :nt], mxb[:nt, p * 128:(p + 1) * 128], IDB[:nt, :nt], ["BS5", "IDB"], [P(2)])
        mxt = BS[1]
        mxt3 = mxt[:, :].rearrange("p (k t) -> p k t", t=128)
        cp(mxt3[:, :, :nt], pt[:, 0:4, :nt], (), [P(2), "BS1"])
        for hf in range(2):
            for kc in range(4):
                mm(PB[hf][:nt, :], mxt3[:, kc, :nt], WOUT[:, kc, hf * 512:(hf + 1) * 512], kc == 0, kc == 3,
                   ["BS1", "WOUT"], [P(hf)])
            tt(YT[:nt, hf * 512:(hf + 1) * 512], PB[hf][:nt, :], YT[:nt, hf * 512:(hf + 1) * 512], ALU.add,
               ["YT"], [P(hf), "YT"])
        for i, (dst, key) in enumerate(zip(out_dsts, out_keys)):
            dma("pool", dst, YT[:nt, :], ["YT"], [key] if key else (), "xo%d" % i)

    def state_out(Cd, nd_, md, cvd, chs):
        dma("pool", Cd.rearrange("h k v -> k h v"), CNS[:, :, 0:128], ["CNS"], (), chs + "C")
        dma("pool", nd_.rearrange("h k -> k h"), CNS[:, :, 128], ["CNS"], (), chs + "n", nonc=True)
        tt(SG[0:4, 8:9], SG[0:4, 0:1], SG[0:4, 1:2], ALU.subtract, ["MU", "NBC"], ["MOUT"])
        dma("pool", md.rearrange("(h o) -> h o", o=1), SG[0:4, 8:9], ["MOUT"], (), chs + "m", nonc=True)
        for j in range(3):
            dma("pool", cvd[j].rearrange("(b f) -> f b", f=128), UEXT[:, :, j], ["UEXT"], (), chs + "v%d" % j, nonc=True)

    for l in range(DEPTH):
        load_layer_params(l)
        load_weights_A(l)
        xin_p = XPs[l % 2]
        xout_p = XPs[(l + 1) % 2]
        xin_s = XSs[l % 2]
        xout_s = XSs[(l + 1) % 2]
        barrier(ALIAS)
        mset(QTB[1][:, :, :, :], 0.0, [("QTB", 1)])
        prev = None
        for ti in range(NT + 1):
            if ti == 0:
                nt, pos0 = 16, 0
                xsrc = meta if l == 0 else xin_p[0:16, :]
                blocks = [(0, 16, 0)]
                cm = False
            else:
                fi = ti - 1
                nt, pos0 = 128, 16 + 128 * fi
                xsrc = xp[128 * fi:128 * fi + 128, :] if l == 0 else xin_p[pos0:pos0 + 128, :]
                blocks = [(0, 16, 0)] + [(16 + 128 * j, 128, 1 + j) for j in range(fi + 1)]
                cm = True
            ctx = passA_front(l, nt, xsrc, ("XP", l % 2, ti), pos0, blocks, ti,
                              pk[l, pos0:pos0 + nt, :], pv[l, pos0:pos0 + nt, :], pki[l, pos0:pos0 + nt, :],
                              YAp[pos0:pos0 + nt, :], ("YAp", ti), cfg.ksel_p, cm)
            if prev is not None:
                passA_back(prev)
            prev = ctx
        passA_back(prev)
        for b in range(NSB):
            for j in range(NCB):
                kst = FS[j % 2]
                vst = FS[2 + (j % 2)]
                kkst = "FS%d" % (j % 2)
                kvst = "FS%d" % (2 + j % 2)
                dma("sp", kst[:, 0:512], ck[l, b, 128 * j:128 * (j + 1), :], (), [kkst], "ks%d" % (j % 2))
                dma("sp", vst[:, 0:512], cv[l, b, 128 * j:128 * (j + 1), :], (), [kvst], "vs%d" % (j % 2))
                pf = PB[5][:, :].rearrange("p (k t) -> p k t", t=128)
                for p in range(4):
                    tr(pf[:, p, :], kst[:, p * 128:(p + 1) * 128], IDF[:, :], [kkst, "IDF"], [P(5)])
                cp(KT[:, :, 16 + 128 * j:16 + 128 * (j + 1)], pf[:, :, :], (), [P(5), ("KT", 1 + j)], eng="act")
                cp(VA[:, 1 + j, :, 0:64], vst[:, 0:512].rearrange("p (h d) -> p h d", d=64), [kvst], [("V", 1 + j)])
            for g in range(0, NCB, 4):
                ng = min(4, NCB - g)
                kis = FS[4 + (g // 4) % 2]
                kkey = "FS%d" % (4 + (g // 4) % 2)
                kdkey = "BS%d" % ((g // 4) % 2)
                dma("sp", kis[:, 0:64 * ng].rearrange("p (j d) -> p j d", d=64),
                    cki[l, b, 128 * g:128 * (g + ng), :].rearrange("(j p) d -> p j d", p=128), (), [kkey],
                    "kc%d" % ((g // 4) % 2))
                kd = BS[(g // 4) % 2]
                kd4 = kd[:, 0:128 * ng].rearrange("p (j e d) -> p j e d", e=2, d=64)
                k3 = kis[:, 0:64 * ng].rearrange("p (j d) -> p j d", d=64)
                cp(kd4[:, :, 0, :], k3, [kkey], [kdkey])
                cp(kd4[:, :, 1, :], k3, [kkey], [kdkey])
                pt = pbf(2).rearrange("p (k t) -> p k t", t=128)
                for jj in range(ng):
                    tr(pt[:, jj, :], kd[:, 128 * jj:128 * (jj + 1)], IDB[:, :], [kdkey, "IDB"], [P(2)])
                cp(KI2[:, 16 + 128 * g:16 + 128 * (g + ng)], pbf(2)[:, 0:128 * ng], (), [P(2)] + [("KI", 1 + g + q) for q in range(ng)], eng="act")
            xsrc = xs[b] if l == 0 else xin_s[b]
            pos0 = 16 + PAST
            blocks = [(0, 16, 0)] + [(16 + 128 * j, 128, 1 + j) for j in range(NCB)] + [(pos0, 32, NSLOT - 1)]
            ctx = passA_front(l, 32, xsrc, ("XS", l % 2, b), pos0, blocks, NSLOT - 1,
                              sk[l, b], sv[l, b], ski[l, b], YAs[b], ("YAs", b), cfg.ksel_s, False)
            passA_back(ctx)
        barrier(ALIAS)
        load_weights_B(l)
        mset(VM[:, :, 128:129], 1.0, ["VM"])
        mset(CNS[:, :, :], 0.0, ["CNS"])
        mset(UEXT[:, :, 0:3], 0.0, ["UEXT"])
        mset(SG[0:4, 0:2], 0.0, ["MU", "NBC"])
        last = (l == DEPTH - 1)
        for ti in range(NT + 1):
            if ti == 0:
                nt, pos0 = 16, 0
                xsrc = meta if l == 0 else xin_p[0:16, :]
                dsts, keys = ([], []) if last else ([xout_p[0:16, :]], [("XP", (l + 1) % 2, 0)])
            else:
                fi = ti - 1
                nt, pos0 = 128, 16 + 128 * fi
                xsrc = xp[128 * fi:128 * fi + 128, :] if l == 0 else xin_p[pos0:pos0 + 128, :]
                if last:
                    dsts, keys = [yp[128 * fi:128 * fi + 128, :]], [None]
                else:
                    dsts, keys = [xout_p[pos0:pos0 + 128, :]], [("XP", (l + 1) % 2, ti)]
            passB(l, nt, xsrc, ("XP", l % 2, ti), YAp[pos0:pos0 + nt, :], ("YAp", ti), dsts, keys)
        state_out(pC[l], pn[l], pm[l], pconv[l], "p")
        for b in range(NSB):
            dma("sp", CNS[:, :, 0:128], stC[l, b].rearrange("h k v -> k h v"), (), ["CNS"], "si0")
            dma("sp", CNS[:, :, 128], stn[l, b].rearrange("h k -> k h"), (), ["CNS"], "si1", nonc=True)
            dma("sp", SG[0:4, 0:1], stm[l, b].rearrange("(h o) -> h o", o=1), (), ["MU"], "si2", nonc=True)
            for j in range(3):
                dma("sp", UEXT[:, :, j], stconv[l, b, j].rearrange("(b f) -> f b", f=128), (), ["UEXT"], "si3%d" % j, nonc=True)
            mset(SG[0:4, 1:2], 0.0, ["NBC"])
            xsrc = xs[b] if l == 0 else xin_s[b]
            if last:
                dsts, keys = [ys[b]], [None]
            else:
                dsts, keys = [xout_s[b]], [("XS", (l + 1) % 2, b)]
            passB(l, 32, xsrc, ("XS", l % 2, b), YAs[b], ("YAs", b), dsts, keys)
            state_out(sC[l, b], sn[l, b], sm[l, b], sconv[l, b], "s")

    allch = list(S.chan_last.keys())
    S.add("sp", None, [], [], None)
    fin = S.ops[-1]
    fin.raw = set(S.chan_last[c] for c in allch)

    S.finalize()
    engs = ["pe", "act", "dve", "pool", "sp"]
    esem = {}
    for e in engs:
        n = (S.eng_cnt.get(e, 0) + SEM_CH - 1) // SEM_CH
        esem[e] = [es.enter_context(nc.semaphore("s_%s_%d" % (e, i))) for i in range(max(n, 1))]
    csem = {}
    for c, n in S.chan_cnt.items():
        k = (16 * n + SEM_CH - 1) // SEM_CH
        csem[c] = [es.enter_context(nc.semaphore("c_%s_%d" % (c, i))) for i in range(max(k, 1))]
    DCH = SEM_CH // 16 * 16

    def sem_of(t, v):
        if t[0] == "e":
            return esem[t[1]][(v - 1) // SEM_CH], (v - 1) % SEM_CH + 1
        return csem[t[1]][(v - 1) // DCH], (v - 1) % DCH + 1

    by_eng = {e: [] for e in engs}
    for op in S.ops:
        by_eng[op.eng].append(op)
    block = es.enter_context(nc.Block())

    def emit(ename, h):
        for op in by_eng[ename]:
            for (t, v) in op.waits:
                sem, val = sem_of(t, v)
                h.wait_ge(sem, val)
            if op.fn is None:
                continue
            ins = op.fn(h)
            if op.chan is not None:
                sem, val = sem_of(("c", op.chan), 16 * op.chan_n)
                ins.then_inc(sem, 16)
            elif op.inc:
                sem, val = sem_of(("e", ename), op.incval)
                ins.then_inc(sem, 1)

    @block.tensor
    def _(h):
        emit("pe", h)

    @block.scalar
    def _(h):
        emit("act", h)

    @block.vector
    def _(h):
        emit("dve", h)

    @block.gpsimd
    def _(h):
        emit("pool", h)

    @block.sync
    def _(h):
        emit("sp", h)

    es.close()
    return nc, len(S.ops)


def const_tables(cfg):
    half = 32
    freqs = (np.float32(10000.0) ** (-np.arange(half, dtype=np.float32) / np.float32(half))).astype(np.float32)
    pos = np.arange(cfg.kc, dtype=np.float32)
    ang = (pos[:, None] * freqs[None, :]).astype(np.float32)
    rope = np.concatenate([np.cos(ang.astype(np.float64)), np.sin(ang.astype(np.float64))], axis=1).astype(np.float32)
    tri = np.triu(np.ones((128, 128), np.float32))
    return {
        "c_ident": np.eye(128, dtype=np.float32),
        "c_tri": tri,
        "c_rope": rope,
        "c_i4": np.eye(4, dtype=np.float32),
        "c_ones4": np.ones((4, 128), np.float32),
    }


WNAMES = ["meta", "norm_g", "w_in", "q_norm_g", "k_norm_g", "conv_w", "conv_b", "wq_m", "wk_m",
          "b_igate", "b_fgate", "head_norm_g", "skip", "w_out"]


def run(cfg, inputs, ncores):
    nc, nops = build(cfg)
    consts = const_tables(cfg)
    f = lambda a: np.ascontiguousarray(np.asarray(a, dtype=np.float32))
    nsb = cfg.nsb
    in_maps = []
    for c in range(ncores):
        m = {
            "xp": f(inputs["x_prompt"][c]),
            "xs": f(inputs["x_sample"][nsb * c:nsb * (c + 1)]),
            "ck": f(inputs["cache_k"][:, nsb * c:nsb * (c + 1)]).reshape(cfg.depth, nsb, cfg.past, 512),
            "cv": f(inputs["cache_v"][:, nsb * c:nsb * (c + 1)]).reshape(cfg.depth, nsb, cfg.past, 512),
            "cki": f(inputs["cache_kidx"][:, nsb * c:nsb * (c + 1)]),
            "stC": f(inputs["state_C"][:, nsb * c:nsb * (c + 1)]),
            "stn": f(inputs["state_n"][:, nsb * c:nsb * (c + 1)]),
            "stm": f(inputs["state_m"][:, nsb * c:nsb * (c + 1)]),
            "stconv": f(inputs["state_conv"][:, nsb * c:nsb * (c + 1)]),
        }
        for wn in WNAMES:
            m[wn] = f(inputs[wn])
        m.update(consts)
        in_maps.append(m)
    res = run_bass_kernel_spmd(nc, in_maps, core_ids=list(range(ncores)))
    R = res.results
    D = cfg.depth

    def cat_b(name):
        return np.stack([R[c][name] for c in range(ncores)], axis=1)

    def cat_s(name):
        return np.concatenate([R[c][name] for c in range(ncores)], axis=1)

    y_prompt = np.stack([R[c]["yp"] for c in range(ncores)], axis=0)
    y_sample = np.concatenate([R[c]["ys"] for c in range(ncores)], axis=0)
    pk = cat_b("pk").reshape(D, ncores, cfg.npos, 8, 64)
    pv = cat_b("pv").reshape(D, ncores, cfg.npos, 8, 64)
    pki = cat_b("pki")
    sk = cat_s("sk").reshape(D, ncores * nsb, 32, 8, 64)
    sv = cat_s("sv").reshape(D, ncores * nsb, 32, 8, 64)
    return (y_prompt, y_sample, pk, pv, pki, cat_b("pC"), cat_b("pn"), cat_b("pm"), cat_b("pconv"),
            sk, sv, cat_s("ski"), cat_s("sC"), cat_s("sn"), cat_s("sm"), cat_s("sconv"))


def kernel(**inputs):
    cfg = Cfg()
    outs = run(cfg, inputs, 8)
    return tuple(np.ascontiguousarray(o.astype(np.float32)) for o in outs)
```

```python
import numpy as np
from contextlib import ExitStack
import concourse.bass as bass
import concourse.mybir as mybir
from concourse.bass_utils import run_bass_kernel_spmd

F32 = mybir.dt.float32
BF16 = mybir.dt.bfloat16
AF = mybir.ActivationFunctionType
ALU = mybir.AluOpType
AX = mybir.AxisListType

EPS = 1e-6
NEG = -30000.0
SEM_CH = 24000


class Cfg:
    def __init__(self, ntiles=32, past=4096, depth=4, ksel_p=256, ksel_s=256, nsb=4, nbis=17, brk=4.0):
        self.ntiles, self.past, self.depth = ntiles, past, depth
        self.ksel_p, self.ksel_s, self.nsb, self.nbis, self.brk = ksel_p, ksel_s, nsb, nbis, brk
        self.npos = 16 + 128 * ntiles
        self.ncb = past // 128
        self.kc = 16 + past + 32
        self.cw = max(self.npos, self.kc)
        self.nslot = 2 + max(ntiles, self.ncb)


class Op:
    __slots__ = ("eng", "fn", "raw", "oth", "chan", "chan_n", "inc", "waits", "incval")


class Sched:
    def __init__(self):
        self.ops = []
        self.last_w = {}
        self.readers = {}
        self.chan_last = {}
        self.chan_cnt = {}

    def add(self, eng, fn, r=(), w=(), chan=None):
        op = Op()
        op.eng, op.fn, op.chan = eng, fn, chan
        idx = len(self.ops)
        raw, oth = set(), set()
        for k in r:
            lw = self.last_w.get(k)
            if lw is not None:
                raw.add(lw)
        for k in w:
            lw = self.last_w.get(k)
            if lw is not None:
                oth.add(lw)
            for rd in self.readers.get(k, {}).values():
                oth.add(rd)
        if chan is not None:
            pl = self.chan_last.get(chan)
            if pl is not None:
                raw.add(pl)
            self.chan_last[chan] = idx
            self.chan_cnt[chan] = self.chan_cnt.get(chan, 0) + 1
            op.chan_n = self.chan_cnt[chan]
        op.raw, op.oth = raw, oth
        rk = ("c", chan) if chan is not None else ("e", eng)
        for k in r:
            self.readers.setdefault(k, {})[rk] = idx
        for k in w:
            self.last_w[k] = idx
            self.readers[k] = {}
        self.ops.append(op)
        return idx

    def finalize(self):
        ops = self.ops
        seq = {}
        for i, op in enumerate(ops):
            if op.chan is None:
                seq[op.eng] = seq.get(op.eng, 0) + 1
                op.incval = seq[op.eng]
        waited = {}
        needed = set()
        for op in ops:
            deps = {}
            for j in op.raw:
                deps[j] = True
            for j in op.oth:
                deps.setdefault(j, False)
            tg = {}
            for j, is_raw in deps.items():
                o = ops[j]
                if o.chan is None and o.eng == op.eng and not is_raw:
                    continue
                if o.chan is not None:
                    t, v = ("c", o.chan), 16 * o.chan_n
                else:
                    t, v = ("e", o.eng), o.incval
                if v > tg.get(t, (0, None))[0]:
                    tg[t] = (v, j)
            wl = []
            wd = waited.setdefault(op.eng, {})
            for t, (v, j) in tg.items():
                if v > wd.get(t, 0):
                    wd[t] = v
                    wl.append((t, j))
                    if t[0] == "e":
                        needed.add(j)
            op.waits = wl
        cnt = {}
        for i, op in enumerate(ops):
            op.inc = False
            if op.chan is None and i in needed:
                cnt[op.eng] = cnt.get(op.eng, 0) + 1
                op.inc = True
                op.incval = cnt[op.eng]
        self.eng_cnt = cnt
        for op in ops:
            wl = []
            for (t, j) in op.waits:
                o = ops[j]
                wl.append((t, 16 * o.chan_n if t[0] == "c" else o.incval))
            op.waits = wl


def build(cfg):
    NT, PAST, DEPTH = cfg.ntiles, cfg.past, cfg.depth
    NPOS, NCB, KCOLS, CW, NSLOT, NSB = cfg.npos, cfg.ncb, cfg.kc, cfg.cw, cfg.nslot, cfg.nsb
    nc = bass.Bass("TRN2", target_bir_lowering=False)
    S = Sched()

    def din(name, shape):
        return nc.dram_tensor(name, list(shape), F32, kind="ExternalInput").ap()

    def dout(name, shape):
        return nc.dram_tensor(name, list(shape), F32, kind="ExternalOutput").ap()

    def dscr(name, shape):
        return nc.dram_tensor(name, list(shape), F32, kind="Internal").ap()

    xp = din("xp", [128 * NT, 1024])
    xs = din("xs", [NSB, 32, 1024])
    ck = din("ck", [DEPTH, NSB, PAST, 512])
    cv = din("cv", [DEPTH, NSB, PAST, 512])
    cki = din("cki", [DEPTH, NSB, PAST, 64])
    stC = din("stC", [DEPTH, NSB, 4, 128, 128])
    stn = din("stn", [DEPTH, NSB, 4, 128])
    stm = din("stm", [DEPTH, NSB, 4])
    stconv = din("stconv", [DEPTH, NSB, 3, 512])
    meta = din("meta", [16, 1024])
    norm_g = din("norm_g", [DEPTH, 1024])
    w_in = din("w_in", [DEPTH, 1024, 4688])
    q_norm_g = din("q_norm_g", [DEPTH, 64])
    k_norm_g = din("k_norm_g", [DEPTH, 64])
    conv_w = din("conv_w", [DEPTH, 4, 512])
    conv_b = din("conv_b", [DEPTH, 512])
    wq_m = din("wq_m", [DEPTH, 4, 128, 128])
    wk_m = din("wk_m", [DEPTH, 4, 128, 128])
    b_igate = din("b_igate", [DEPTH, 4])
    b_fgate = din("b_fgate", [DEPTH, 4])
    head_norm_g = din("head_norm_g", [DEPTH, 512])
    skip = din("skip", [DEPTH, 512])
    w_out = din("w_out", [DEPTH, 1024, 1024])
    c_ident = din("c_ident", [128, 128])
    c_tri = din("c_tri", [128, 128])
    c_rope = din("c_rope", [KCOLS, 64])
    c_i4 = din("c_i4", [4, 4])
    c_ones4 = din("c_ones4", [4, 128])

    yp = dout("yp", [128 * NT, 1024])
    ys = dout("ys", [NSB, 32, 1024])
    pk = dout("pk", [DEPTH, NPOS, 512])
    pv = dout("pv", [DEPTH, NPOS, 512])
    pki = dout("pki", [DEPTH, NPOS, 64])
    pC = dout("pC", [DEPTH, 4, 128, 128])
    pn = dout("pn", [DEPTH, 4, 128])
    pm = dout("pm", [DEPTH, 4])
    pconv = dout("pconv", [DEPTH, 3, 512])
    sk = dout("sk", [DEPTH, NSB, 32, 512])
    sv = dout("sv", [DEPTH, NSB, 32, 512])
    ski = dout("ski", [DEPTH, NSB, 32, 64])
    sC = dout("sC", [DEPTH, NSB, 4, 128, 128])
    sn = dout("sn", [DEPTH, NSB, 4, 128])
    sm = dout("sm", [DEPTH, NSB, 4])
    sconv = dout("sconv", [DEPTH, NSB, 3, 512])

    XPs = [dscr("XP0", [NPOS, 1024]), dscr("XP1", [NPOS, 1024])]
    XSs = [dscr("XS0", [NSB, 32, 1024]), dscr("XS1", [NSB, 32, 1024])]
    YAp = dscr("YAp", [NPOS, 1024])
    YAs = dscr("YAs", [NSB, 32, 1024])

    es = ExitStack()

    def sb(name, shape, dt=F32):
        return es.enter_context(nc.sbuf_tensor(name, list(shape), dt))

    WBUF = sb("WBUF", [128, 8, 2632], BF16)
    WOUT = sb("WOUT", [128, 4, 1024], BF16)
    KT = sb("KT", [128, 4, CW], BF16)
    VA = sb("VA", [128, NSLOT, 8, 65], BF16)
    KI2 = sb("KI2", [128, CW], BF16)
    SC = sb("SC", [128, CW], F32)
    MB = [sb("MB0", [128, CW], BF16), sb("MB1", [128, max(CW, 4128)], BF16)]
    XT = [sb("XT0", [128, 1024]), sb("XT1", [128, 1024])]
    YT = sb("YT", [128, 1024])
    CS = [sb("CS0", [128, 64]), sb("CS1", [128, 64])]
    HN = sb("HN", [128, 1024], BF16)
    HNT = sb("HNT", [128, 8, 128], BF16)
    FS = [sb("FS%d" % i, [128, 528]) for i in range(9)]
    BS = [sb("BS%d" % i, [128, 512], BF16) for i in range(6)]
    QTB = [sb("QTB0", [128, 4, 2, 128], BF16), sb("QTB1", [128, 4, 2, 128], BF16)]
    QIT = sb("QIT", [128, 4, 128], BF16)
    SM = sb("SM", [128, 128])
    SR = sb("SR", [4, 3, 128])
    SG = sb("SG", [4, 16])
    IDF = sb("IDF", [128, 128])
    IDB = sb("IDB", [128, 128], BF16)
    I2 = sb("I2", [128, 2, 128], BF16)
    TRI = sb("TRI", [128, 128])
    I4 = sb("I4", [4, 4])
    DD = sb("DD", [4, 4])
    ONES4 = sb("ONES4", [4, 128])
    GT = sb("GT", [128, 8])
    GQ = sb("GQ", [128, 64])
    GK = sb("GK", [128, 64])
    GBI = sb("GBI", [128, 4])
    GBF = sb("GBF", [128, 4])
    CVW = sb("CVW", [128, 4, 4])
    CVB = sb("CVB", [128, 4])
    MB1f = MB[1][:, 0:4128].bitcast(F32)
    GH = MB1f[:, 0:512]
    SKP = MB1f[:, 512:1024]
    UEXT = MB1f[:, 1024:1548].rearrange("p (b t) -> p b t", t=131)
    CNS = MB1f[:, 1548:2064].rearrange("p (h d) -> p h d", d=129)
    WQK = QTB[1][:, :, :, :].rearrange("p a b c -> p (a b c)").rearrange("p (q h e) -> p q h e", q=2, h=4)
    FS8b = FS[8][:, 0:516].bitcast(BF16)
    CNB = FS8b[:, 0:516].rearrange("p (h d) -> p h d", d=129)
    VM = FS8b[:, 516:1032].rearrange("p (h d) -> p h d", d=129)
    ALIAS = [("MB", 1), "GH", "SKP", "UEXT", "CNS", ("QTB", 1), "WQK", "FS8", "CNB", "VM"]
    PB = [es.enter_context(nc.psum_tensor("PB%d" % i, [128, 512], F32)) for i in range(8)]

    def pbf(i):
        return PB[i][:, :].bitcast(BF16)

    def P(i):
        return ("P", i)

    def dma(q, out, in_, r, w, chan, nonc=False):
        if nonc:
            S.add(q, lambda e: e.dma_start(out=out, in_=in_, allow_slow_non_contiguous=True), r, w, chan)
        else:
            S.add(q, lambda e: e.dma_start(out=out, in_=in_), r, w, chan)

    def mm(out, lhsT, rhs, start, stop, r, w, skip=False):
        S.add("pe", lambda e: e.matmul(out, lhsT=lhsT, rhs=rhs, start=start, stop=stop,
                                       skip_group_check=skip), r, w)

    def tr(out, in_, ident, r, w):
        S.add("pe", lambda e: e.transpose(out=out, in_=in_, identity=ident), r, w)

    def act(out, in_, func, r, w, bias=0.0, scale=1.0, accum=None):
        if accum is None:
            S.add("act", lambda e: e.activation(out=out, in_=in_, func=func, bias=bias, scale=scale), r, w)
        else:
            S.add("act", lambda e: e.activation(out=out, in_=in_, func=func, bias=bias, scale=scale,
                                                accum_out=accum), r, w)

    def ts(out, in0, s1, s2, op0, op1, r, w, eng="dve", accum=None):
        if op1 is None:
            S.add(eng, lambda e: e.tensor_scalar(out=out, in0=in0, scalar1=s1, scalar2=None, op0=op0), r, w)
        elif accum is None:
            S.add(eng, lambda e: e.tensor_scalar(out=out, in0=in0, scalar1=s1, scalar2=s2, op0=op0, op1=op1), r, w)
        else:
            S.add(eng, lambda e: e.tensor_scalar(out=out, in0=in0, scalar1=s1, scalar2=s2, op0=op0, op1=op1,
                                                 accum_out=accum), r, w)

    def tt(out, in0, in1, op, r, w, eng="dve"):
        S.add(eng, lambda e: e.tensor_tensor(out=out, in0=in0, in1=in1, op=op), r, w)

    def stt(out, in0, scalar, in1, op0, op1, r, w):
        S.add("dve", lambda e: e.scalar_tensor_tensor(out=out, in0=in0, scalar=scalar, in1=in1, op0=op0, op1=op1), r, w)

    def cp(out, in_, r, w, eng="dve"):
        if eng == "act":
            S.add(eng, lambda e: e.activation(out=out, in_=in_, func=AF.Copy), r, w)
        else:
            S.add(eng, lambda e: e.tensor_copy(out=out, in_=in_), r, w)

    def mset(ap, val, w, eng="dve"):
        S.add(eng, lambda e: e.memset(ap, val), (), w)

    def red(out, in_, op, r, w):
        S.add("dve", lambda e: e.tensor_reduce(out=out, in_=in_, axis=AX.X, op=op), r, w)

    def recip(out, in_, r, w):
        S.add("dve", lambda e: e.reciprocal(out=out, in_=in_), r, w)

    dma("sp", IDF[:, :], c_ident, (), ["IDF"], "c0")
    dma("sp", TRI[:, :], c_tri, (), ["TRI"], "c1")
    dma("sp", I4[:, :], c_i4, (), ["I4"], "c2")
    dma("sp", ONES4[:, :], c_ones4, (), ["ONES4"], "c3")
    cp(IDB[:, :], IDF[:, :], ["IDF"], ["IDB"])
    cp(I2[:, 0, :], IDF[:, :], ["IDF"], ["I2"])
    cp(I2[:, 1, :], IDF[:, :], ["IDF"], ["I2"])
    mset(QTB[0][:, :, :, :], 0.0, [("QTB", 0)])
    mset(VA[:, :, :, 64:65], 1.0, [("V", sl) for sl in range(NSLOT)])

    def barrier(keys):
        S.add("dve", lambda e: e.memset(SM[:, 127:128], 0.0), (), list(keys))

    def load_layer_params(l):
        dma("sp", GT[:, :], norm_g[l].rearrange("(k p) -> p k", p=128), (), ["GT"], "g0", nonc=True)
        dma("sp", GQ[:, :], q_norm_g[l].partition_broadcast(128), (), ["GQ"], "g1")
        dma("sp", GK[:, :], k_norm_g[l].partition_broadcast(128), (), ["GK"], "g2")
        dma("sp", GBI[:, :], b_igate[l].partition_broadcast(128), (), ["GBI"], "g3")
        dma("sp", GBF[:, :], b_fgate[l].partition_broadcast(128), (), ["GBF"], "g4")
        for j in range(4):
            dma("sp", CVW[:, :, j], conv_w[l, j].rearrange("(b f) -> f b", f=128), (), ["CVW"], "g7%d" % j, nonc=True)
        dma("sp", CVB[:, :], conv_b[l].rearrange("(b f) -> f b", f=128), (), ["CVB"], "g8", nonc=True)

    def load_weights_A(l):
        dma("pool", WBUF[:, :, 0:2632], w_in[l][:, 0:2632].rearrange("(k p) n -> p k n", p=128), (), ["WBUF"], "w0")
        dma("pool", WOUT[:, :, :], w_out[l][0:512, :].rearrange("(k p) n -> p k n", p=128), (), ["WOUT"], "w1")

    def load_weights_B(l):
        dma("pool", WBUF[:, :, 0:2056], w_in[l][:, 2632:4688].rearrange("(k p) n -> p k n", p=128), (), ["WBUF"], "w0")
        dma("pool", WOUT[:, :, :], w_out[l][512:1024, :].rearrange("(k p) n -> p k n", p=128), (), ["WOUT"], "w1")
        dma("pool", WQK[:, 0, :, :], wq_m[l].rearrange("h d e -> d h e"), (), ["WQK"], "w2")
        dma("pool", WQK[:, 1, :, :], wk_m[l].rearrange("h d e -> d h e"), (), ["WQK"], "w3")
        dma("sp", GH, head_norm_g[l].partition_broadcast(128), (), ["GH"], "g5")
        dma("sp", SKP, skip[l].partition_broadcast(128), (), ["SKP"], "g6")

    xslot = [0]

    def load_norm(nt, xsrc, xkey, pos0=None):
        s = xslot[0]
        xslot[0] ^= 1
        xt = XT[s]
        dma("sp", xt[:nt, :], xsrc, [xkey], [("XT", s)], "xt%d" % s)
        if pos0 is not None:
            dma("sp", CS[s][:nt, :], c_rope[pos0:pos0 + nt, :], (), [("CS", s)], "cs%d" % s)
        act(HN[:nt, :], xt[:nt, :], AF.Square, [("XT", s)], ["HN", "ss"], accum=SM[:nt, 0:1])
        act(SM[:nt, 1:2], SM[:nt, 0:1], AF.Ln, ["ss"], ["ss1"], bias=EPS, scale=1.0 / 1024)
        act(SM[:nt, 2:3], SM[:nt, 1:2], AF.Exp, ["ss1"], ["rstd"], scale=-0.5)
        act(HN[:nt, :], xt[:nt, :], AF.Copy, [("XT", s), "rstd"], ["HN"], scale=SM[:nt, 2:3])
        pt = pbf(2).rearrange("p (k t) -> p k t", t=128)
        for kc in range(8):
            tr(pt[:, kc, :nt], HN[:nt, kc * 128:(kc + 1) * 128], IDB[:nt, :nt], ["HN", "IDB"], [P(2)])
        tt(HNT[:, :, :nt], pt[:, :, :nt], GT[:, :].unsqueeze(2).to_broadcast([128, 8, nt]), ALU.mult,
           ["GT"], [P(2), "HNT"])
        return s

    def inproj(nt, c0, n, bank):
        for kc in range(8):
            mm(PB[bank][:nt, 0:n], HNT[:, kc, :nt], WBUF[:, kc, c0:c0 + n], kc == 0, kc == 7,
               ["HNT", "WBUF"], [P(bank)])

    def rope(nt, s, x1, x2, o1, o2, nh, rkeys, wkey, tmp, tkey="FS2"):
        cosb = CS[s][:nt, 0:32].unsqueeze(1).to_broadcast([nt, nh, 32])
        sinb = CS[s][:nt, 32:64].unsqueeze(1).to_broadcast([nt, nh, 32])
        t1 = tmp[:nt, 0:nh * 32].rearrange("p (h d) -> p h d", d=32)
        t2 = tmp[:nt, 256:256 + nh * 32].rearrange("p (h d) -> p h d", d=32)
        rk = list(rkeys) + [("CS", s)]
        tt(t1, x1, cosb, ALU.mult, rk, [tkey])
        tt(t2, x2, sinb, ALU.mult, rk, [tkey])
        tt(o1, t1, t2, ALU.subtract, [tkey], [wkey])
        tt(t1, x2, cosb, ALU.mult, rk, [tkey])
        tt(t2, x1, sinb, ALU.mult, rk, [tkey])
        tt(o2, t1, t2, ALU.add, [tkey], [wkey])

    def qknorm(nt, bank, gtile, gkey, dst, dkey):
        ps3 = PB[bank][:nt, :].rearrange("p (h d) -> p h d", d=64)
        sq = FS[1]
        act(sq[:nt, 0:512], PB[bank][:nt, :], AF.Square, (), [P(bank), "FS1"])
        red(SM[:nt, 8:16], sq[:nt, 0:512].rearrange("p (h d) -> p h d", d=64), ALU.add, ["FS1"], ["qss"])
        act(SM[:nt, 16:24], SM[:nt, 8:16], AF.Ln, ["qss"], ["qss1"], bias=EPS, scale=1.0 / 64)
        act(SM[:nt, 24:32], SM[:nt, 16:24], AF.Exp, ["qss1"], ["qrstd"], scale=-0.5)
        d3 = dst[:nt, 0:512].rearrange("p (h d) -> p h d", d=64)
        tt(d3, ps3, SM[:nt, 24:32].unsqueeze(2).to_broadcast([nt, 8, 64]), ALU.mult, ["qrstd"], [P(bank), dkey])
        tt(d3, d3, gtile[:nt, :].unsqueeze(1).to_broadcast([nt, 8, 64]), ALU.mult, [dkey, gkey], [dkey])

    WSC = 0.125 * (8.0 ** -0.5)

    tcount = [0]

    def slot_cols(slot):
        if slot == 0:
            return (0, 16)
        if slot == NSLOT - 1:
            return (16 + PAST, 16 + PAST + 32)
        return (16 + 128 * (slot - 1), 16 + 128 * slot)

    def passA_front(l, nt, xsrc, xkey, pos0, blocks, slot_own, kdst, vdst, kidst, ya_dst, ya_key, ksel, chunkmask):
        par = tcount[0] % 2
        tcount[0] += 1
        col_own = pos0
        s = load_norm(nt, xsrc, xkey, pos0)
        pt = pbf(2).rearrange("p (k t) -> p k t", t=128)
        qtb = QTB[par]
        qkey = ("QTB", par)
        inproj(nt, 0, 512, 0)
        inproj(nt, 512, 512, 1)
        qn = FS[0]
        qknorm(nt, 0, GQ, "GQ", qn, "FS0")
        q3 = qn[:nt, 0:512].rearrange("p (h d) -> p h d", d=64)
        qr = BS[0]
        qr3 = qr[:nt, :].rearrange("p (h d) -> p h d", d=64)
        rope(nt, s, q3[:, :, 0:32], q3[:, :, 32:64], qr3[:, :, 0:32], qr3[:, :, 32:64], 8, ["FS0"], "BS0", FS[2])
        inproj(nt, 1024, 512, 0)
        for p in range(4):
            tr(pt[:, p, :nt], qr[:nt, p * 128:(p + 1) * 128], IDB[:nt, :nt], ["BS0", "IDB"], [P(2)])
        cp(qtb[0:64, :, 0, :nt], pt[0:64, 0:4, :nt], (), [P(2), qkey])
        cp(qtb[64:128, :, 1, :nt], pt[64:128, 0:4, :nt], (), [P(2), qkey])
        kn = FS[3]
        qknorm(nt, 1, GK, "GK", kn, "FS3")
        k3 = kn[:nt, 0:512].rearrange("p (h d) -> p h d", d=64)
        kr = FS[4]
        kr3 = kr[:nt, 0:512].rearrange("p (h d) -> p h d", d=64)
        rope(nt, s, k3[:, :, 0:32], k3[:, :, 32:64], kr3[:, :, 0:32], kr3[:, :, 32:64], 8, ["FS3"], "FS4", FS[2])
        inproj(nt, 1536, 512, 1)
        dma("pool", kdst, kr[:nt, 0:512], ["FS4"], (), "ko")
        kb16 = BS[1]
        cp(kb16[:nt, :], kr[:nt, 0:512], ["FS4"], ["BS1"])
        for p in range(4):
            tr(pt[:, p, :nt], kb16[:nt, p * 128:(p + 1) * 128], IDB[:nt, :nt], ["BS1", "IDB"], [P(2)])
        cp(KT[:, :, col_own:col_own + nt], pt[:, 0:4, :nt], (), [P(2), ("KT", slot_own)])
        vf = FS[5]
        cp(vf[:nt, 0:512], PB[0][:nt, :], (), [P(0), "FS5"], eng="act")
        inproj(nt, 2048, 512, 0)
        dma("pool", vdst, vf[:nt, 0:512], ["FS5"], (), "vo")
        cp(VA[:nt, slot_own, :, 0:64], vf[:nt, 0:512].rearrange("p (h d) -> p h d", d=64), ["FS5"], [("V", slot_own)])
        zi = 6 if par == 0 else 8
        zas = FS[zi]
        zkey = "FS%d" % zi
        act(zas[:nt, 0:512], PB[1][:nt, :], AF.Silu, (), [P(1), zkey])
        inproj(nt, 2560, 72, 1)
        qir = BS[0]
        qir3 = qir[:nt, :].rearrange("p (h d) -> p h d", d=64)
        cp(FS[0][:nt, 0:512], PB[0][:nt, :], (), [P(0), "FS0"], eng="act")
        qf3 = FS[0][:nt, 0:512].rearrange("p (h d) -> p h d", d=64)
        rope(nt, s, qf3[:, :, 0:32], qf3[:, :, 32:64], qir3[:, :, 0:32], qir3[:, :, 32:64], 8, ["FS0"], "BS0", FS[2])
        for p in range(4):
            tr(pt[:, p, :nt], qir[:nt, p * 128:(p + 1) * 128], IDB[:nt, :nt], ["BS0", "IDB"], [P(2)])
        cp(QIT[:, :, :nt], pt[:, 0:4, :nt], (), [P(2), "QIT"])
        kif = FS[7]
        cp(kif[:nt, 0:72], PB[1][:nt, 0:72], (), [P(1), "FS7"], eng="act")
        ts(SM[:nt, 32:40], kif[:nt, 64:72], WSC, None, ALU.mult, None, ["FS7"], ["wis"])
        ki1 = kif[:nt, 0:64].rearrange("p (h d) -> p h d", d=64)
        kio = kif[:nt, 128:192].rearrange("p (h d) -> p h d", d=64)
        rope(nt, s, ki1[:, :, 0:32], ki1[:, :, 32:64], kio[:, :, 0:32], kio[:, :, 32:64], 1, ["FS7"], "FS7", FS[2])
        dma("pool", kidst, kif[:nt, 128:192], ["FS7"], (), "kio")
        ki2 = BS[1]
        cp(ki2[:nt, 0:64], kif[:nt, 128:192], ["FS7"], ["BS1"])
        cp(ki2[:nt, 64:128], kif[:nt, 128:192], ["FS7"], ["BS1"])
        tr(pt[:, 4, :nt], ki2[:nt, 0:128], IDB[:nt, :nt], ["BS1", "IDB"], [P(2)])
        cp(KI2[:, col_own:col_own + nt], pt[:, 4, :nt], (), [P(2), ("KI", slot_own)])
        attend_front(nt, blocks, ksel, chunkmask, par)
        return dict(nt=nt, s=s, par=par, blocks=blocks, zas=zas, zkey=zkey, ya_dst=ya_dst, ya_key=ya_key)

    def passA_back(c):
        nt, s, par, zas, zkey = c["nt"], c["s"], c["par"], c["zas"], c["zkey"]
        pt = pbf(2).rearrange("p (k t) -> p k t", t=128)
        attend_back(nt, c["blocks"], par)
        at = FS[0]
        akey = "FS0"
        for b in range(2):
            ov = PB[6 + b][:nt, 0:260].rearrange("p (h d) -> p h d", d=65)
            recip(SM[:nt, 40 + 4 * b:44 + 4 * b], ov[:, :, 64], (), [P(6 + b), "rden"])
            tt(at[:nt, 256 * b:256 * (b + 1)].rearrange("p (h d) -> p h d", d=64), ov[:, :, 0:64],
               SM[:nt, 40 + 4 * b:44 + 4 * b].unsqueeze(2).to_broadcast([nt, 4, 64]), ALU.mult,
               ["rden"], [P(6 + b), akey])
        mxa = BS[2]
        tt(mxa[:nt, :], at[:nt, 0:512], zas[:nt, 0:512], ALU.mult, [akey, zkey], ["BS2"])
        for p in range(4):
            tr(pt[:, p, :nt], mxa[:nt, p * 128:(p + 1) * 128], IDB[:nt, :nt], ["BS2", "IDB"], [P(2)])
        mxt = BS[3]
        mxt3 = mxt[:, :].rearrange("p (k t) -> p k t", t=128)
        cp(mxt3[:, :, :nt], pt[:, 0:4, :nt], (), [P(2), "BS3"])
        for hf in range(2):
            for kc in range(4):
                mm(PB[hf][:nt, :], mxt3[:, kc, :nt], WOUT[:, kc, hf * 512:(hf + 1) * 512], kc == 0, kc == 3,
                   ["BS3", "WOUT"], [P(hf)])
            tt(YT[:nt, hf * 512:(hf + 1) * 512], PB[hf][:nt, :], XT[s][:nt, hf * 512:(hf + 1) * 512], ALU.add,
               [("XT", s)], [P(hf), "YT"])
        dma("pool", c["ya_dst"], YT[:nt, :], ["YT"], [c["ya_key"]], "yo")

    def attend_front(nt, blocks, ksel, chunkmask, par):
        L = sum(b[1] for b in blocks)
        assert blocks[0][0] == 0
        mb = MB[par]
        mkey = ("MB", par)
        ibanks = [3, 4, 5]
        nb = 0
        c0 = 0
        while c0 < L:
            n = min(512, L - c0)
            kikeys = [("KI", bl[2]) for bl in blocks if bl[0] < c0 + n and bl[0] + bl[1] > c0]
            for h in range(8):
                p, e = divmod(h, 2)
                bank = ibanks[nb % 3]
                ri = (1, 3)[nb % 2]
                rbuf = FS[ri]
                rkey = "FS%d" % ri
                nb += 1
                mm(PB[bank][:nt, 0:n], QIT[64 * e:64 * e + 64, p, :nt], KI2[64 * e:64 * e + 64, c0:c0 + n],
                   True, True, ["QIT"] + kikeys, [P(bank)])
                act(rbuf[:nt, 0:n], PB[bank][:nt, 0:n], AF.Relu, (), [P(bank), rkey])
                if h == 0:
                    ts(SC[:nt, c0:c0 + n], rbuf[:nt, 0:n], SM[:nt, 32:33], None, ALU.mult, None,
                       [rkey, "wis"], ["SC"])
                else:
                    stt(SC[:nt, c0:c0 + n], rbuf[:nt, 0:n], SM[:nt, 32 + h:33 + h], SC[:nt, c0:c0 + n],
                        ALU.mult, ALU.add, [rkey, "wis", "SC"], ["SC"])
            c0 += n
        if chunkmask:
            mset(SC[0:64, L - 64:L], -1.0e4, ["SC"])
        LO = cfg.brk
        if L > ksel:
            thr = SM[:nt, 48:49]
            cnt = SM[:nt, 49:50]
            tmp = SM[:nt, 50:51]
            nth = [SM[:nt, 51:52], SM[:nt, 52:53]]
            use_act = False
            st = LO
            if not use_act:
                mset(thr, 0.0, ["thr"])
                for it in range(cfg.nbis):
                    ts(mb[:nt, 0:L], SC[:nt, 0:L], thr, None, ALU.is_ge, ALU.add, ["SC", "thr"], [mkey, "cnt"], accum=cnt)
                    ts(tmp, cnt, ksel - 0.5, st, ALU.is_ge, ALU.mult, ["cnt"], ["btmp"])
                    if it < cfg.nbis - 1:
                        stt(thr, tmp, -st / 2, thr, ALU.add, ALU.add, ["btmp", "thr"], ["thr"])
                    else:
                        stt(thr, tmp, -st, thr, ALU.add, ALU.add, ["btmp", "thr"], ["thr"])
                    st /= 2
            else:
                mset(nth[0], 0.0, ["nth0"])
                cc = 2.0 * ksel - 1.0 - L
                for it in range(cfg.nbis):
                    a_, b_ = it % 2, (it + 1) % 2
                    act(mb[:nt, 0:L], SC[:nt, 0:L], AF.Sign, ["SC", "nth%d" % a_], [mkey, "cnt"], bias=nth[a_], accum=cnt)
                    act(tmp, cnt, AF.Sign, ["cnt"], ["btmp"], bias=0.5 - cc)
                    act(nth[b_], tmp, AF.Identity, ["btmp", "nth%d" % a_], ["nth%d" % b_], bias=nth[a_], scale=-st / 2)
                    st /= 2
                fin = cfg.nbis % 2
                st_last = st * 2
                act(thr, nth[fin], AF.Identity, ["nth%d" % fin], ["thr"], bias=-st_last / 2, scale=-1.0)
            ts(mb[:nt, 0:L], SC[:nt, 0:L], thr, NEG, ALU.is_lt, ALU.mult, ["SC", "thr"], [mkey])
        else:
            ts(mb[:nt, 0:L], SC[:nt, 0:L], -LO, NEG, ALU.is_lt, ALU.mult, ["SC"], [mkey])

    def attend_back(nt, blocks, par):
        mb = MB[par]
        mkey = ("MB", par)
        qtb = QTB[par]
        qkey = ("QTB", par)
        qbank = [3, 4, 0, 1]
        ptbufs = [4, 5, 0, 1]
        units = [(bi, half) for bi in range(len(blocks)) for half in range(2)]
        first = [True, True]
        nu = len(units)

        def qk_exp(u):
            bi, half = units[u]
            c0, kb, slot = blocks[bi]
            bank = qbank[u % 4]
            for pp in range(2):
                p = 2 * half + pp
                o3 = PB[bank][:kb, pp * 2 * nt:(pp + 1) * 2 * nt].rearrange("p (e t) -> p e t", e=2)
                mm(o3, KT[:, p, c0:c0 + kb], qtb[:, p, :, :nt], True, False, [("KT", slot), qkey], [P(bank)])
                mm(o3, mb[:nt, c0:c0 + kb], I2[:nt, :, :nt], False, True, [mkey, "I2"], [P(bank)])
            pi = ptbufs[u % 4]
            act(BS[pi][:kb, 0:4 * nt], PB[bank][:kb, 0:4 * nt], AF.Exp, (), [P(bank), "BS%d" % pi], scale=0.125)

        def pv(u):
            bi, half = units[u]
            c0, kb, slot = blocks[bi]
            pi = ptbufs[u % 4]
            ptb = BS[pi]
            for pp in range(2):
                for e in range(2):
                    h = 2 * (2 * half + pp) + e
                    ob = 6 + h // 4
                    hh = h % 4
                    st_ = first[h // 4]
                    first[h // 4] = False
                    mm(PB[ob][:nt, hh * 65:(hh + 1) * 65], ptb[:kb, (pp * 2 + e) * nt:(pp * 2 + e + 1) * nt],
                       VA[:kb, slot, h, :], st_, (u == nu - 1 and hh == 3), ["BS%d" % pi, ("V", slot)], [P(ob)], skip=True)

        LAG = 2
        for u in range(nu):
            qk_exp(u)
            if u >= LAG:
                pv(u - LAG)
        for u in range(max(0, nu - LAG), nu):
            pv(u)

    def passB(l, nt, xsrc, xkey, ya_src, ya_key, out_dsts, out_keys):
        s = load_norm(nt, xsrc, xkey, None)
        dma("sp", YT[:nt, :], ya_src, [ya_key], ["YT"], "yi")
        O = 64
        u3 = PB[3][:, :].rearrange("p (b t) -> p b t", t=128)
        for b in range(4):
            for kc in range(8):
                mm(u3[:, b, :nt], WBUF[:, kc, b * 128:(b + 1) * 128], HNT[:, kc, :nt], kc == 0, kc == 7,
                   ["HNT", "WBUF"], [P(3)])
        cp(UEXT[:, :, 3:3 + nt], u3[:, :, :nt], (), [P(3), "UEXT"], eng="act")
        inproj(nt, 512, 512, 0)
        cp(VM[:nt, :, 0:128], PB[0][:nt, :].rearrange("p (h d) -> p h d", d=128), (), [P(0), "VM"])
        inproj(nt, 1024, 512, 1)
        sgo = FS[0]
        act(sgo[:nt, 0:512], PB[1][:nt, :], AF.Sigmoid, (), [P(1), "FS0"])
        inproj(nt, 1536, 512, 0)
        szm = FS[1]
        act(szm[:nt, 0:512], PB[0][:nt, :], AF.Silu, (), [P(0), "FS1"])
        inproj(nt, 2048, 8, 1)
        tt(SM[:nt, O + 0:O + 4], PB[1][:nt, 0:4], GBI[:nt, :], ALU.add, ["GBI"], [P(1), "igb"])
        tt(SM[:nt, O + 4:O + 8], PB[1][:nt, 4:8], GBF[:nt, :], ALU.add, ["GBF"], [P(1), "fgb"])
        act(SM[:nt, O + 8:O + 12], SM[:nt, O + 4:O + 8], AF.Exp, ["fgb"], ["e1"], scale=-1.0)
        act(SM[:nt, O + 12:O + 16], SM[:nt, O + 8:O + 12], AF.Ln, ["e1"], ["nlf"], bias=1.0)
        mm(PB[5][0:4, 0:nt], SM[:nt, O + 0:O + 4], IDF[:nt, :nt], True, False, ["igb", "IDF"], [P(5)])
        mm(PB[5][0:4, 0:nt], SM[:nt, O + 12:O + 16], TRI[:nt, :nt], False, True, ["nlf", "TRI"], [P(5)])
        mm(PB[5][0:4, 128:128 + nt], SM[:nt, O + 12:O + 16], TRI[:nt, :nt], True, True, ["nlf", "TRI"], [P(5)])
        MU, NBC, MUN, DLT, DEC, NMU, CB, AMX = [SG[0:4, i:i + 1] for i in range(8)]
        arow = SR[0:4, 0, :nt]
        orow = SR[0:4, 1, :nt]
        crow = SR[0:4, 2, :nt]
        ts(arow, PB[5][0:4, 0:nt], NBC, None, ALU.add, None, ["NBC"], [P(5), "arow"])
        red(AMX, arow, ALU.max, ["arow"], ["AMX"])
        tt(MUN, AMX, MU, ALU.max, ["AMX", "MU"], ["MUN"])
        tt(DLT, MU, MUN, ALU.subtract, ["MU", "MUN"], ["DLT"])
        act(DEC, DLT, AF.Exp, ["DLT"], ["DEC"])
        ts(NMU, MUN, -1.0, None, ALU.mult, None, ["MUN"], ["NMU"])
        tt(CB, NBC, MUN, ALU.subtract, ["NBC", "MUN"], ["CB"])
        act(orow, arow, AF.Exp, ["arow", "NMU"], ["orow"], bias=NMU)
        act(crow, PB[5][0:4, 128:128 + nt], AF.Exp, ["CB"], [P(5), "crow"], bias=CB)
        tt(NBC, NBC, PB[5][0:4, 128 + nt - 1:128 + nt], ALU.add, ["NBC"], [P(5), "NBC"])
        cp(MU, MUN, ["MUN"], ["MU"])
        ts(DD[:, :], I4[:, :], DEC, None, ALU.mult, None, ["I4", "DEC"], ["DD"])
        mm(PB[5][:nt, 256:260], orow, I4[:, :], True, True, ["orow", "I4"], [P(5)])
        mm(PB[5][:nt, 260:264], crow, I4[:, :], True, True, ["crow", "I4"], [P(5)])
        mm(PB[5][:, 264:268], ONES4[:, :], DD[:, :], True, True, ["ONES4", "DD"], [P(5)])
        cp(SM[:nt, O + 16:O + 24], PB[5][:nt, 256:264], (), [P(5), "wc"])
        cp(SM[:, O + 24:O + 28], PB[5][:, 264:268], (), [P(5), "dbc"])
        WCO = O + 16
        CLO = O + 20
        DBO = O + 24
        cacc = FS[2]
        ca3 = cacc[:, 0:512].rearrange("p (b t) -> p b t", t=128)
        for b in range(4):
            ts(ca3[:, b, :nt], UEXT[:, b, 0:nt], CVW[:, b, 0:1], CVB[:, b:b + 1], ALU.mult, ALU.add,
               ["UEXT", "CVW", "CVB"], ["FS2"])
            for j in range(1, 4):
                stt(ca3[:, b, :nt], UEXT[:, b, j:j + nt], CVW[:, b, j:j + 1], ca3[:, b, :nt], ALU.mult, ALU.add,
                    ["UEXT", "CVW", "FS2"], ["FS2"])
        ct = BS[0]
        ct3 = ct[:, :].rearrange("p (b t) -> p b t", t=128)
        act(ct3[:, :, :nt], ca3[:, :, :nt], AF.Silu, ["FS2"], ["BS0"])
        cp(UEXT[:, :, 0:3], UEXT[:, :, nt:nt + 3], ["UEXT"], ["UEXT"])
        q3p = PB[4][:, :].rearrange("p (h t) -> p h t", t=128)
        for h in range(4):
            mm(q3p[:, h, :nt], WQK[:, 0, h, :], ct3[:, h, :nt], True, True, ["WQK", "BS0"], [P(4)])
        qmt = BS[1]
        qmt3 = qmt[:, :].rearrange("p (h t) -> p h t", t=128)
        ts(qmt3[:, :, :nt], q3p[:, :, :nt], 128.0 ** -0.5, None, ALU.mult, None, (), [P(4), "BS1"])
        for h in range(4):
            mm(q3p[:, h, :nt], WQK[:, 1, h, :], ct3[:, h, :nt], True, True, ["WQK", "BS0"], [P(4)])
        kmt = BS[2]
        kmt3 = kmt[:, :].rearrange("p (h t) -> p h t", t=128)
        cp(kmt3[:, :, :nt], q3p[:, :, :nt], (), [P(4), "BS2"])
        k3p = PB[3][:nt, :].rearrange("p (h e) -> p h e", e=128)
        for h in range(4):
            mm(k3p[:, h, :], ct3[:, h, :nt], WQK[:, 1, h, :], True, True, ["WQK", "BS0"], [P(3)])
        kw = BS[3]
        kw3 = kw[:nt, :].rearrange("p (h e) -> p h e", e=128)
        for h in range(4):
            ts(kw3[:, h, :], k3p[:, h, :], SM[:nt, WCO + h:WCO + h + 1], None, ALU.mult, None, ["wc"], [P(3), "BS3"])
        s3p = PB[4][:nt, :].rearrange("p (h t) -> p h t", t=128)
        for h in range(4):
            mm(s3p[:, h, :nt], kmt3[:, h, :nt], qmt3[:, h, :nt], True, True, ["BS2", "BS1"], [P(4)])
        pmt = BS[4]
        pmt3 = pmt[:nt, :].rearrange("p (h t) -> p h t", t=128)
        for h in range(4):
            stt(pmt3[:, h, :nt], s3p[:, h, :nt], SM[:nt, WCO + h:WCO + h + 1], TRI[:nt, :nt], ALU.mult, ALU.mult,
                ["wc", "TRI"], [P(4), "BS4"])
        for h in range(4):
            ts(CNS[:, h, :], CNS[:, h, :], SM[:, DBO + h:DBO + h + 1], None, ALU.mult, None, ["dbc", "CNS"], ["CNS"])
        cp(CNB[:, :, :], CNS[:, :, :], ["CNS"], ["CNB"], eng="act")
        for h in range(4):
            b, hh = divmod(h, 2)
            o = PB[6 + b][:nt, hh * 129:(hh + 1) * 129]
            mm(o, pmt3[:, h, :nt], VM[:nt, h, :], True, False, ["BS4", "VM"], [P(6 + b)])
            mm(o, qmt3[:, h, :nt], CNB[:, h, :], False, True, ["BS1", "CNB"], [P(6 + b)])
        for h in range(4):
            b, hh = divmod(h, 2)
            mm(PB[b][:, hh * 129:(hh + 1) * 129], kw3[:, h, :], VM[:nt, h, :], True, True, ["BS3", "VM"], [P(b)])
        for b in range(2):
            tt(CNS[:, 2 * b:2 * b + 2, :], CNS[:, 2 * b:2 * b + 2, :],
               PB[b][:, 0:258].rearrange("p (h d) -> p h d", d=129), ALU.add, ["CNS"], [P(b), "CNS"])
        hb = FS[3]
        for b in range(2):
            nd = PB[6 + b][:nt, 0:258].rearrange("p (h d) -> p h d", d=129)
            tt(SM[:nt, O + 28 + 2 * b:O + 30 + 2 * b], nd[:, :, 128], SM[:nt, CLO + 2 * b:CLO + 2 + 2 * b], ALU.max,
               ["wc"], [P(6 + b), "dmax"])
            stt(SM[:nt, O + 28 + 2 * b:O + 30 + 2 * b], nd[:, :, 128], -1.0, SM[:nt, O + 28 + 2 * b:O + 30 + 2 * b],
                ALU.mult, ALU.max, ["dmax"], [P(6 + b), "dmax"])
            recip(SM[:nt, O + 32 + 2 * b:O + 34 + 2 * b], SM[:nt, O + 28 + 2 * b:O + 30 + 2 * b], ["dmax"], ["rdm"])
            tt(hb[:nt, 256 * b:256 * (b + 1)].rearrange("p (h d) -> p h d", d=128), nd[:, :, 0:128],
               SM[:nt, O + 32 + 2 * b:O + 34 + 2 * b].unsqueeze(2).to_broadcast([nt, 2, 128]), ALU.mult,
               ["rdm"], [P(6 + b), "FS3"])
        sq = FS[4]
        act(sq[:nt, 0:512], hb[:nt, 0:512], AF.Square, ["FS3"], ["FS4"])
        red(SM[:nt, O + 36:O + 40], sq[:nt, 0:512].rearrange("p (h d) -> p h d", d=128), ALU.add, ["FS4"], ["hss"])
        act(SM[:nt, O + 40:O + 44], SM[:nt, O + 36:O + 40], AF.Ln, ["hss"], ["hss1"], bias=EPS, scale=1.0 / 128)
        act(SM[:nt, O + 44:O + 48], SM[:nt, O + 40:O + 44], AF.Exp, ["hss1"], ["hrstd"], scale=-0.5)
        hb3 = hb[:nt, 0:512].rearrange("p (h d) -> p h d", d=128)
        tt(hb3, hb3, SM[:nt, O + 44:O + 48].unsqueeze(2).to_broadcast([nt, 4, 128]), ALU.mult, ["hrstd", "FS3"], ["FS3"])
        tt(hb[:nt, 0:512], hb[:nt, 0:512], GH[:nt, :], ALU.mult, ["FS3", "GH"], ["FS3"])
        tt(hb[:nt, 0:512], hb[:nt, 0:512], sgo[:nt, 0:512], ALU.mult, ["FS3", "FS0"], ["FS3"])
        pt = pbf(2).rearrange("p (k t) -> p k t", t=128)
        for b in range(4):
            tr(pt[:nt, b, :], ct3[:, b, :nt], IDB[:, :], ["BS0", "IDB"], [P(2)])
        t2 = FS[4]
        tt(t2[:nt, 0:512].rearrange("p (k t) -> p k t", t=128), pt[:nt, 0:4, :],
           SKP[:nt, :].rearrange("p (k t) -> p k t", t=128), ALU.mult, ["SKP"], [P(2), "FS4"])
        tt(hb[:nt, 0:512], hb[:nt, 0:512], t2[:nt, 0:512], ALU.add, ["FS3", "FS4"], ["FS3"])
        mxb = BS[5]
        tt(mxb[:nt, :], hb[:nt, 0:512], szm[:nt, 0:512], ALU.mult, ["FS3", "FS1"], ["BS5"])
        for p in range(4):
            tr(pt[:, p, :nt], mxb[:nt, p * 128:(p + 1) * 128], IDB[:nt, :nt], ["BS5", "IDB"], [P(2)])
        mxt = BS[1]
        mxt3 = mxt[:, :].rearrange("p (k t) -> p k t", t=128)
        cp(mxt3[:, :, :nt], pt[:, 0:4, :nt], (), [P(2), "BS1"])
        for hf in range(2):
            for kc in range(4):
                mm(PB[hf][:nt, :], mxt3[:, kc, :nt], WOUT[:, kc, hf * 512:(hf + 1) * 512], kc == 0, kc == 3,
                   ["BS1", "WOUT"], [P(hf)])
            tt(YT[:nt, hf * 512:(hf + 1) * 512], PB[hf][:nt, :], YT[:nt, hf * 512:(hf + 1) * 512], ALU.add,
               ["YT"], [P(hf), "YT"])
        for i, (dst, key) in enumerate(zip(out_dsts, out_keys)):
            dma("pool", dst, YT[:nt, :], ["YT"], [key] if key else (), "xo%d" % i)

    def state_out(Cd, nd_, md, cvd, chs):
        dma("pool", Cd.rearrange("h k v -> k h v"), CNS[:, :, 0:128], ["CNS"], (), chs + "C")
        dma("pool", nd_.rearrange("h k -> k h"), CNS[:, :, 128], ["CNS"], (), chs + "n", nonc=True)
        tt(SG[0:4, 8:9], SG[0:4, 0:1], SG[0:4, 1:2], ALU.subtract, ["MU", "NBC"], ["MOUT"])
        dma("pool", md.rearrange("(h o) -> h o", o=1), SG[0:4, 8:9], ["MOUT"], (), chs + "m", nonc=True)
        for j in range(3):
            dma("pool", cvd[j].rearrange("(b f) -> f b", f=128), UEXT[:, :, j], ["UEXT"], (), chs + "v%d" % j, nonc=True)

    for l in range(DEPTH):
        load_layer_params(l)
        load_weights_A(l)
        xin_p = XPs[l % 2]
        xout_p = XPs[(l + 1) % 2]
        xin_s = XSs[l % 2]
        xout_s = XSs[(l + 1) % 2]
        barrier(ALIAS)
        mset(QTB[1][:, :, :, :], 0.0, [("QTB", 1)])
        prev = None
        for ti in range(NT + 1):
            if ti == 0:
                nt, pos0 = 16, 0
                xsrc = meta if l == 0 else xin_p[0:16, :]
                blocks = [(0, 16, 0)]
                cm = False
            else:
                fi = ti - 1
                nt, pos0 = 128, 16 + 128 * fi
                xsrc = xp[128 * fi:128 * fi + 128, :] if l == 0 else xin_p[pos0:pos0 + 128, :]
                blocks = [(0, 16, 0)] + [(16 + 128 * j, 128, 1 + j) for j in range(fi + 1)]
                cm = True
            ctx = passA_front(l, nt, xsrc, ("XP", l % 2, ti), pos0, blocks, ti,
                              pk[l, pos0:pos0 + nt, :], pv[l, pos0:pos0 + nt, :], pki[l, pos0:pos0 + nt, :],
                              YAp[pos0:pos0 + nt, :], ("YAp", ti), cfg.ksel_p, cm)
            if prev is not None:
                passA_back(prev)
            prev = ctx
        passA_back(prev)
        for b in range(NSB):
            for g in range(0, NCB, 4):
                ng = min(4, NCB - g)
                kis = FS[4 + (g // 4) % 2]
                kkey = "FS%d" % (4 + (g // 4) % 2)
                kdkey = "BS%d" % ((g // 4) % 2)
                dma("sp", kis[:, 0:64 * ng].rearrange("p (j d) -> p j d", d=64),
                    cki[l, b, 128 * g:128 * (g + ng), :].rearrange("(j p) d -> p j d", p=128), (), [kkey],
                    "kc%d" % ((g // 4) % 2))
                kd = BS[(g // 4) % 2]
                kd4 = kd[:, 0:128 * ng].rearrange("p (j e d) -> p j e d", e=2, d=64)
                k3 = kis[:, 0:64 * ng].rearrange("p (j d) -> p j d", d=64)
                cp(kd4[:, :, 0, :], k3, [kkey], [kdkey])
                cp(kd4[:, :, 1, :], k3, [kkey], [kdkey])
                pt = pbf(2).rearrange("p (k t) -> p k t", t=128)
                for jj in range(ng):
                    tr(pt[:, jj, :], kd[:, 128 * jj:128 * (jj + 1)], IDB[:, :], [kdkey, "IDB"], [P(2)])
                cp(KI2[:, 16 + 128 * g:16 + 128 * (g + ng)], pbf(2)[:, 0:128 * ng], (), [P(2)] + [("KI", 1 + g + q) for q in range(ng)], eng="act")
            xsrc = xs[b] if l == 0 else xin_s[b]
            pos0 = 16 + PAST
            blocks = [(0, 16, 0)] + [(16 + 128 * j, 128, 1 + j) for j in range(NCB)] + [(pos0, 32, NSLOT - 1)]
            ctx = passA_front(l, 32, xsrc, ("XS", l % 2, b), pos0, blocks, NSLOT - 1,
                              sk[l, b], sv[l, b], ski[l, b], YAs[b], ("YAs", b), cfg.ksel_s, False)
            for j in range(NCB):
                kst = FS[(1, 3)[j % 2]]
                vst = FS[(4, 5)[j % 2]]
                kkst = "FS%d" % ((1, 3)[j % 2])
                kvst = "FS%d" % ((4, 5)[j % 2])
                dma("sp", kst[:, 0:512], ck[l, b, 128 * j:128 * (j + 1), :], (), [kkst], "ks%d" % (j % 2))
                dma("sp", vst[:, 0:512], cv[l, b, 128 * j:128 * (j + 1), :], (), [kvst], "vs%d" % (j % 2))
                pf = PB[5][:, :].rearrange("p (k t) -> p k t", t=128)
                for p in range(4):
                    tr(pf[:, p, :], kst[:, p * 128:(p + 1) * 128], IDF[:, :], [kkst, "IDF"], [P(5)])
                cp(KT[:, :, 16 + 128 * j:16 + 128 * (j + 1)], pf[:, :, :], (), [P(5), ("KT", 1 + j)], eng="act")
                cp(VA[:, 1 + j, :, 0:64], vst[:, 0:512].rearrange("p (h d) -> p h d", d=64), [kvst], [("V", 1 + j)], eng="pool")
            passA_back(ctx)
        barrier(ALIAS)
        load_weights_B(l)
        mset(VM[:, :, 128:129], 1.0, ["VM"])
        mset(CNS[:, :, :], 0.0, ["CNS"])
        mset(UEXT[:, :, 0:3], 0.0, ["UEXT"])
        mset(SG[0:4, 0:2], 0.0, ["MU", "NBC"])
        last = (l == DEPTH - 1)
        for ti in range(NT + 1):
            if ti == 0:
                nt, pos0 = 16, 0
                xsrc = meta if l == 0 else xin_p[0:16, :]
                dsts, keys = ([], []) if last else ([xout_p[0:16, :]], [("XP", (l + 1) % 2, 0)])
            else:
                fi = ti - 1
                nt, pos0 = 128, 16 + 128 * fi
                xsrc = xp[128 * fi:128 * fi + 128, :] if l == 0 else xin_p[pos0:pos0 + 128, :]
                if last:
                    dsts, keys = [yp[128 * fi:128 * fi + 128, :]], [None]
                else:
                    dsts, keys = [xout_p[pos0:pos0 + 128, :]], [("XP", (l + 1) % 2, ti)]
            passB(l, nt, xsrc, ("XP", l % 2, ti), YAp[pos0:pos0 + nt, :], ("YAp", ti), dsts, keys)
        state_out(pC[l], pn[l], pm[l], pconv[l], "p")
        for b in range(NSB):
            dma("sp", CNS[:, :, 0:128], stC[l, b].rearrange("h k v -> k h v"), (), ["CNS"], "si0")
            dma("sp", CNS[:, :, 128], stn[l, b].rearrange("h k -> k h"), (), ["CNS"], "si1", nonc=True)
            dma("sp", SG[0:4, 0:1], stm[l, b].rearrange("(h o) -> h o", o=1), (), ["MU"], "si2", nonc=True)
            for j in range(3):
                dma("sp", UEXT[:, :, j], stconv[l, b, j].rearrange("(b f) -> f b", f=128), (), ["UEXT"], "si3%d" % j, nonc=True)
            mset(SG[0:4, 1:2], 0.0, ["NBC"])
            xsrc = xs[b] if l == 0 else xin_s[b]
            if last:
                dsts, keys = [ys[b]], [None]
            else:
                dsts, keys = [xout_s[b]], [("XS", (l + 1) % 2, b)]
            passB(l, 32, xsrc, ("XS", l % 2, b), YAs[b], ("YAs", b), dsts, keys)
            state_out(sC[l, b], sn[l, b], sm[l, b], sconv[l, b], "s")

    allch = list(S.chan_last.keys())
    S.add("sp", None, [], [], None)
    fin = S.ops[-1]
    fin.raw = set(S.chan_last[c] for c in allch)

    S.finalize()
    engs = ["pe", "act", "dve", "pool", "sp"]
    esem = {}
    for e in engs:
        n = (S.eng_cnt.get(e, 0) + SEM_CH - 1) // SEM_CH
        esem[e] = [es.enter_context(nc.semaphore("s_%s_%d" % (e, i))) for i in range(max(n, 1))]
    csem = {}
    for c, n in S.chan_cnt.items():
        k = (16 * n + SEM_CH - 1) // SEM_CH
        csem[c] = [es.enter_context(nc.semaphore("c_%s_%d" % (c, i))) for i in range(max(k, 1))]
    DCH = SEM_CH // 16 * 16

    def sem_of(t, v):
        if t[0] == "e":
            return esem[t[1]][(v - 1) // SEM_CH], (v - 1) % SEM_CH + 1
        return csem[t[1]][(v - 1) // DCH], (v - 1) % DCH + 1

    by_eng = {e: [] for e in engs}
    for op in S.ops:
        by_eng[op.eng].append(op)
    block = es.enter_context(nc.Block())

    def emit(ename, h):
        for op in by_eng[ename]:
            for (t, v) in op.waits:
                sem, val = sem_of(t, v)
                h.wait_ge(sem, val)
            if op.fn is None:
                continue
            ins = op.fn(h)
            if op.chan is not None:
                sem, val = sem_of(("c", op.chan), 16 * op.chan_n)
                ins.then_inc(sem, 16)
            elif op.inc:
                sem, val = sem_of(("e", ename), op.incval)
                ins.then_inc(sem, 1)

    @block.tensor
    def _(h):
        emit("pe", h)

    @block.scalar
    def _(h):
        emit("act", h)

    @block.vector
    def _(h):
        emit("dve", h)

    @block.gpsimd
    def _(h):
        emit("pool", h)

    @block.sync
    def _(h):
        emit("sp", h)

    es.close()
    return nc, len(S.ops)


def const_tables(cfg):
    half = 32
    freqs = (np.float32(10000.0) ** (-np.arange(half, dtype=np.float32) / np.float32(half))).astype(np.float32)
    pos = np.arange(cfg.kc, dtype=np.float32)
    ang = (pos[:, None] * freqs[None, :]).astype(np.float32)
    rope = np.concatenate([np.cos(ang.astype(np.float64)), np.sin(ang.astype(np.float64))], axis=1).astype(np.float32)
    tri = np.triu(np.ones((128, 128), np.float32))
    return {
        "c_ident": np.eye(128, dtype=np.float32),
        "c_tri": tri,
        "c_rope": rope,
        "c_i4": np.eye(4, dtype=np.float32),
        "c_ones4": np.ones((4, 128), np.float32),
    }


WNAMES = ["meta", "norm_g", "w_in", "q_norm_g", "k_norm_g", "conv_w", "conv_b", "wq_m", "wk_m",
          "b_igate", "b_fgate", "head_norm_g", "skip", "w_out"]


def run(cfg, inputs, ncores):
    nc, nops = build(cfg)
    consts = const_tables(cfg)
    f = lambda a: np.ascontiguousarray(np.asarray(a, dtype=np.float32))
    nsb = cfg.nsb
    in_maps = []
    for c in range(ncores):
        m = {
            "xp": f(inputs["x_prompt"][c]),
            "xs": f(inputs["x_sample"][nsb * c:nsb * (c + 1)]),
            "ck": f(inputs["cache_k"][:, nsb * c:nsb * (c + 1)]).reshape(cfg.depth, nsb, cfg.past, 512),
            "cv": f(inputs["cache_v"][:, nsb * c:nsb * (c + 1)]).reshape(cfg.depth, nsb, cfg.past, 512),
            "cki": f(inputs["cache_kidx"][:, nsb * c:nsb * (c + 1)]),
            "stC": f(inputs["state_C"][:, nsb * c:nsb * (c + 1)]),
            "stn": f(inputs["state_n"][:, nsb * c:nsb * (c + 1)]),
            "stm": f(inputs["state_m"][:, nsb * c:nsb * (c + 1)]),
            "stconv": f(inputs["state_conv"][:, nsb * c:nsb * (c + 1)]),
        }
        for wn in WNAMES:
            m[wn] = f(inputs[wn])
        m.update(consts)
        in_maps.append(m)
    res = run_bass_kernel_spmd(nc, in_maps, core_ids=list(range(ncores)))
    R = res.results
    D = cfg.depth

    def cat_b(name):
        return np.stack([R[c][name] for c in range(ncores)], axis=1)

    def cat_s(name):
        return np.concatenate([R[c][name] for c in range(ncores)], axis=1)

    y_prompt = np.stack([R[c]["yp"] for c in range(ncores)], axis=0)
    y_sample = np.concatenate([R[c]["ys"] for c in range(ncores)], axis=0)
    pk = cat_b("pk").reshape(D, ncores, cfg.npos, 8, 64)
    pv = cat_b("pv").reshape(D, ncores, cfg.npos, 8, 64)
    pki = cat_b("pki")
    sk = cat_s("sk").reshape(D, ncores * nsb, 32, 8, 64)
    sv = cat_s("sv").reshape(D, ncores * nsb, 32, 8, 64)
    return (y_prompt, y_sample, pk, pv, pki, cat_b("pC"), cat_b("pn"), cat_b("pm"), cat_b("pconv"),
            sk, sv, cat_s("ski"), cat_s("sC"), cat_s("sn"), cat_s("sm"), cat_s("sconv"))


def kernel(**inputs):
    cfg = Cfg()
    outs = run(cfg, inputs, 8)
    return tuple(np.ascontiguousarray(o.astype(np.float32)) for o in outs)
```

```python
import numpy as np
from contextlib import ExitStack
import concourse.bass as bass
import concourse.mybir as mybir
from concourse.bass_utils import run_bass_kernel_spmd

F32 = mybir.dt.float32
BF16 = mybir.dt.bfloat16
AF = mybir.ActivationFunctionType
ALU = mybir.AluOpType
AX = mybir.AxisListType

EPS = 1e-6
NEG = -30000.0
SEM_CH = 24000


class Cfg:
    def __init__(self, ntiles=32, past=4096, depth=4, ksel_p=256, ksel_s=256, nsb=4, nbis=17, brk=4.0):
        self.ntiles, self.past, self.depth = ntiles, past, depth
        self.ksel_p, self.ksel_s, self.nsb, self.nbis, self.brk = ksel_p, ksel_s, nsb, nbis, brk
        self.npos = 16 + 128 * ntiles
        self.ncb = past // 128
        self.kc = 16 + past + 32
        self.cw = max(self.npos, self.kc)
        self.nslot = 2 + max(ntiles, self.ncb)


class Op:
    __slots__ = ("eng", "fn", "raw", "oth", "chan", "chan_n", "inc", "waits", "incval")


class Sched:
    def __init__(self):
        self.ops = []
        self.last_w = {}
        self.readers = {}
        self.chan_last = {}
        self.chan_cnt = {}

    def add(self, eng, fn, r=(), w=(), chan=None):
        op = Op()
        op.eng, op.fn, op.chan = eng, fn, chan
        idx = len(self.ops)
        raw, oth = set(), set()
        for k in r:
            lw = self.last_w.get(k)
            if lw is not None:
                raw.add(lw)
        for k in w:
            lw = self.last_w.get(k)
            if lw is not None:
                oth.add(lw)
            for rd in self.readers.get(k, {}).values():
                oth.add(rd)
        if chan is not None:
            pl = self.chan_last.get(chan)
            if pl is not None:
                raw.add(pl)
            self.chan_last[chan] = idx
            self.chan_cnt[chan] = self.chan_cnt.get(chan, 0) + 1
            op.chan_n = self.chan_cnt[chan]
        op.raw, op.oth = raw, oth
        rk = ("c", chan) if chan is not None else ("e", eng)
        for k in r:
            self.readers.setdefault(k, {})[rk] = idx
        for k in w:
            self.last_w[k] = idx
            self.readers[k] = {}
        self.ops.append(op)
        return idx

    def finalize(self):
        ops = self.ops
        seq = {}
        for i, op in enumerate(ops):
            if op.chan is None:
                seq[op.eng] = seq.get(op.eng, 0) + 1
                op.incval = seq[op.eng]
        waited = {}
        needed = set()
        for op in ops:
            deps = {}
            for j in op.raw:
                deps[j] = True
            for j in op.oth:
                deps.setdefault(j, False)
            tg = {}
            for j, is_raw in deps.items():
                o = ops[j]
                if o.chan is None and o.eng == op.eng and not is_raw:
                    continue
                if o.chan is not None:
                    t, v = ("c", o.chan), 16 * o.chan_n
                else:
                    t, v = ("e", o.eng), o.incval
                if v > tg.get(t, (0, None))[0]:
                    tg[t] = (v, j)
            wl = []
            wd = waited.setdefault(op.eng, {})
            for t, (v, j) in tg.items():
                if v > wd.get(t, 0):
                    wd[t] = v
                    wl.append((t, j))
                    if t[0] == "e":
                        needed.add(j)
            op.waits = wl
        cnt = {}
        for i, op in enumerate(ops):
            op.inc = False
            if op.chan is None and i in needed:
                cnt[op.eng] = cnt.get(op.eng, 0) + 1
                op.inc = True
                op.incval = cnt[op.eng]
        self.eng_cnt = cnt
        for op in ops:
            wl = []
            for (t, j) in op.waits:
                o = ops[j]
                wl.append((t, 16 * o.chan_n if t[0] == "c" else o.incval))
            op.waits = wl


def build(cfg):
    NT, PAST, DEPTH = cfg.ntiles, cfg.past, cfg.depth
    NPOS, NCB, KCOLS, CW, NSLOT, NSB = cfg.npos, cfg.ncb, cfg.kc, cfg.cw, cfg.nslot, cfg.nsb
    nc = bass.Bass("TRN2", target_bir_lowering=False)
    S = Sched()

    def din(name, shape):
        return nc.dram_tensor(name, list(shape), F32, kind="ExternalInput").ap()

    def dout(name, shape):
        return nc.dram_tensor(name, list(shape), F32, kind="ExternalOutput").ap()

    def dscr(name, shape):
        return nc.dram_tensor(name, list(shape), F32, kind="Internal").ap()

    xp = din("xp", [128 * NT, 1024])
    xs = din("xs", [NSB, 32, 1024])
    ck = din("ck", [DEPTH, NSB, PAST, 512])
    cv = din("cv", [DEPTH, NSB, PAST, 512])
    cki = din("cki", [DEPTH, NSB, PAST, 64])
    stC = din("stC", [DEPTH, NSB, 4, 128, 128])
    stn = din("stn", [DEPTH, NSB, 4, 128])
    stm = din("stm", [DEPTH, NSB, 4])
    stconv = din("stconv", [DEPTH, NSB, 3, 512])
    meta = din("meta", [16, 1024])
    norm_g = din("norm_g", [DEPTH, 1024])
    w_in = din("w_in", [DEPTH, 1024, 4688])
    q_norm_g = din("q_norm_g", [DEPTH, 64])
    k_norm_g = din("k_norm_g", [DEPTH, 64])
    conv_w = din("conv_w", [DEPTH, 4, 512])
    conv_b = din("conv_b", [DEPTH, 512])
    wq_m = din("wq_m", [DEPTH, 4, 128, 128])
    wk_m = din("wk_m", [DEPTH, 4, 128, 128])
    b_igate = din("b_igate", [DEPTH, 4])
    b_fgate = din("b_fgate", [DEPTH, 4])
    head_norm_g = din("head_norm_g", [DEPTH, 512])
    skip = din("skip", [DEPTH, 512])
    w_out = din("w_out", [DEPTH, 1024, 1024])
    c_ident = din("c_ident", [128, 128])
    c_tri = din("c_tri", [128, 128])
    c_rope = din("c_rope", [KCOLS, 64])
    c_i4 = din("c_i4", [4, 4])
    c_ones4 = din("c_ones4", [4, 128])

    yp = dout("yp", [128 * NT, 1024])
    ys = dout("ys", [NSB, 32, 1024])
    pk = dout("pk", [DEPTH, NPOS, 512])
    pv = dout("pv", [DEPTH, NPOS, 512])
    pki = dout("pki", [DEPTH, NPOS, 64])
    pC = dout("pC", [DEPTH, 4, 128, 128])
    pn = dout("pn", [DEPTH, 4, 128])
    pm = dout("pm", [DEPTH, 4])
    pconv = dout("pconv", [DEPTH, 3, 512])
    sk = dout("sk", [DEPTH, NSB, 32, 512])
    sv = dout("sv", [DEPTH, NSB, 32, 512])
    ski = dout("ski", [DEPTH, NSB, 32, 64])
    sC = dout("sC", [DEPTH, NSB, 4, 128, 128])
    sn = dout("sn", [DEPTH, NSB, 4, 128])
    sm = dout("sm", [DEPTH, NSB, 4])
    sconv = dout("sconv", [DEPTH, NSB, 3, 512])

    XPs = [dscr("XP0", [NPOS, 1024]), dscr("XP1", [NPOS, 1024])]
    XSs = [dscr("XS0", [NSB, 32, 1024]), dscr("XS1", [NSB, 32, 1024])]
    YAp = dscr("YAp", [NPOS, 1024])
    YAs = dscr("YAs", [NSB, 32, 1024])

    es = ExitStack()

    def sb(name, shape, dt=F32):
        return es.enter_context(nc.sbuf_tensor(name, list(shape), dt))

    WBUF = sb("WBUF", [128, 8, 2632], BF16)
    WOUT = sb("WOUT", [128, 4, 1024], BF16)
    KT = sb("KT", [128, 4, CW], BF16)
    VA = sb("VA", [128, NSLOT, 8, 65], BF16)
    KI2 = sb("KI2", [128, CW], BF16)
    SC = sb("SC", [128, CW], F32)
    MB = [sb("MB0", [128, CW], BF16), sb("MB1", [128, max(CW, 4128)], BF16)]
    XT = [sb("XT0", [128, 1024]), sb("XT1", [128, 1024])]
    YT = sb("YT", [128, 1024])
    CS = [sb("CS0", [128, 64]), sb("CS1", [128, 64])]
    HN = sb("HN", [128, 1024], BF16)
    HNT = sb("HNT", [128, 8, 128], BF16)
    FS = [sb("FS%d" % i, [128, 528]) for i in range(9)]
    BS = [sb("BS%d" % i, [128, 512], BF16) for i in range(6)]
    QTB = [sb("QTB0", [128, 4, 2, 128], BF16), sb("QTB1", [128, 4, 2, 128], BF16)]
    QIT = sb("QIT", [128, 4, 128], BF16)
    SM = sb("SM", [128, 128])
    SR = sb("SR", [4, 3, 128])
    SG = sb("SG", [4, 16])
    IDF = sb("IDF", [128, 128])
    IDB = sb("IDB", [128, 128], BF16)
    I2 = sb("I2", [128, 2, 128], BF16)
    TRI = sb("TRI", [128, 128])
    I4 = sb("I4", [4, 4])
    DD = sb("DD", [4, 4])
    ONES4 = sb("ONES4", [4, 128])
    GT = sb("GT", [128, 8])
    GQ = sb("GQ", [128, 64])
    GK = sb("GK", [128, 64])
    GBI = sb("GBI", [128, 4])
    GBF = sb("GBF", [128, 4])
    CVW = sb("CVW", [128, 4, 4])
    CVB = sb("CVB", [128, 4])
    MB1f = MB[1][:, 0:4128].bitcast(F32)
    GH = MB1f[:, 0:512]
    SKP = MB1f[:, 512:1024]
    UEXT = MB1f[:, 1024:1548].rearrange("p (b t) -> p b t", t=131)
    CNS = MB1f[:, 1548:2064].rearrange("p (h d) -> p h d", d=129)
    WQK = QTB[1][:, :, :, :].rearrange("p a b c -> p (a b c)").rearrange("p (q h e) -> p q h e", q=2, h=4)
    FS8b = FS[8][:, 0:516].bitcast(BF16)
    CNB = FS8b[:, 0:516].rearrange("p (h d) -> p h d", d=129)
    VM = FS8b[:, 516:1032].rearrange("p (h d) -> p h d", d=129)
    ALIAS = [("MB", 1), "GH", "SKP", "UEXT", "CNS", ("QTB", 1), "WQK", "FS8", "CNB", "VM"]
    PB = [es.enter_context(nc.psum_tensor("PB%d" % i, [128, 512], F32)) for i in range(8)]

    def pbf(i):
        return PB[i][:, :].bitcast(BF16)

    def P(i):
        return ("P", i)

    def dma(q, out, in_, r, w, chan, nonc=False):
        if nonc:
            S.add(q, lambda e: e.dma_start(out=out, in_=in_, allow_slow_non_contiguous=True), r, w, chan)
        else:
            S.add(q, lambda e: e.dma_start(out=out, in_=in_), r, w, chan)

    def mm(out, lhsT, rhs, start, stop, r, w, skip=False):
        S.add("pe", lambda e: e.matmul(out, lhsT=lhsT, rhs=rhs, start=start, stop=stop,
                                       skip_group_check=skip), r, w)

    def tr(out, in_, ident, r, w):
        S.add("pe", lambda e: e.transpose(out=out, in_=in_, identity=ident), r, w)

    def act(out, in_, func, r, w, bias=0.0, scale=1.0, accum=None):
        if accum is None:
            S.add("act", lambda e: e.activation(out=out, in_=in_, func=func, bias=bias, scale=scale), r, w)
        else:
            S.add("act", lambda e: e.activation(out=out, in_=in_, func=func, bias=bias, scale=scale,
                                                accum_out=accum), r, w)

    def ts(out, in0, s1, s2, op0, op1, r, w, eng="dve", accum=None):
        if op1 is None:
            S.add(eng, lambda e: e.tensor_scalar(out=out, in0=in0, scalar1=s1, scalar2=None, op0=op0), r, w)
        elif accum is None:
            S.add(eng, lambda e: e.tensor_scalar(out=out, in0=in0, scalar1=s1, scalar2=s2, op0=op0, op1=op1), r, w)
        else:
            S.add(eng, lambda e: e.tensor_scalar(out=out, in0=in0, scalar1=s1, scalar2=s2, op0=op0, op1=op1,
                                                 accum_out=accum), r, w)

    def tt(out, in0, in1, op, r, w, eng="dve"):
        S.add(eng, lambda e: e.tensor_tensor(out=out, in0=in0, in1=in1, op=op), r, w)

    def stt(out, in0, scalar, in1, op0, op1, r, w):
        S.add("dve", lambda e: e.scalar_tensor_tensor(out=out, in0=in0, scalar=scalar, in1=in1, op0=op0, op1=op1), r, w)

    def cp(out, in_, r, w, eng="dve"):
        if eng == "act":
            S.add(eng, lambda e: e.activation(out=out, in_=in_, func=AF.Copy), r, w)
        else:
            S.add(eng, lambda e: e.tensor_copy(out=out, in_=in_), r, w)

    def mset(ap, val, w, eng="dve"):
        S.add(eng, lambda e: e.memset(ap, val), (), w)

    def red(out, in_, op, r, w):
        S.add("dve", lambda e: e.tensor_reduce(out=out, in_=in_, axis=AX.X, op=op), r, w)

    def recip(out, in_, r, w):
        S.add("dve", lambda e: e.reciprocal(out=out, in_=in_), r, w)

    dma("sp", IDF[:, :], c_ident, (), ["IDF"], "c0")
    dma("sp", TRI[:, :], c_tri, (), ["TRI"], "c1")
    dma("sp", I4[:, :], c_i4, (), ["I4"], "c2")
    dma("sp", ONES4[:, :], c_ones4, (), ["ONES4"], "c3")
    cp(IDB[:, :], IDF[:, :], ["IDF"], ["IDB"])
    cp(I2[:, 0, :], IDF[:, :], ["IDF"], ["I2"])
    cp(I2[:, 1, :], IDF[:, :], ["IDF"], ["I2"])
    mset(QTB[0][:, :, :, :], 0.0, [("QTB", 0)])
    mset(VA[:, :, :, 64:65], 1.0, [("V", sl) for sl in range(NSLOT)])

    def barrier(keys):
        S.add("dve", lambda e: e.memset(SM[:, 127:128], 0.0), (), list(keys))

    def load_layer_params(l):
        dma("sp", GT[:, :], norm_g[l].rearrange("(k p) -> p k", p=128), (), ["GT"], "g0", nonc=True)
        dma("sp", GQ[:, :], q_norm_g[l].partition_broadcast(128), (), ["GQ"], "g1")
        dma("sp", GK[:, :], k_norm_g[l].partition_broadcast(128), (), ["GK"], "g2")
        dma("sp", GBI[:, :], b_igate[l].partition_broadcast(128), (), ["GBI"], "g3")
        dma("sp", GBF[:, :], b_fgate[l].partition_broadcast(128), (), ["GBF"], "g4")
        for j in range(4):
            dma("sp", CVW[:, :, j], conv_w[l, j].rearrange("(b f) -> f b", f=128), (), ["CVW"], "g7%d" % j, nonc=True)
        dma("sp", CVB[:, :], conv_b[l].rearrange("(b f) -> f b", f=128), (), ["CVB"], "g8", nonc=True)

    def load_weights_A(l):
        dma("pool", WBUF[:, :, 0:2632], w_in[l][:, 0:2632].rearrange("(k p) n -> p k n", p=128), (), ["WBUF"], "w0")
        dma("pool", WOUT[:, :, :], w_out[l][0:512, :].rearrange("(k p) n -> p k n", p=128), (), ["WOUT"], "w1")

    def load_weights_B(l):
        dma("pool", WBUF[:, :, 0:2056], w_in[l][:, 2632:4688].rearrange("(k p) n -> p k n", p=128), (), ["WBUF"], "w0")
        dma("pool", WOUT[:, :, :], w_out[l][512:1024, :].rearrange("(k p) n -> p k n", p=128), (), ["WOUT"], "w1")
        dma("pool", WQK[:, 0, :, :], wq_m[l].rearrange("h d e -> d h e"), (), ["WQK"], "w2")
        dma("pool", WQK[:, 1, :, :], wk_m[l].rearrange("h d e -> d h e"), (), ["WQK"], "w3")
        dma("sp", GH, head_norm_g[l].partition_broadcast(128), (), ["GH"], "g5")
        dma("sp", SKP, skip[l].partition_broadcast(128), (), ["SKP"], "g6")

    xslot = [0]

    def load_norm(nt, xsrc, xkey, pos0=None):
        s = xslot[0]
        xslot[0] ^= 1
        xt = XT[s]
        dma("sp", xt[:nt, :], xsrc, [xkey], [("XT", s)], "xt%d" % s)
        if pos0 is not None:
            dma("sp", CS[s][:nt, :], c_rope[pos0:pos0 + nt, :], (), [("CS", s)], "cs%d" % s)
        act(HN[:nt, :], xt[:nt, :], AF.Square, [("XT", s)], ["HN", "ss"], accum=SM[:nt, 0:1])
        act(SM[:nt, 1:2], SM[:nt, 0:1], AF.Ln, ["ss"], ["ss1"], bias=EPS, scale=1.0 / 1024)
        act(SM[:nt, 2:3], SM[:nt, 1:2], AF.Exp, ["ss1"], ["rstd"], scale=-0.5)
        act(HN[:nt, :], xt[:nt, :], AF.Copy, [("XT", s), "rstd"], ["HN"], scale=SM[:nt, 2:3])
        pt = pbf(2).rearrange("p (k t) -> p k t", t=128)
        for kc in range(8):
            tr(pt[:, kc, :nt], HN[:nt, kc * 128:(kc + 1) * 128], IDB[:nt, :nt], ["HN", "IDB"], [P(2)])
        tt(HNT[:, :, :nt], pt[:, :, :nt], GT[:, :].unsqueeze(2).to_broadcast([128, 8, nt]), ALU.mult,
           ["GT"], [P(2), "HNT"])
        return s

    def inproj(nt, c0, n, bank):
        for kc in range(8):
            mm(PB[bank][:nt, 0:n], HNT[:, kc, :nt], WBUF[:, kc, c0:c0 + n], kc == 0, kc == 7,
               ["HNT", "WBUF"], [P(bank)])

    def rope(nt, s, x1, x2, o1, o2, nh, rkeys, wkey, tmp, tkey="FS2"):
        cosb = CS[s][:nt, 0:32].unsqueeze(1).to_broadcast([nt, nh, 32])
        sinb = CS[s][:nt, 32:64].unsqueeze(1).to_broadcast([nt, nh, 32])
        t1 = tmp[:nt, 0:nh * 32].rearrange("p (h d) -> p h d", d=32)
        t2 = tmp[:nt, 256:256 + nh * 32].rearrange("p (h d) -> p h d", d=32)
        rk = list(rkeys) + [("CS", s)]
        tt(t1, x1, cosb, ALU.mult, rk, [tkey])
        tt(t2, x2, sinb, ALU.mult, rk, [tkey])
        tt(o1, t1, t2, ALU.subtract, [tkey], [wkey])
        tt(t1, x2, cosb, ALU.mult, rk, [tkey])
        tt(t2, x1, sinb, ALU.mult, rk, [tkey])
        tt(o2, t1, t2, ALU.add, [tkey], [wkey])

    def qknorm(nt, bank, gtile, gkey, dst, dkey):
        ps3 = PB[bank][:nt, :].rearrange("p (h d) -> p h d", d=64)
        sq = FS[1]
        act(sq[:nt, 0:512], PB[bank][:nt, :], AF.Square, (), [P(bank), "FS1"])
        red(SM[:nt, 8:16], sq[:nt, 0:512].rearrange("p (h d) -> p h d", d=64), ALU.add, ["FS1"], ["qss"])
        act(SM[:nt, 16:24], SM[:nt, 8:16], AF.Ln, ["qss"], ["qss1"], bias=EPS, scale=1.0 / 64)
        act(SM[:nt, 24:32], SM[:nt, 16:24], AF.Exp, ["qss1"], ["qrstd"], scale=-0.5)
        d3 = dst[:nt, 0:512].rearrange("p (h d) -> p h d", d=64)
        tt(d3, ps3, SM[:nt, 24:32].unsqueeze(2).to_broadcast([nt, 8, 64]), ALU.mult, ["qrstd"], [P(bank), dkey])
        tt(d3, d3, gtile[:nt, :].unsqueeze(1).to_broadcast([nt, 8, 64]), ALU.mult, [dkey, gkey], [dkey])

    WSC = 0.125 * (8.0 ** -0.5)

    tcount = [0]

    def slot_cols(slot):
        if slot == 0:
            return (0, 16)
        if slot == NSLOT - 1:
            return (16 + PAST, 16 + PAST + 32)
        return (16 + 128 * (slot - 1), 16 + 128 * slot)

    def passA_front(l, nt, xsrc, xkey, pos0, blocks, slot_own, kdst, vdst, kidst, ya_dst, ya_key, ksel, chunkmask):
        par = tcount[0] % 2
        tcount[0] += 1
        col_own = pos0
        s = load_norm(nt, xsrc, xkey, pos0)
        pt = pbf(2).rearrange("p (k t) -> p k t", t=128)
        qtb = QTB[par]
        qkey = ("QTB", par)
        inproj(nt, 0, 512, 0)
        inproj(nt, 512, 512, 1)
        qn = FS[0]
        qknorm(nt, 0, GQ, "GQ", qn, "FS0")
        q3 = qn[:nt, 0:512].rearrange("p (h d) -> p h d", d=64)
        qr = BS[0]
        qr3 = qr[:nt, :].rearrange("p (h d) -> p h d", d=64)
        rope(nt, s, q3[:, :, 0:32], q3[:, :, 32:64], qr3[:, :, 0:32], qr3[:, :, 32:64], 8, ["FS0"], "BS0", FS[2])
        inproj(nt, 1024, 512, 0)
        for p in range(4):
            tr(pt[:, p, :nt], qr[:nt, p * 128:(p + 1) * 128], IDB[:nt, :nt], ["BS0", "IDB"], [P(2)])
        cp(qtb[0:64, :, 0, :nt], pt[0:64, 0:4, :nt], (), [P(2), qkey])
        cp(qtb[64:128, :, 1, :nt], pt[64:128, 0:4, :nt], (), [P(2), qkey])
        kn = FS[3]
        qknorm(nt, 1, GK, "GK", kn, "FS3")
        k3 = kn[:nt, 0:512].rearrange("p (h d) -> p h d", d=64)
        kr = FS[4]
        kr3 = kr[:nt, 0:512].rearrange("p (h d) -> p h d", d=64)
        rope(nt, s, k3[:, :, 0:32], k3[:, :, 32:64], kr3[:, :, 0:32], kr3[:, :, 32:64], 8, ["FS3"], "FS4", FS[2])
        inproj(nt, 1536, 512, 1)
        dma("pool", kdst, kr[:nt, 0:512], ["FS4"], (), "ko")
        kb16 = BS[1]
        cp(kb16[:nt, :], kr[:nt, 0:512], ["FS4"], ["BS1"])
        for p in range(4):
            tr(pt[:, p, :nt], kb16[:nt, p * 128:(p + 1) * 128], IDB[:nt, :nt], ["BS1", "IDB"], [P(2)])
        cp(KT[:, :, col_own:col_own + nt], pt[:, 0:4, :nt], (), [P(2), ("KT", slot_own)])
        vf = FS[5]
        cp(vf[:nt, 0:512], PB[0][:nt, :], (), [P(0), "FS5"], eng="act")
        inproj(nt, 2048, 512, 0)
        dma("pool", vdst, vf[:nt, 0:512], ["FS5"], (), "vo")
        cp(VA[:nt, slot_own, :, 0:64], vf[:nt, 0:512].rearrange("p (h d) -> p h d", d=64), ["FS5"], [("V", slot_own)])
        zi = 6 if par == 0 else 8
        zas = FS[zi]
        zkey = "FS%d" % zi
        act(zas[:nt, 0:512], PB[1][:nt, :], AF.Silu, (), [P(1), zkey])
        inproj(nt, 2560, 72, 1)
        qir = BS[0]
        qir3 = qir[:nt, :].rearrange("p (h d) -> p h d", d=64)
        cp(FS[0][:nt, 0:512], PB[0][:nt, :], (), [P(0), "FS0"], eng="act")
        qf3 = FS[0][:nt, 0:512].rearrange("p (h d) -> p h d", d=64)
        rope(nt, s, qf3[:, :, 0:32], qf3[:, :, 32:64], qir3[:, :, 0:32], qir3[:, :, 32:64], 8, ["FS0"], "BS0", FS[2])
        for p in range(4):
            tr(pt[:, p, :nt], qir[:nt, p * 128:(p + 1) * 128], IDB[:nt, :nt], ["BS0", "IDB"], [P(2)])
        cp(QIT[:, :, :nt], pt[:, 0:4, :nt], (), [P(2), "QIT"])
        kif = FS[7]
        cp(kif[:nt, 0:72], PB[1][:nt, 0:72], (), [P(1), "FS7"], eng="act")
        ts(SM[:nt, 32:40], kif[:nt, 64:72], WSC, None, ALU.mult, None, ["FS7"], ["wis"])
        ki1 = kif[:nt, 0:64].rearrange("p (h d) -> p h d", d=64)
        kio = kif[:nt, 128:192].rearrange("p (h d) -> p h d", d=64)
        rope(nt, s, ki1[:, :, 0:32], ki1[:, :, 32:64], kio[:, :, 0:32], kio[:, :, 32:64], 1, ["FS7"], "FS7", FS[2])
        dma("pool", kidst, kif[:nt, 128:192], ["FS7"], (), "kio")
        ki2 = BS[1]
        cp(ki2[:nt, 0:64], kif[:nt, 128:192], ["FS7"], ["BS1"])
        cp(ki2[:nt, 64:128], kif[:nt, 128:192], ["FS7"], ["BS1"])
        tr(pt[:, 4, :nt], ki2[:nt, 0:128], IDB[:nt, :nt], ["BS1", "IDB"], [P(2)])
        cp(KI2[:, col_own:col_own + nt], pt[:, 4, :nt], (), [P(2), ("KI", slot_own)])
        attend_front(nt, blocks, ksel, chunkmask, par)
        return dict(nt=nt, s=s, par=par, blocks=blocks, zas=zas, zkey=zkey, ya_dst=ya_dst, ya_key=ya_key)

    def passA_back(c):
        nt, s, par, zas, zkey = c["nt"], c["s"], c["par"], c["zas"], c["zkey"]
        pt = pbf(2).rearrange("p (k t) -> p k t", t=128)
        attend_back(nt, c["blocks"], par)
        at = FS[0]
        akey = "FS0"
        for b in range(2):
            ov = PB[6 + b][:nt, 0:260].rearrange("p (h d) -> p h d", d=65)
            recip(SM[:nt, 40 + 4 * b:44 + 4 * b], ov[:, :, 64], (), [P(6 + b), "rden"])
            tt(at[:nt, 256 * b:256 * (b + 1)].rearrange("p (h d) -> p h d", d=64), ov[:, :, 0:64],
               SM[:nt, 40 + 4 * b:44 + 4 * b].unsqueeze(2).to_broadcast([nt, 4, 64]), ALU.mult,
               ["rden"], [P(6 + b), akey])
        mxa = BS[2]
        tt(mxa[:nt, :], at[:nt, 0:512], zas[:nt, 0:512], ALU.mult, [akey, zkey], ["BS2"])
        for p in range(4):
            tr(pt[:, p, :nt], mxa[:nt, p * 128:(p + 1) * 128], IDB[:nt, :nt], ["BS2", "IDB"], [P(2)])
        mxt = BS[3]
        mxt3 = mxt[:, :].rearrange("p (k t) -> p k t", t=128)
        cp(mxt3[:, :, :nt], pt[:, 0:4, :nt], (), [P(2), "BS3"])
        for hf in range(2):
            for kc in range(4):
                mm(PB[hf][:nt, :], mxt3[:, kc, :nt], WOUT[:, kc, hf * 512:(hf + 1) * 512], kc == 0, kc == 3,
                   ["BS3", "WOUT"], [P(hf)])
            tt(YT[:nt, hf * 512:(hf + 1) * 512], PB[hf][:nt, :], XT[s][:nt, hf * 512:(hf + 1) * 512], ALU.add,
               [("XT", s)], [P(hf), "YT"])
        dma("pool", c["ya_dst"], YT[:nt, :], ["YT"], [c["ya_key"]], "yo")

    def attend_front(nt, blocks, ksel, chunkmask, par):
        L = sum(b[1] for b in blocks)
        assert blocks[0][0] == 0
        mb = MB[par]
        mkey = ("MB", par)
        ibanks = [3, 4, 5]
        nb = 0
        c0 = 0
        while c0 < L:
            n = min(512, L - c0)
            kikeys = [("KI", bl[2]) for bl in blocks if bl[0] < c0 + n and bl[0] + bl[1] > c0]
            for h in range(8):
                p, e = divmod(h, 2)
                bank = ibanks[nb % 3]
                ri = (1, 3)[nb % 2]
                rbuf = FS[ri]
                rkey = "FS%d" % ri
                nb += 1
                mm(PB[bank][:nt, 0:n], QIT[64 * e:64 * e + 64, p, :nt], KI2[64 * e:64 * e + 64, c0:c0 + n],
                   True, True, ["QIT"] + kikeys, [P(bank)])
                act(rbuf[:nt, 0:n], PB[bank][:nt, 0:n], AF.Relu, (), [P(bank), rkey])
                if h == 0:
                    ts(SC[:nt, c0:c0 + n], rbuf[:nt, 0:n], SM[:nt, 32:33], None, ALU.mult, None,
                       [rkey, "wis"], ["SC"])
                else:
                    stt(SC[:nt, c0:c0 + n], rbuf[:nt, 0:n], SM[:nt, 32 + h:33 + h], SC[:nt, c0:c0 + n],
                        ALU.mult, ALU.add, [rkey, "wis", "SC"], ["SC"])
            c0 += n
        if chunkmask:
            mset(SC[0:64, L - 64:L], -1.0e4, ["SC"])
        LO = cfg.brk
        if L > ksel:
            thr = SM[:nt, 48:49]
            cnt = SM[:nt, 49:50]
            tmp = SM[:nt, 50:51]
            nth = [SM[:nt, 51:52], SM[:nt, 52:53]]
            use_act = False
            st = LO
            if not use_act:
                mset(thr, 0.0, ["thr"])
                for it in range(cfg.nbis):
                    ts(mb[:nt, 0:L], SC[:nt, 0:L], thr, None, ALU.is_ge, ALU.add, ["SC", "thr"], [mkey, "cnt"], accum=cnt)
                    ts(tmp, cnt, ksel - 0.5, st, ALU.is_ge, ALU.mult, ["cnt"], ["btmp"])
                    if it < cfg.nbis - 1:
                        stt(thr, tmp, -st / 2, thr, ALU.add, ALU.add, ["btmp", "thr"], ["thr"])
                    else:
                        stt(thr, tmp, -st, thr, ALU.add, ALU.add, ["btmp", "thr"], ["thr"])
                    st /= 2
            else:
                mset(nth[0], 0.0, ["nth0"])
                cc = 2.0 * ksel - 1.0 - L
                for it in range(cfg.nbis):
                    a_, b_ = it % 2, (it + 1) % 2
                    act(mb[:nt, 0:L], SC[:nt, 0:L], AF.Sign, ["SC", "nth%d" % a_], [mkey, "cnt"], bias=nth[a_], accum=cnt)
                    act(tmp, cnt, AF.Sign, ["cnt"], ["btmp"], bias=0.5 - cc)
                    act(nth[b_], tmp, AF.Identity, ["btmp", "nth%d" % a_], ["nth%d" % b_], bias=nth[a_], scale=-st / 2)
                    st /= 2
                fin = cfg.nbis % 2
                st_last = st * 2
                act(thr, nth[fin], AF.Identity, ["nth%d" % fin], ["thr"], bias=-st_last / 2, scale=-1.0)
            ts(mb[:nt, 0:L], SC[:nt, 0:L], thr, NEG, ALU.is_lt, ALU.mult, ["SC", "thr"], [mkey])
        else:
            ts(mb[:nt, 0:L], SC[:nt, 0:L], -LO, NEG, ALU.is_lt, ALU.mult, ["SC"], [mkey])

    def attend_back(nt, blocks, par):
        mb = MB[par]
        mkey = ("MB", par)
        qtb = QTB[par]
        qkey = ("QTB", par)
        qbank = [3, 4, 0, 1]
        ptbufs = [4, 5, 0, 1]
        units = [(bi, half) for bi in range(len(blocks)) for half in range(2)]
        first = [True, True]
        nu = len(units)

        def qk_exp(u):
            bi, half = units[u]
            c0, kb, slot = blocks[bi]
            bank = qbank[u % 4]
            for pp in range(2):
                p = 2 * half + pp
                o3 = PB[bank][:kb, pp * 2 * nt:(pp + 1) * 2 * nt].rearrange("p (e t) -> p e t", e=2)
                mm(o3, KT[:, p, c0:c0 + kb], qtb[:, p, :, :nt], True, False, [("KT", slot), qkey], [P(bank)])
                mm(o3, mb[:nt, c0:c0 + kb], I2[:nt, :, :nt], False, True, [mkey, "I2"], [P(bank)])
            pi = ptbufs[u % 4]
            act(BS[pi][:kb, 0:4 * nt], PB[bank][:kb, 0:4 * nt], AF.Exp, (), [P(bank), "BS%d" % pi], scale=0.125)

        def pv(u):
            bi, half = units[u]
            c0, kb, slot = blocks[bi]
            pi = ptbufs[u % 4]
            ptb = BS[pi]
            for pp in range(2):
                for e in range(2):
                    h = 2 * (2 * half + pp) + e
                    ob = 6 + h // 4
                    hh = h % 4
                    st_ = first[h // 4]
                    first[h // 4] = False
                    mm(PB[ob][:nt, hh * 65:(hh + 1) * 65], ptb[:kb, (pp * 2 + e) * nt:(pp * 2 + e + 1) * nt],
                       VA[:kb, slot, h, :], st_, (u == nu - 1 and hh == 3), ["BS%d" % pi, ("V", slot)], [P(ob)], skip=True)

        LAG = 2
        for u in range(nu):
            qk_exp(u)
            if u >= LAG:
                pv(u - LAG)
        for u in range(max(0, nu - LAG), nu):
            pv(u)

    def passB(l, nt, xsrc, xkey, ya_src, ya_key, out_dsts, out_keys):
        s = load_norm(nt, xsrc, xkey, None)
        dma("sp", YT[:nt, :], ya_src, [ya_key], ["YT"], "yi")
        O = 64
        u3 = PB[3][:, :].rearrange("p (b t) -> p b t", t=128)
        for b in range(4):
            for kc in range(8):
                mm(u3[:, b, :nt], WBUF[:, kc, b * 128:(b + 1) * 128], HNT[:, kc, :nt], kc == 0, kc == 7,
                   ["HNT", "WBUF"], [P(3)])
        cp(UEXT[:, :, 3:3 + nt], u3[:, :, :nt], (), [P(3), "UEXT"], eng="act")
        cacc = FS[2]
        ca3 = cacc[:, 0:512].rearrange("p (b t) -> p b t", t=128)
        for b in range(4):
            ts(ca3[:, b, :nt], UEXT[:, b, 0:nt], CVW[:, b, 0:1], CVB[:, b:b + 1], ALU.mult, ALU.add,
               ["UEXT", "CVW", "CVB"], ["FS2"])
            for j in range(1, 4):
                stt(ca3[:, b, :nt], UEXT[:, b, j:j + nt], CVW[:, b, j:j + 1], ca3[:, b, :nt], ALU.mult, ALU.add,
                    ["UEXT", "CVW", "FS2"], ["FS2"])
        ct = BS[0]
        ct3 = ct[:, :].rearrange("p (b t) -> p b t", t=128)
        act(ct3[:, :, :nt], ca3[:, :, :nt], AF.Silu, ["FS2"], ["BS0"])
        cp(UEXT[:, :, 0:3], UEXT[:, :, nt:nt + 3], ["UEXT"], ["UEXT"])
        ptc = pbf(2).rearrange("p (k t) -> p k t", t=128)
        for b in range(4):
            tr(ptc[:nt, b, :], ct3[:, b, :nt], IDB[:, :], ["BS0", "IDB"], [P(2)])
        t2 = FS[5]
        tt(t2[:nt, 0:512].rearrange("p (k t) -> p k t", t=128), ptc[:nt, 0:4, :],
           SKP[:nt, :].rearrange("p (k t) -> p k t", t=128), ALU.mult, ["SKP"], [P(2), "FS5"])
        inproj(nt, 512, 512, 0)
        cp(VM[:nt, :, 0:128], PB[0][:nt, :].rearrange("p (h d) -> p h d", d=128), (), [P(0), "VM"])
        inproj(nt, 1024, 512, 1)
        sgo = FS[0]
        act(sgo[:nt, 0:512], PB[1][:nt, :], AF.Sigmoid, (), [P(1), "FS0"])
        inproj(nt, 1536, 512, 0)
        szm = FS[1]
        act(szm[:nt, 0:512], PB[0][:nt, :], AF.Silu, (), [P(0), "FS1"])
        inproj(nt, 2048, 8, 1)
        tt(SM[:nt, O + 0:O + 4], PB[1][:nt, 0:4], GBI[:nt, :], ALU.add, ["GBI"], [P(1), "igb"])
        tt(SM[:nt, O + 4:O + 8], PB[1][:nt, 4:8], GBF[:nt, :], ALU.add, ["GBF"], [P(1), "fgb"])
        act(SM[:nt, O + 8:O + 12], SM[:nt, O + 4:O + 8], AF.Exp, ["fgb"], ["e1"], scale=-1.0)
        act(SM[:nt, O + 12:O + 16], SM[:nt, O + 8:O + 12], AF.Ln, ["e1"], ["nlf"], bias=1.0)
        mm(PB[5][0:4, 0:nt], SM[:nt, O + 0:O + 4], IDF[:nt, :nt], True, False, ["igb", "IDF"], [P(5)])
        mm(PB[5][0:4, 0:nt], SM[:nt, O + 12:O + 16], TRI[:nt, :nt], False, True, ["nlf", "TRI"], [P(5)])
        mm(PB[5][0:4, 128:128 + nt], SM[:nt, O + 12:O + 16], TRI[:nt, :nt], True, True, ["nlf", "TRI"], [P(5)])
        MU, NBC, MUN, DLT, DEC, NMU, CB, AMX = [SG[0:4, i:i + 1] for i in range(8)]
        arow = SR[0:4, 0, :nt]
        orow = SR[0:4, 1, :nt]
        crow = SR[0:4, 2, :nt]
        ts(arow, PB[5][0:4, 0:nt], NBC, None, ALU.add, None, ["NBC"], [P(5), "arow"])
        red(AMX, arow, ALU.max, ["arow"], ["AMX"])
        tt(MUN, AMX, MU, ALU.max, ["AMX", "MU"], ["MUN"])
        tt(DLT, MU, MUN, ALU.subtract, ["MU", "MUN"], ["DLT"])
        act(DEC, DLT, AF.Exp, ["DLT"], ["DEC"])
        ts(NMU, MUN, -1.0, None, ALU.mult, None, ["MUN"], ["NMU"])
        tt(CB, NBC, MUN, ALU.subtract, ["NBC", "MUN"], ["CB"])
        act(orow, arow, AF.Exp, ["arow", "NMU"], ["orow"], bias=NMU)
        act(crow, PB[5][0:4, 128:128 + nt], AF.Exp, ["CB"], [P(5), "crow"], bias=CB)
        tt(NBC, NBC, PB[5][0:4, 128 + nt - 1:128 + nt], ALU.add, ["NBC"], [P(5), "NBC"])
        cp(MU, MUN, ["MUN"], ["MU"])
        ts(DD[:, :], I4[:, :], DEC, None, ALU.mult, None, ["I4", "DEC"], ["DD"])
        mm(PB[5][:nt, 256:260], orow, I4[:, :], True, True, ["orow", "I4"], [P(5)])
        mm(PB[5][:nt, 260:264], crow, I4[:, :], True, True, ["crow", "I4"], [P(5)])
        mm(PB[5][:, 264:268], ONES4[:, :], DD[:, :], True, True, ["ONES4", "DD"], [P(5)])
        cp(SM[:nt, O + 16:O + 24], PB[5][:nt, 256:264], (), [P(5), "wc"])
        cp(SM[:, O + 24:O + 28], PB[5][:, 264:268], (), [P(5), "dbc"])
        WCO = O + 16
        CLO = O + 20
        DBO = O + 24
        q3p = PB[4][:, :].rearrange("p (h t) -> p h t", t=128)
        for h in range(4):
            mm(q3p[:, h, :nt], WQK[:, 0, h, :], ct3[:, h, :nt], True, True, ["WQK", "BS0"], [P(4)])
        qmt = BS[1]
        qmt3 = qmt[:, :].rearrange("p (h t) -> p h t", t=128)
        ts(qmt3[:, :, :nt], q3p[:, :, :nt], 128.0 ** -0.5, None, ALU.mult, None, (), [P(4), "BS1"])
        for h in range(4):
            mm(q3p[:, h, :nt], WQK[:, 1, h, :], ct3[:, h, :nt], True, True, ["WQK", "BS0"], [P(4)])
        kmt = BS[2]
        kmt3 = kmt[:, :].rearrange("p (h t) -> p h t", t=128)
        cp(kmt3[:, :, :nt], q3p[:, :, :nt], (), [P(4), "BS2"])
        k3p = PB[3][:nt, :].rearrange("p (h e) -> p h e", e=128)
        for h in range(4):
            mm(k3p[:, h, :], ct3[:, h, :nt], WQK[:, 1, h, :], True, True, ["WQK", "BS0"], [P(3)])
        kw = BS[3]
        kw3 = kw[:nt, :].rearrange("p (h e) -> p h e", e=128)
        for h in range(4):
            ts(kw3[:, h, :], k3p[:, h, :], SM[:nt, WCO + h:WCO + h + 1], None, ALU.mult, None, ["wc"], [P(3), "BS3"])
        s3p = PB[4][:nt, :].rearrange("p (h t) -> p h t", t=128)
        for h in range(4):
            mm(s3p[:, h, :nt], kmt3[:, h, :nt], qmt3[:, h, :nt], True, True, ["BS2", "BS1"], [P(4)])
        pmt = BS[4]
        pmt3 = pmt[:nt, :].rearrange("p (h t) -> p h t", t=128)
        for h in range(4):
            stt(pmt3[:, h, :nt], s3p[:, h, :nt], SM[:nt, WCO + h:WCO + h + 1], TRI[:nt, :nt], ALU.mult, ALU.mult,
                ["wc", "TRI"], [P(4), "BS4"])
        for h in range(4):
            ts(CNS[:, h, :], CNS[:, h, :], SM[:, DBO + h:DBO + h + 1], None, ALU.mult, None, ["dbc", "CNS"], ["CNS"])
        cp(CNB[:, :, :], CNS[:, :, :], ["CNS"], ["CNB"], eng="act")
        for h in range(4):
            b, hh = divmod(h, 2)
            o = PB[6 + b][:nt, hh * 129:(hh + 1) * 129]
            mm(o, pmt3[:, h, :nt], VM[:nt, h, :], True, False, ["BS4", "VM"], [P(6 + b)])
            mm(o, qmt3[:, h, :nt], CNB[:, h, :], False, True, ["BS1", "CNB"], [P(6 + b)])
        for h in range(4):
            b, hh = divmod(h, 2)
            mm(PB[b][:, hh * 129:(hh + 1) * 129], kw3[:, h, :], VM[:nt, h, :], True, True, ["BS3", "VM"], [P(b)])
        for b in range(2):
            tt(CNS[:, 2 * b:2 * b + 2, :], CNS[:, 2 * b:2 * b + 2, :],
               PB[b][:, 0:258].rearrange("p (h d) -> p h d", d=129), ALU.add, ["CNS"], [P(b), "CNS"])
        hb = FS[3]
        for b in range(2):
            nd = PB[6 + b][:nt, 0:258].rearrange("p (h d) -> p h d", d=129)
            tt(SM[:nt, O + 28 + 2 * b:O + 30 + 2 * b], nd[:, :, 128], SM[:nt, CLO + 2 * b:CLO + 2 + 2 * b], ALU.max,
               ["wc"], [P(6 + b), "dmax"])
            stt(SM[:nt, O + 28 + 2 * b:O + 30 + 2 * b], nd[:, :, 128], -1.0, SM[:nt, O + 28 + 2 * b:O + 30 + 2 * b],
                ALU.mult, ALU.max, ["dmax"], [P(6 + b), "dmax"])
            recip(SM[:nt, O + 32 + 2 * b:O + 34 + 2 * b], SM[:nt, O + 28 + 2 * b:O + 30 + 2 * b], ["dmax"], ["rdm"])
            tt(hb[:nt, 256 * b:256 * (b + 1)].rearrange("p (h d) -> p h d", d=128), nd[:, :, 0:128],
               SM[:nt, O + 32 + 2 * b:O + 34 + 2 * b].unsqueeze(2).to_broadcast([nt, 2, 128]), ALU.mult,
               ["rdm"], [P(6 + b), "FS3"])
        sq = FS[4]
        act(sq[:nt, 0:512], hb[:nt, 0:512], AF.Square, ["FS3"], ["FS4"])
        red(SM[:nt, O + 36:O + 40], sq[:nt, 0:512].rearrange("p (h d) -> p h d", d=128), ALU.add, ["FS4"], ["hss"])
        act(SM[:nt, O + 40:O + 44], SM[:nt, O + 36:O + 40], AF.Ln, ["hss"], ["hss1"], bias=EPS, scale=1.0 / 128)
        act(SM[:nt, O + 44:O + 48], SM[:nt, O + 40:O + 44], AF.Exp, ["hss1"], ["hrstd"], scale=-0.5)
        hb3 = hb[:nt, 0:512].rearrange("p (h d) -> p h d", d=128)
        tt(hb3, hb3, SM[:nt, O + 44:O + 48].unsqueeze(2).to_broadcast([nt, 4, 128]), ALU.mult, ["hrstd", "FS3"], ["FS3"])
        tt(hb[:nt, 0:512], hb[:nt, 0:512], GH[:nt, :], ALU.mult, ["FS3", "GH"], ["FS3"])
        tt(hb[:nt, 0:512], hb[:nt, 0:512], sgo[:nt, 0:512], ALU.mult, ["FS3", "FS0"], ["FS3"])
        pt = pbf(2).rearrange("p (k t) -> p k t", t=128)
        tt(hb[:nt, 0:512], hb[:nt, 0:512], t2[:nt, 0:512], ALU.add, ["FS3", "FS5"], ["FS3"])
        mxb = BS[5]
        tt(mxb[:nt, :], hb[:nt, 0:512], szm[:nt, 0:512], ALU.mult, ["FS3", "FS1"], ["BS5"])
        for p in range(4):
            tr(pt[:, p, :nt], mxb[:nt, p * 128:(p + 1) * 128], IDB[:nt, :nt], ["BS5", "IDB"], [P(2)])
        mxt = BS[1]
        mxt3 = mxt[:, :].rearrange("p (k t) -> p k t", t=128)
        cp(mxt3[:, :, :nt], pt[:, 0:4, :nt], (), [P(2), "BS1"])
        for hf in range(2):
            for kc in range(4):
                mm(PB[hf][:nt, :], mxt3[:, kc, :nt], WOUT[:, kc, hf * 512:(hf + 1) * 512], kc == 0, kc == 3,
                   ["BS1", "WOUT"], [P(hf)])
            tt(YT[:nt, hf * 512:(hf + 1) * 512], PB[hf][:nt, :], YT[:nt, hf * 512:(hf + 1) * 512], ALU.add,
               ["YT"], [P(hf), "YT"])
        for i, (dst, key) in enumerate(zip(out_dsts, out_keys)):
            dma("pool", dst, YT[:nt, :], ["YT"], [key] if key else (), "xo%d" % i)

    def state_out(Cd, nd_, md, cvd, chs):
        dma("pool", Cd.rearrange("h k v -> k h v"), CNS[:, :, 0:128], ["CNS"], (), chs + "C")
        dma("pool", nd_.rearrange("h k -> k h"), CNS[:, :, 128], ["CNS"], (), chs + "n", nonc=True)
        tt(SG[0:4, 8:9], SG[0:4, 0:1], SG[0:4, 1:2], ALU.subtract, ["MU", "NBC"], ["MOUT"])
        dma("pool", md.rearrange("(h o) -> h o", o=1), SG[0:4, 8:9], ["MOUT"], (), chs + "m", nonc=True)
        for j in range(3):
            dma("pool", cvd[j].rearrange("(b f) -> f b", f=128), UEXT[:, :, j], ["UEXT"], (), chs + "v%d" % j, nonc=True)

    for l in range(DEPTH):
        load_layer_params(l)
        load_weights_A(l)
        xin_p = XPs[l % 2]
        xout_p = XPs[(l + 1) % 2]
        xin_s = XSs[l % 2]
        xout_s = XSs[(l + 1) % 2]
        barrier(ALIAS)
        mset(QTB[1][:, :, :, :], 0.0, [("QTB", 1)])
        prev = None
        for ti in range(NT + 1):
            if ti == 0:
                nt, pos0 = 16, 0
                xsrc = meta if l == 0 else xin_p[0:16, :]
                blocks = [(0, 16, 0)]
                cm = False
            else:
                fi = ti - 1
                nt, pos0 = 128, 16 + 128 * fi
                xsrc = xp[128 * fi:128 * fi + 128, :] if l == 0 else xin_p[pos0:pos0 + 128, :]
                blocks = [(0, 16, 0)] + [(16 + 128 * j, 128, 1 + j) for j in range(fi + 1)]
                cm = True
            ctx = passA_front(l, nt, xsrc, ("XP", l % 2, ti), pos0, blocks, ti,
                              pk[l, pos0:pos0 + nt, :], pv[l, pos0:pos0 + nt, :], pki[l, pos0:pos0 + nt, :],
                              YAp[pos0:pos0 + nt, :], ("YAp", ti), cfg.ksel_p, cm)
            if prev is not None:
                passA_back(prev)
            prev = ctx
        passA_back(prev)
        for b in range(NSB):
            for g in range(0, NCB, 4):
                ng = min(4, NCB - g)
                kis = FS[4 + (g // 4) % 2]
                kkey = "FS%d" % (4 + (g // 4) % 2)
                kdkey = "BS%d" % ((g // 4) % 2)
                dma("sp", kis[:, 0:64 * ng].rearrange("p (j d) -> p j d", d=64),
                    cki[l, b, 128 * g:128 * (g + ng), :].rearrange("(j p) d -> p j d", p=128), (), [kkey],
                    "kc%d" % ((g // 4) % 2))
                kd = BS[(g // 4) % 2]
                kd4 = kd[:, 0:128 * ng].rearrange("p (j e d) -> p j e d", e=2, d=64)
                k3 = kis[:, 0:64 * ng].rearrange("p (j d) -> p j d", d=64)
                cp(kd4[:, :, 0, :], k3, [kkey], [kdkey])
                cp(kd4[:, :, 1, :], k3, [kkey], [kdkey])
                pt = pbf(2).rearrange("p (k t) -> p k t", t=128)
                for jj in range(ng):
                    tr(pt[:, jj, :], kd[:, 128 * jj:128 * (jj + 1)], IDB[:, :], [kdkey, "IDB"], [P(2)])
                cp(KI2[:, 16 + 128 * g:16 + 128 * (g + ng)], pbf(2)[:, 0:128 * ng], (), [P(2)] + [("KI", 1 + g + q) for q in range(ng)], eng="act")
            xsrc = xs[b] if l == 0 else xin_s[b]
            pos0 = 16 + PAST
            blocks = [(0, 16, 0)] + [(16 + 128 * j, 128, 1 + j) for j in range(NCB)] + [(pos0, 32, NSLOT - 1)]
            ctx = passA_front(l, 32, xsrc, ("XS", l % 2, b), pos0, blocks, NSLOT - 1,
                              sk[l, b], sv[l, b], ski[l, b], YAs[b], ("YAs", b), cfg.ksel_s, False)
            for j in range(NCB):
                kst = FS[(1, 3)[j % 2]]
                vst = FS[(4, 5)[j % 2]]
                kkst = "FS%d" % ((1, 3)[j % 2])
                kvst = "FS%d" % ((4, 5)[j % 2])
                dma("sp", kst[:, 0:512], ck[l, b, 128 * j:128 * (j + 1), :], (), [kkst], "ks%d" % (j % 2))
                dma("sp", vst[:, 0:512], cv[l, b, 128 * j:128 * (j + 1), :], (), [kvst], "vs%d" % (j % 2))
                pf = PB[5][:, :].rearrange("p (k t) -> p k t", t=128)
                for p in range(4):
                    tr(pf[:, p, :], kst[:, p * 128:(p + 1) * 128], IDF[:, :], [kkst, "IDF"], [P(5)])
                cp(KT[:, :, 16 + 128 * j:16 + 128 * (j + 1)], pf[:, :, :], (), [P(5), ("KT", 1 + j)], eng="act")
                cp(VA[:, 1 + j, :, 0:64], vst[:, 0:512].rearrange("p (h d) -> p h d", d=64), [kvst], [("V", 1 + j)], eng="pool")
            passA_back(ctx)
        barrier(ALIAS)
        load_weights_B(l)
        mset(VM[:, :, 128:129], 1.0, ["VM"])
        mset(CNS[:, :, :], 0.0, ["CNS"])
        mset(UEXT[:, :, 0:3], 0.0, ["UEXT"])
        mset(SG[0:4, 0:2], 0.0, ["MU", "NBC"])
        last = (l == DEPTH - 1)
        for ti in range(NT + 1):
            if ti == 0:
                nt, pos0 = 16, 0
                xsrc = meta if l == 0 else xin_p[0:16, :]
                dsts, keys = ([], []) if last else ([xout_p[0:16, :]], [("XP", (l + 1) % 2, 0)])
            else:
                fi = ti - 1
                nt, pos0 = 128, 16 + 128 * fi
                xsrc = xp[128 * fi:128 * fi + 128, :] if l == 0 else xin_p[pos0:pos0 + 128, :]
                if last:
                    dsts, keys = [yp[128 * fi:128 * fi + 128, :]], [None]
                else:
                    dsts, keys = [xout_p[pos0:pos0 + 128, :]], [("XP", (l + 1) % 2, ti)]
            passB(l, nt, xsrc, ("XP", l % 2, ti), YAp[pos0:pos0 + nt, :], ("YAp", ti), dsts, keys)
        state_out(pC[l], pn[l], pm[l], pconv[l], "p")
        for b in range(NSB):
            dma("sp", CNS[:, :, 0:128], stC[l, b].rearrange("h k v -> k h v"), (), ["CNS"], "si0")
            dma("sp", CNS[:, :, 128], stn[l, b].rearrange("h k -> k h"), (), ["CNS"], "si1", nonc=True)
            dma("sp", SG[0:4, 0:1], stm[l, b].rearrange("(h o) -> h o", o=1), (), ["MU"], "si2", nonc=True)
            for j in range(3):
                dma("sp", UEXT[:, :, j], stconv[l, b, j].rearrange("(b f) -> f b", f=128), (), ["UEXT"], "si3%d" % j, nonc=True)
            mset(SG[0:4, 1:2], 0.0, ["NBC"])
            xsrc = xs[b] if l == 0 else xin_s[b]
            if last:
                dsts, keys = [ys[b]], [None]
            else:
                dsts, keys = [xout_s[b]], [("XS", (l + 1) % 2, b)]
            passB(l, 32, xsrc, ("XS", l % 2, b), YAs[b], ("YAs", b), dsts, keys)
            state_out(sC[l, b], sn[l, b], sm[l, b], sconv[l, b], "s")

    allch = list(S.chan_last.keys())
    S.add("sp", None, [], [], None)
    fin = S.ops[-1]
    fin.raw = set(S.chan_last[c] for c in allch)

    S.finalize()
    engs = ["pe", "act", "dve", "pool", "sp"]
    esem = {}
    for e in engs:
        n = (S.eng_cnt.get(e, 0) + SEM_CH - 1) // SEM_CH
        esem[e] = [es.enter_context(nc.semaphore("s_%s_%d" % (e, i))) for i in range(max(n, 1))]
    csem = {}
    for c, n in S.chan_cnt.items():
        k = (16 * n + SEM_CH - 1) // SEM_CH
        csem[c] = [es.enter_context(nc.semaphore("c_%s_%d" % (c, i))) for i in range(max(k, 1))]
    DCH = SEM_CH // 16 * 16

    def sem_of(t, v):
        if t[0] == "e":
            return esem[t[1]][(v - 1) // SEM_CH], (v - 1) % SEM_CH + 1
        return csem[t[1]][(v - 1) // DCH], (v - 1) % DCH + 1

    by_eng = {e: [] for e in engs}
    for op in S.ops:
        by_eng[op.eng].append(op)
    block = es.enter_context(nc.Block())

    def emit(ename, h):
        for op in by_eng[ename]:
            for (t, v) in op.waits:
                sem, val = sem_of(t, v)
                h.wait_ge(sem, val)
            if op.fn is None:
                continue
            ins = op.fn(h)
            if op.chan is not None:
                sem, val = sem_of(("c", op.chan), 16 * op.chan_n)
                ins.then_inc(sem, 16)
            elif op.inc:
                sem, val = sem_of(("e", ename), op.incval)
                ins.then_inc(sem, 1)

    @block.tensor
    def _(h):
        emit("pe", h)

    @block.scalar
    def _(h):
        emit("act", h)

    @block.vector
    def _(h):
        emit("dve", h)

    @block.gpsimd
    def _(h):
        emit("pool", h)

    @block.sync
    def _(h):
        emit("sp", h)

    es.close()
    return nc, len(S.ops)


def const_tables(cfg):
    half = 32
    freqs = (np.float32(10000.0) ** (-np.arange(half, dtype=np.float32) / np.float32(half))).astype(np.float32)
    pos = np.arange(cfg.kc, dtype=np.float32)
    ang = (pos[:, None] * freqs[None, :]).astype(np.float32)
    rope = np.concatenate([np.cos(ang.astype(np.float64)), np.sin(ang.astype(np.float64))], axis=1).astype(np.float32)
    tri = np.triu(np.ones((128, 128), np.float32))
    return {
        "c_ident": np.eye(128, dtype=np.float32),
        "c_tri": tri,
        "c_rope": rope,
        "c_i4": np.eye(4, dtype=np.float32),
        "c_ones4": np.ones((4, 128), np.float32),
    }


WNAMES = ["meta", "norm_g", "w_in", "q_norm_g", "k_norm_g", "conv_w", "conv_b", "wq_m", "wk_m",
          "b_igate", "b_fgate", "head_norm_g", "skip", "w_out"]


def run(cfg, inputs, ncores):
    nc, nops = build(cfg)
    consts = const_tables(cfg)
    f = lambda a: np.ascontiguousarray(np.asarray(a, dtype=np.float32))
    nsb = cfg.nsb
    in_maps = []
    for c in range(ncores):
        m = {
            "xp": f(inputs["x_prompt"][c]),
            "xs": f(inputs["x_sample"][nsb * c:nsb * (c + 1)]),
            "ck": f(inputs["cache_k"][:, nsb * c:nsb * (c + 1)]).reshape(cfg.depth, nsb, cfg.past, 512),
            "cv": f(inputs["cache_v"][:, nsb * c:nsb * (c + 1)]).reshape(cfg.depth, nsb, cfg.past, 512),
            "cki": f(inputs["cache_kidx"][:, nsb * c:nsb * (c + 1)]),
            "stC": f(inputs["state_C"][:, nsb * c:nsb * (c + 1)]),
            "stn": f(inputs["state_n"][:, nsb * c:nsb * (c + 1)]),
            "stm": f(inputs["state_m"][:, nsb * c:nsb * (c + 1)]),
            "stconv": f(inputs["state_conv"][:, nsb * c:nsb * (c + 1)]),
        }
        for wn in WNAMES:
            m[wn] = f(inputs[wn])
        m.update(consts)
        in_maps.append(m)
    res = run_bass_kernel_spmd(nc, in_maps, core_ids=list(range(ncores)))
    R = res.results
    D = cfg.depth

    def cat_b(name):
        return np.stack([R[c][name] for c in range(ncores)], axis=1)

    def cat_s(name):
        return np.concatenate([R[c][name] for c in range(ncores)], axis=1)

    y_prompt = np.stack([R[c]["yp"] for c in range(ncores)], axis=0)
    y_sample = np.concatenate([R[c]["ys"] for c in range(ncores)], axis=0)
    pk = cat_b("pk").reshape(D, ncores, cfg.npos, 8, 64)
    pv = cat_b("pv").reshape(D, ncores, cfg.npos, 8, 64)
    pki = cat_b("pki")
    sk = cat_s("sk").reshape(D, ncores * nsb, 32, 8, 64)
    sv = cat_s("sv").reshape(D, ncores * nsb, 32, 8, 64)
    return (y_prompt, y_sample, pk, pv, pki, cat_b("pC"), cat_b("pn"), cat_b("pm"), cat_b("pconv"),
            sk, sv, cat_s("ski"), cat_s("sC"), cat_s("sn"), cat_s("sm"), cat_s("sconv"))


def kernel(**inputs):
    cfg = Cfg()
    outs = run(cfg, inputs, 8)
    return tuple(np.ascontiguousarray(o.astype(np.float32)) for o in outs)
```
